# Optimizing a Trainium2 kernel written in Bass

```python
import math
import jax, jax.numpy as jnp
from jax import lax
import numpy as np

D_MODEL = 1024
BATCH = 8
SEQ = 4096
DEPTH = 1

D_MIX = D_MODEL
D_NSA = D_MIX // 2
D_GMLP = D_MIX - D_NSA
HEAD_DIM = 64
N_HEADS = D_NSA // HEAD_DIM
N_KV = 2
Q_PER_KV = N_HEADS // N_KV
CMP_BLOCK = 32
CMP_STRIDE = 16
CMP_HIDDEN = 256
SEL_BLOCK = 64
N_SEL = 16
WINDOW = 512
Q_BLOCK = 64
GMLP_CHUNK = 128
GMLP_GROUPS = 8
GMLP_GDIM = D_GMLP // GMLP_GROUPS
D_FF = 4 * D_MODEL
D_PLE = 256
EPS = 1e-6
FORCED_SCORE = 1e4
KV_COLS = N_KV * HEAD_DIM
COL_SIZES = [D_NSA, KV_COLS, KV_COLS, KV_COLS, KV_COLS, KV_COLS, KV_COLS, 3 * N_HEADS, D_GMLP, D_GMLP]
COL_OFFS = [int(v) for v in np.concatenate([[0], np.cumsum(COL_SIZES)])]
IN_COLS = COL_OFFS[-1]

kernel_name = "hybrid_nsa_gmlp_block"


def alibi_slopes():
    s = np.array([2.0 ** (-8.0 * (h + 1) / N_HEADS) for h in range(N_HEADS)], np.float32)
    return s.reshape(N_KV, Q_PER_KV)


def rmsnorm(x, g):
    xf = x.astype(jnp.float32)
    y = xf * lax.rsqrt(jnp.mean(xf * xf, axis=-1, keepdims=True) + EPS)
    return (y * g.astype(jnp.float32)).astype(x.dtype)


def layernorm(x, g, b):
    xf = x.astype(jnp.float32)
    mu = jnp.mean(xf, axis=-1, keepdims=True)
    var = jnp.mean(jnp.square(xf - mu), axis=-1, keepdims=True)
    y = (xf - mu) * lax.rsqrt(var + EPS)
    return (y * g.astype(jnp.float32) + b.astype(jnp.float32)).astype(x.dtype)


def masked_softmax(s, mask):
    s = jnp.where(mask, s, -jnp.inf)
    m = jnp.max(s, axis=-1, keepdims=True)
    m = jnp.where(jnp.isfinite(m), m, 0.0)
    e = jnp.exp(s - m)
    return e / jnp.maximum(jnp.sum(e, axis=-1, keepdims=True), 1e-30)


def compress(kv, pos, w1, b1, w2, b2):
    B, T = kv.shape[0], kv.shape[1]
    c = CMP_BLOCK // CMP_STRIDE
    nch = T // CMP_STRIDE
    nc = nch - c + 1
    kc = kv.reshape(B, nch, CMP_STRIDE, N_KV, HEAD_DIM)
    blocks = jnp.concatenate([kc[:, i:i + nc] for i in range(c)], axis=2)
    blocks = blocks + pos[None, None, :, None, :]
    flat = blocks.transpose(0, 1, 3, 2, 4).reshape(B, nc, N_KV, CMP_BLOCK * HEAD_DIM)
    hdn = jax.nn.gelu(flat @ w1 + b1)
    return hdn @ w2 + b2


def nsa_attention(q, k_c, v_c, k_s, v_s, k_w, v_w, gates):
    B, T = q.shape[0], q.shape[1]
    nc = k_c.shape[1]
    nb = T // SEL_BLOCK
    n_sel = min(N_SEL, nb)
    r = SEL_BLOCK // CMP_STRIDE
    c = CMP_BLOCK // CMP_STRIDE
    w_imp = [float(v) for v in np.convolve(np.ones(r), np.ones(c))]
    span = r * (nb - 1) + 1
    scale = HEAD_DIM ** -0.5
    slopes = jnp.asarray(alibi_slopes())[None, :, :, None, None]
    pos_c_end = jnp.arange(nc, dtype=jnp.int32) * CMP_STRIDE + (CMP_BLOCK - 1)
    pos_c_mid = jnp.arange(nc, dtype=jnp.int32).astype(jnp.float32) * CMP_STRIDE + (CMP_BLOCK - 1) / 2.0
    ks_blk = k_s.reshape(B, nb, SEL_BLOCK, N_KV, HEAD_DIM).transpose(0, 3, 1, 2, 4)
    vs_blk = v_s.reshape(B, nb, SEL_BLOCK, N_KV, HEAD_DIM).transpose(0, 3, 1, 2, 4)
    kw_pad = jnp.pad(k_w, ((0, 0), (WINDOW, 0), (0, 0), (0, 0)))
    vw_pad = jnp.pad(v_w, ((0, 0), (WINDOW, 0), (0, 0), (0, 0)))
    bi = jnp.arange(B)[:, None, None, None]
    gi = jnp.arange(N_KV)[None, :, None, None]
    blk = jnp.arange(nb, dtype=jnp.int32)

    def one_block(qb):
        q0 = qb * Q_BLOCK
        t = q0 + jnp.arange(Q_BLOCK, dtype=jnp.int32)
        tf = t.astype(jnp.float32)
        qq = lax.dynamic_slice_in_dim(q, q0, Q_BLOCK, axis=1)
        gg = lax.dynamic_slice_in_dim(gates, q0, Q_BLOCK, axis=1)
        s = jnp.einsum('bqgrd,bcgd->bgrqc', qq, k_c).astype(jnp.float32) * scale
        s = s - slopes * jnp.abs(tf[:, None] - pos_c_mid[None, :])
        p_c = masked_softmax(s, pos_c_end[None, :] <= t[:, None])
        o_c = jnp.einsum('bgrqc,bcgd->bqgrd', p_c.astype(v_c.dtype), v_c)
        imp_c = jnp.pad(jnp.sum(p_c, axis=2), ((0, 0), (0, 0), (0, 0), (c - 1, c - 1)))
        imp = w_imp[0] * imp_c[..., 0:span:r]
        for k in range(1, r + c - 1):
            imp = imp + w_imp[k] * imp_c[..., k:k + span:r]
        cur = t // SEL_BLOCK
        forced = (blk[None, :] == 0) | (blk[None, :] == cur[:, None]) | (blk[None, :] == cur[:, None] - 1)
        visible = blk[None, :] * SEL_BLOCK <= t[:, None]
        imp = jnp.where(forced, FORCED_SCORE, imp)
        imp = jnp.where(visible, imp, -1.0)
        _, idx = lax.top_k(imp, n_sel)
        k_g = ks_blk[bi, gi, idx].reshape(B, N_KV, Q_BLOCK, n_sel * SEL_BLOCK, HEAD_DIM)
        v_g = vs_blk[bi, gi, idx].reshape(B, N_KV, Q_BLOCK, n_sel * SEL_BLOCK, HEAD_DIM)
        pos_s = (idx[..., None] * SEL_BLOCK + jnp.arange(SEL_BLOCK, dtype=jnp.int32)).reshape(
            B, N_KV, Q_BLOCK, n_sel * SEL_BLOCK)
        dist_s = tf[None, None, :, None] - pos_s.astype(jnp.float32)
        s = jnp.einsum('bqgrd,bgqkd->bgrqk', qq, k_g).astype(jnp.float32) * scale
        s = s - slopes * dist_s[:, :, None]
        p_s = masked_softmax(s, (dist_s >= 0.0)[:, :, None])
        o_s = jnp.einsum('bgrqk,bgqkd->bqgrd', p_s.astype(v_g.dtype), v_g)
        kw = lax.dynamic_slice_in_dim(kw_pad, q0, WINDOW + Q_BLOCK, axis=1)
        vw = lax.dynamic_slice_in_dim(vw_pad, q0, WINDOW + Q_BLOCK, axis=1)
        pos_w = q0 - WINDOW + jnp.arange(WINDOW + Q_BLOCK, dtype=jnp.int32)
        dw = t[:, None] - pos_w[None, :]
        mask_w = (dw >= 0) & (dw < WINDOW) & (pos_w[None, :] >= 0)
        s = jnp.einsum('bqgrd,bkgd->bgrqk', qq, kw).astype(jnp.float32) * scale
        s = s - slopes * dw.astype(jnp.float32)
        p_w = masked_softmax(s, mask_w)
        o_w = jnp.einsum('bgrqk,bkgd->bqgrd', p_w.astype(vw.dtype), vw)
        return gg[..., 0:1] * o_c + gg[..., 1:2] * o_s + gg[..., 2:3] * o_w

    out = lax.map(one_block, jnp.arange(T // Q_BLOCK, dtype=jnp.int32))
    return out.transpose(1, 0, 2, 3, 4, 5).reshape(B, T, N_HEADS * HEAD_DIM)


def gmlp_mix(u, v, ln_g, ln_b, ws, bs):
    B, T = u.shape[0], u.shape[1]
    vn = layernorm(v, ln_g, ln_b)
    vc = vn.reshape(B, T // GMLP_CHUNK, GMLP_CHUNK, GMLP_GROUPS, GMLP_GDIM)
    w = ws * jnp.tril(jnp.ones((GMLP_CHUNK, GMLP_CHUNK), ws.dtype))
    mixed = jnp.einsum('gts,bcsgd->bctgd', w, vc) + bs.T[None, None, :, :, None]
    return u * mixed.reshape(B, T, D_GMLP)


def setup_inputs(seed: int = 0) -> dict:
    key = jax.random.key(seed)
    ks = jax.random.split(key, 40)
    L = DEPTH

    def nrm(k, shape, s):
        return jax.random.normal(k, shape, jnp.float32) * s

    def gain(k, shape):
        return 1.0 + 0.05 * jax.random.normal(k, shape, jnp.float32)

    return {
        "x": nrm(ks[0], (BATCH, SEQ, D_MODEL), 1.0),
        "p": nrm(ks[1], (DEPTH, BATCH, SEQ, D_PLE), 1.0),
        "g_mix": gain(ks[2], (L, D_MODEL)),
        "w_in": nrm(ks[3], (L, D_MODEL, IN_COLS), D_MODEL ** -0.5),
        "q_norm_g": gain(ks[4], (L, HEAD_DIM)),
        "kc_norm_g": gain(ks[5], (L, HEAD_DIM)),
        "ks_norm_g": gain(ks[6], (L, HEAD_DIM)),
        "kw_norm_g": gain(ks[7], (L, HEAD_DIM)),
        "cmp_pos_k": nrm(ks[8], (L, CMP_BLOCK, HEAD_DIM), 0.1),
        "cmp_pos_v": nrm(ks[9], (L, CMP_BLOCK, HEAD_DIM), 0.1),
        "cmp_k_w1": nrm(ks[10], (L, CMP_BLOCK * HEAD_DIM, CMP_HIDDEN), (CMP_BLOCK * HEAD_DIM) ** -0.5),
        "cmp_k_b1": nrm(ks[11], (L, CMP_HIDDEN), 0.02),
        "cmp_k_w2": nrm(ks[12], (L, CMP_HIDDEN, HEAD_DIM), CMP_HIDDEN ** -0.5),
        "cmp_k_b2": nrm(ks[13], (L, HEAD_DIM), 0.02),
        "cmp_v_w1": nrm(ks[14], (L, CMP_BLOCK * HEAD_DIM, CMP_HIDDEN), (CMP_BLOCK * HEAD_DIM) ** -0.5),
        "cmp_v_b1": nrm(ks[15], (L, CMP_HIDDEN), 0.02),
        "cmp_v_w2": nrm(ks[16], (L, CMP_HIDDEN, HEAD_DIM), CMP_HIDDEN ** -0.5),
        "cmp_v_b2": nrm(ks[17], (L, HEAD_DIM), 0.02),
        "gmlp_ln_g": gain(ks[18], (L, D_GMLP)),
        "gmlp_ln_b": nrm(ks[19], (L, D_GMLP), 0.02),
        "gmlp_ws": nrm(ks[20], (L, GMLP_GROUPS, GMLP_CHUNK, GMLP_CHUNK), GMLP_CHUNK ** -0.5),
        "gmlp_bs": 1.0 + nrm(ks[21], (L, GMLP_GROUPS, GMLP_CHUNK), 0.1),
        "out_g_nsa": gain(ks[22], (L, D_NSA)),
        "out_g_gmlp": gain(ks[23], (L, D_GMLP)),
        "w_out": nrm(ks[24], (L, D_MIX, D_MODEL), D_MIX ** -0.5),
        "g_ff": gain(ks[25], (L, D_MODEL)),
        "w_ff1": nrm(ks[26], (L, D_MODEL, D_FF), D_MODEL ** -0.5),
        "w_ff2": nrm(ks[27], (L, D_FF, D_MODEL), D_FF ** -0.5),
        "g_ple": gain(ks[28], (L, D_MODEL)),
        "w_ple_gate": nrm(ks[29], (L, D_MODEL, D_MODEL), D_MODEL ** -0.5),
        "w_ple": nrm(ks[30], (L, D_PLE, D_MODEL), D_PLE ** -0.5),
    }


def reference(x, p, g_mix, w_in, q_norm_g, kc_norm_g, ks_norm_g, kw_norm_g, cmp_pos_k, cmp_pos_v,
              cmp_k_w1, cmp_k_b1, cmp_k_w2, cmp_k_b2, cmp_v_w1, cmp_v_b1, cmp_v_w2, cmp_v_b2,
              gmlp_ln_g, gmlp_ln_b, gmlp_ws, gmlp_bs, out_g_nsa, out_g_gmlp, w_out,
              g_ff, w_ff1, w_ff2, g_ple, w_ple_gate, w_ple):
    B, T = x.shape[0], x.shape[1]
    o = COL_OFFS
    for i in range(DEPTH):
        h = rmsnorm(x, g_mix[i])
        z = h @ w_in[i]
        kv = lambda j: z[..., o[j]:o[j + 1]].reshape(B, T, N_KV, HEAD_DIM)
        q = rmsnorm(z[..., o[0]:o[1]].reshape(B, T, N_HEADS, HEAD_DIM), q_norm_g[i])
        q = q.reshape(B, T, N_KV, Q_PER_KV, HEAD_DIM)
        k_c = rmsnorm(compress(kv(1), cmp_pos_k[i], cmp_k_w1[i], cmp_k_b1[i], cmp_k_w2[i], cmp_k_b2[i]),
                      kc_norm_g[i])
        v_c = compress(kv(2), cmp_pos_v[i], cmp_v_w1[i], cmp_v_b1[i], cmp_v_w2[i], cmp_v_b2[i])
        k_s = rmsnorm(kv(3), ks_norm_g[i])
        v_s = kv(4)
        k_w = rmsnorm(kv(5), kw_norm_g[i])
        v_w = kv(6)
        gates = jax.nn.sigmoid(z[..., o[7]:o[8]]).reshape(B, T, N_KV, Q_PER_KV, 3)
        u = jax.nn.gelu(z[..., o[8]:o[9]])
        v = jax.nn.gelu(z[..., o[9]:o[10]])
        a_out = nsa_attention(q, k_c, v_c, k_s, v_s, k_w, v_w, gates)
        g_out = gmlp_mix(u, v, gmlp_ln_g[i], gmlp_ln_b[i], gmlp_ws[i], gmlp_bs[i])
        mix = jnp.concatenate([rmsnorm(a_out, out_g_nsa[i]), rmsnorm(g_out, out_g_gmlp[i])], axis=-1)
        x = x + mix @ w_out[i]
        h = rmsnorm(x, g_ff[i])
        x = x + jnp.square(jax.nn.relu(h @ w_ff1[i])) @ w_ff2[i]
        gate = jax.nn.sigmoid(rmsnorm(x, g_ple[i]) @ w_ple_gate[i])
        x = x + gate * (p[i] @ w_ple[i])
    return x
```

```python
import contextlib
import math
import numpy as np
import ml_dtypes
import concourse.bass as bass
import concourse.mybir as mybir
from concourse.bass_utils import run_bass_kernel_spmd

F32 = mybir.dt.float32
BF = mybir.dt.bfloat16
ALU = mybir.AluOpType
AF = mybir.ActivationFunctionType
AX = mybir.AxisListType
DSZ = {F32: 4, BF: 2}

D_MODEL = 1024
IN_COLS = 2328
D_FF = 4096
D_PLE = 256
EPS = 1e-6
NDS = 24
SLOPES = [2.0 ** (-(h + 1)) for h in range(8)]
NEG = -30000.0


class Prog:
    ENGS = ("pe", "act", "dve", "pool", "sp")

    def __init__(self):
        self.ops = {e: [] for e in self.ENGS}
        self.all = []
        self.track = {}
        self.seen = {e: {} for e in self.ENGS}
        self.seen_dma = {e: set() for e in self.ENGS}
        self.ndma = 0
        self.dram = set()
        self.psum = set()
        self.pending = {}

    def barrier(self):
        lasts = {E: self.ops[E][-1]["idx"] for E in self.ENGS if self.ops[E]}
        dmas = {}
        for op in self.all:
            if op["dma"]:
                dmas[op["dsem"]] = op["idx"]
        for X in self.ENGS:
            lst = self.pending.setdefault(X, [])
            for E, d in lasts.items():
                if E != X:
                    lst.append(d)
            lst.extend(dmas.values())

    def box(self, ap):
        name = ap.name
        a = ap.ap
        off = int(ap.offset)
        esz = DSZ.get(ap.dtype, 4)
        if name in self.dram:
            ext = 1
            for st, cnt in a:
                ext += (cnt - 1) * abs(st)
            return name, (0, 1, off * esz, (off + ext) * esz)
        if name in self.psum:
            return name, (0, 128, 0, 2048)
        pstride = a[0][0]
        if pstride == 0:
            p0, f0 = 0, off
        else:
            p0, f0 = off // pstride, off % pstride
        ext = 1
        for st, cnt in a[1:]:
            ext += (cnt - 1) * abs(st)
        return name, (p0, p0 + a[0][1], f0 * esz, (f0 + ext) * esz)

    @staticmethod
    def _ov(a, b):
        return a[0] < b[1] and b[0] < a[1] and a[2] < b[3] and b[2] < a[3]

    @staticmethod
    def _inside(a, b):
        return a[0] >= b[0] and a[1] <= b[1] and a[2] >= b[2] and a[3] <= b[3]

    def add(self, eng, fn, outs, ins, dma=False):
        idx = len(self.all)
        op = dict(eng=eng, fn=fn, waits=[], sig=False, dma=dma, seq=len(self.ops[eng]) + 1, idx=idx)
        deps = {}
        for d in self.pending.pop(eng, []):
            deps[d] = True
        inb = [self.box(a) for a in ins]
        outb = [self.box(a) for a in outs]
        for name, b in inb:
            isps = name in self.psum
            for key, ent in self.track.get(name, {}).items():
                if not self._ov(key[0], b):
                    continue
                if key[2] == "w":
                    deps[ent] = True
                elif isps and key[1] != eng:
                    deps.setdefault(ent, False)
        for name, b in outb:
            tr = self.track.setdefault(name, {})
            for key in list(tr.keys()):
                if self._ov(key[0], b):
                    deps.setdefault(tr[key], False)
                    if self._inside(key[0], b):
                        del tr[key]
        for name, b in inb:
            self.track.setdefault(name, {})[(b, eng, "r")] = idx
        for name, b in outb:
            self.track.setdefault(name, {})[(b, eng, "w")] = idx
        for d in sorted(deps):
            raw = deps[d]
            dop = self.all[d]
            if dop["dma"]:
                if d in self.seen_dma[eng]:
                    continue
                self.seen_dma[eng].add(d)
                op["waits"].append(("d", d))
            else:
                E = dop["eng"]
                if E == eng and not dma:
                    if eng == "pe":
                        continue
                if self.seen[eng].get(E, 0) >= dop["seq"]:
                    continue
                self.seen[eng][E] = dop["seq"]
                dop["sig"] = True
                op["waits"].append(("c", d))
        if dma:
            j = self.ndma
            self.ndma += 1
            op["dsem"] = j % NDS
            op["dval"] = 16 * (j // NDS + 1)
            op["dprev"] = 16 * (j // NDS)
        self.all.append(op)
        self.ops[eng].append(op)
        return op

    def emit(self, nc):
        with contextlib.ExitStack() as st:
            esem = {e: st.enter_context(nc.semaphore("s_" + e)) for e in self.ENGS}
            dsem = [st.enter_context(nc.semaphore("d%d" % i)) for i in range(NDS)]
            for e in self.ENGS:
                c = 0
                for op in self.ops[e]:
                    if op["sig"] and not op["dma"]:
                        c += 1
                        op["cnt"] = c
            dfinal = [0] * NDS
            for op in self.all:
                if op["dma"]:
                    dfinal[op["dsem"]] = max(dfinal[op["dsem"]], op["dval"])
            block = st.enter_context(nc.Block())

            def run(e, eng):
                for op in self.ops[e]:
                    if op["dma"] and op["dprev"] > 0:
                        eng.wait_ge(dsem[op["dsem"]], op["dprev"])
                    for kind, d in op["waits"]:
                        dop = self.all[d]
                        if kind == "d":
                            eng.wait_ge(dsem[dop["dsem"]], dop["dval"])
                        else:
                            eng.wait_ge(esem[dop["eng"]], dop["cnt"])
                    ins = op["fn"](eng)
                    if op["dma"]:
                        ins.then_inc(dsem[op["dsem"]], 16)
                    elif op["sig"]:
                        ins.then_inc(esem[e], 1)
                if e == "sp":
                    for i in range(NDS):
                        if dfinal[i] > 0:
                            eng.wait_ge(dsem[i], dfinal[i])

            block.tensor(lambda eng: run("pe", eng))
            block.scalar(lambda eng: run("act", eng))
            block.vector(lambda eng: run("dve", eng))
            block.gpsimd(lambda eng: run("pool", eng))
            block.sync(lambda eng: run("sp", eng))


def _aps(*xs):
    return [x for x in xs if x is not None and not isinstance(x, (int, float))]


class K:
    def __init__(self, P):
        self.P = P
        self.consts = {}

    def eps_ap(self, val, like):
        t = self.consts[round(float(val), 12)]
        p0 = like.base_partition()
        return t[p0:p0 + like.partition_size(), 0:1]

    def mm(self, out, lhsT, rhs, start=True, stop=True, skip=False):
        if skip:
            self.P.add("pe", lambda e: e.matmul(out, lhsT, rhs, start=start, stop=stop, skip_group_check=True), [out], [lhsT, rhs])
        else:
            self.P.add("pe", lambda e: e.matmul(out, lhsT, rhs, start=start, stop=stop), [out], [lhsT, rhs])

    def tr(self, out, in_, ident):
        self.P.add("pe", lambda e: e.transpose(out, in_, ident), [out], [in_, ident])

    def act(self, out, in_, func, bias=None, scale=None, accum=None):
        kw = {}
        if bias is not None:
            kw["bias"] = bias
        if scale is not None:
            kw["scale"] = scale
        if accum is not None:
            kw["accum_out"] = accum
        self.P.add("act", lambda e: e.activation(out, in_, func, **kw), _aps(out, accum), _aps(in_, bias, scale))

    def ts(self, eng, out, in0, s1, s2, op0, op1=None):
        if op1 is None:
            self.P.add(eng, lambda e: e.tensor_scalar(out, in0, s1, None, op0), [out], _aps(in0, s1))
        else:
            self.P.add(eng, lambda e: e.tensor_scalar(out, in0, s1, s2, op0, op1), [out], _aps(in0, s1, s2))

    def tt(self, eng, out, a, b, op):
        self.P.add(eng, lambda e: e.tensor_tensor(out, a, b, op), [out], [a, b])

    def stt(self, eng, out, in0, scalar, in1, op0, op1):
        self.P.add(eng, lambda e: e.scalar_tensor_tensor(out, in0, scalar, in1, op0, op1), [out], _aps(in0, scalar, in1))

    def rsq(self, out, in_, eps, mul=1.0):
        self.act(out, in_, AF.Ln, bias=float(eps))
        self.act(out, out, AF.Exp, scale=-0.5)

    def cp(self, eng, out, in_):
        if eng == "act":
            self.P.add("act", lambda e: e.copy(out, in_), [out], [in_])
        else:
            self.P.add(eng, lambda e: e.tensor_copy(out, in_), [out], [in_])

    def memset(self, eng, out, val):
        self.P.add(eng, lambda e: e.memset(out, val), [out], [])

    def red(self, out, in_, op=ALU.add):
        self.P.add("dve", lambda e: e.tensor_reduce(out, in_, AX.X, op), [out], [in_])

    def recip(self, out, in_):
        self.P.add("dve", lambda e: e.reciprocal(out, in_), [out], [in_])

    def max8(self, out, in_):
        self.P.add("dve", lambda e: e.max(out, in_), [out], [in_])

    def mrep(self, out, rep, vals, imm):
        self.P.add("dve", lambda e: e.match_replace(out, rep, vals, imm), [out], [rep, vals])

    def dma(self, out, in_, slow=False):
        if slow:
            self.P.add("sp", lambda e: e.dma_start(out=out, in_=in_, allow_slow_non_contiguous=True), [out], [in_], dma=True)
        else:
            self.P.add("sp", lambda e: e.dma_start(out=out, in_=in_), [out], [in_], dma=True)


def make_consts(T):
    NT = T // 128
    NB = T // 64
    NC = T // 16 - 1
    bf = ml_dtypes.bfloat16
    c = {}
    c["c_ident"] = np.eye(128, dtype=np.float32).astype(bf)
    key = np.arange(T)
    E = (key[None, :] // 64 == np.arange(64)[:, None]).astype(np.float32)
    c["c_E"] = E.astype(bf)
    D = 64.0 * (np.arange(64)[:, None] - (key[None, :] // 64))
    c["c_D"] = D.astype(np.float32).astype(bf)
    p = np.arange(128)[:, None]
    f = np.arange(128)[None, :]
    c["c_trile"] = (p <= f).astype(np.float32).astype(bf)
    c["c_trigt"] = (p > f).astype(np.float32).astype(bf)
    W = NC + 8 * (NT - 1)
    m = np.arange(W)[None, :] - 8 * (NT - 1)
    G = np.where(16 * m + 31 <= p, -(p - 16.0 * m - 15.5), -1.0e6)
    c["c_G"] = G.astype(np.float32)
    W2 = NB + 2 * (NT - 1)
    jp = np.arange(W2)[None, :] - 2 * (NT - 1)
    cur = (p >= 64).astype(np.int64)
    Fw = np.where(jp == cur, 2.0e4, np.where(jp == cur - 1, 1.0e4, 0.0))
    Uw = np.where(jp <= cur, 1.0e9, -1.0)
    c["c_Fw"] = Fw.astype(np.float32)
    c["c_Uw"] = Uw.astype(np.float32)
    wb = np.zeros((128, 8), np.float32)
    for h in range(8):
        wb[:, h] = SLOPES[h] * (np.arange(128) % 64)
    c["c_wb"] = wb
    return c


class _Stop(Exception):
    pass


def build_nc(T, stop=None):
    NT = T // 128
    NQ = T // 512
    NB = T // 64
    NC = T // 16 - 1
    NCP = T // 16
    NCT = (NCP + 127) // 128
    WG = NC + 8 * (NT - 1)
    W2 = NB + 2 * (NT - 1)
    nc = bass.Bass("TRN2", target_bir_lowering=False)
    P = Prog()
    k = K(P)

    def din(name, shape, dt=F32):
        P.dram.add(name)
        return nc.dram_tensor(name, list(shape), dt, kind="ExternalInput").ap()

    x_d = din("x", [T, D_MODEL])
    p_d = din("p", [T, D_PLE])
    g_mix_d = din("g_mix", [D_MODEL])
    w_in_d = din("w_in", [D_MODEL, IN_COLS])
    q_g_d = din("q_norm_g", [64])
    kc_g_d = din("kc_norm_g", [64])
    ks_g_d = din("ks_norm_g", [64])
    kw_g_d = din("kw_norm_g", [64])
    pos_k_d = din("cmp_pos_k", [32, 64])
    pos_v_d = din("cmp_pos_v", [32, 64])
    cw1_d = [din("cmp_k_w1", [2048, 256]), din("cmp_v_w1", [2048, 256])]
    cb1_d = [din("cmp_k_b1", [256]), din("cmp_v_b1", [256])]
    cw2_d = [din("cmp_k_w2", [256, 64]), din("cmp_v_w2", [256, 64])]
    cb2_d = [din("cmp_k_b2", [64]), din("cmp_v_b2", [64])]
    ln_g_d = din("gmlp_ln_g", [512])
    ln_b_d = din("gmlp_ln_b", [512])
    ws_d = din("gmlp_ws", [8, 128, 128])
    bs_d = din("gmlp_bs", [8, 128])
    og_nsa_d = din("out_g_nsa", [512])
    og_gmlp_d = din("out_g_gmlp", [512])
    w_out_d = din("w_out", [D_MODEL, D_MODEL])
    g_ff_d = din("g_ff", [D_MODEL])
    w_ff1_d = din("w_ff1", [D_MODEL, D_FF])
    w_ff2_d = din("w_ff2", [D_FF, D_MODEL])
    g_ple_d = din("g_ple", [D_MODEL])
    w_pg_d = din("w_ple_gate", [D_MODEL, D_MODEL])
    w_ple_d = din("w_ple", [D_PLE, D_MODEL])
    c_ident_d = din("c_ident", [128, 128], BF)
    c_E_d = din("c_E", [64, T], BF)
    c_D_d = din("c_D", [64, T], BF)
    c_trile_d = din("c_trile", [128, 128], BF)
    c_trigt_d = din("c_trigt", [128, 128], BF)
    c_G_d = din("c_G", [128, WG])
    c_Fw_d = din("c_Fw", [128, W2])
    c_Uw_d = din("c_Uw", [128, W2])
    c_wb_d = din("c_wb", [128, 8])
    P.dram.add("x1s")
    x1_d = nc.dram_tensor("x1s", [T, D_MODEL], F32, kind="Internal").ap()
    P.dram.add("out")
    out_d = nc.dram_tensor("out", [T, D_MODEL], F32, kind="ExternalOutput").ap()

    ES = contextlib.ExitStack()

    def ck(stage, aps):
        if stop != stage:
            return
        for i, a in enumerate(aps):
            n = a.shape[-1] if len(a.shape) == 2 else None
            d = stg[i % 2]
            k.cp("dve", d[0:a.shape[0], 0:n], a)
            k.dma(out_d[i * 128:i * 128 + a.shape[0], 0:n], d[0:a.shape[0], 0:n])
        raise _Stop()

    cur = [ES]

    def sb(name, shape, dt=F32, st=None):
        return (st or cur[0]).enter_context(nc.sbuf_tensor(name, list(shape), dt))

    def col(d_ap, n):
        return d_ap.rearrange("(p o) -> p o", o=1)

    try:
      with ES:
          ps = [ES.enter_context(nc.psum_tensor("ps%d" % i, [128, 512], F32)) for i in range(8)]
          for i in range(8):
              P.psum.add("ps%d" % i)
          psb = [t[:].bitcast(BF) for t in ps]

          ident = sb("ident", [128, 128], BF)
          k.dma(ident[:], c_ident_d)
          trile = sb("trile", [128, 128], BF)
          k.dma(trile[:], c_trile_d)
          trigt = sb("trigt", [128, 128], BF)
          k.dma(trigt[:], c_trigt_d)
          wb = sb("wb", [128, 8])
          k.dma(wb[:], c_wb_d)
          gcols = sb("gcols", [128, 4, 8])
          k.dma(gcols[:, 0, :], g_mix_d.rearrange("(k p) -> p k", p=128), slow=True)
          k.dma(gcols[:, 1, :], g_ff_d.rearrange("(k p) -> p k", p=128), slow=True)
          k.dma(gcols[:, 2, :], g_ple_d.rearrange("(k p) -> p k", p=128), slow=True)
          k.dma(gcols[:, 3, 0:4], og_nsa_d.rearrange("(k p) -> p k", p=128), slow=True)
          k.dma(gcols[:, 3, 4:8], og_gmlp_d.rearrange("(k p) -> p k", p=128), slow=True)
          eps_c = sb("eps_c", [128, 1]); k.memset("dve", eps_c[:], EPS)
          stg = [sb("stg%d" % i, [128, 2048]) for i in range(2)]
          cnt = [0]

          def load_w(dst_fn, src, nk, ncols, gcol=None, segs=None, stages=None):
              if segs is None:
                  segs = [(0, ncols, 0)]
              if stages is None:
                  stages = stg
              for kk in range(nk):
                  for c0 in range(0, ncols, 2048):
                      c1 = min(ncols, c0 + 2048)
                      s = stages[cnt[0] % len(stages)]
                      cnt[0] += 1
                      k.dma(s[:, 0:c1 - c0], src[kk * 128:(kk + 1) * 128, c0:c1])
                      for (a0, a1, d0) in segs:
                          lo, hi = max(a0, c0), min(a1, c1)
                          if lo >= hi:
                              continue
                          dst = dst_fn(kk, d0 + lo - a0, d0 + hi - a0)
                          if gcol is None:
                              k.cp(("act", "dve")[cnt[0] % 2], dst, s[:, lo - c0:hi - c0])
                          else:
                              k.act(dst, s[:, lo - c0:hi - c0], AF.Copy, scale=gcol[:, kk:kk + 1])

          SATT = contextlib.ExitStack()
          cur[0] = SATT
          Gw = sb("Gw", [128, WG])
          k.dma(Gw[:], c_G_d)
          Fw = sb("Fw", [128, W2])
          k.dma(Fw[:], c_Fw_d)
          Uw = sb("Uw", [128, W2])
          k.dma(Uw[:], c_Uw_d)
          Dt = sb("Dt", [128, T], BF)
          k.dma(Dt[64:128, :], c_D_d)
          KEs = [sb("KEs%d" % g, [128, T], BF) for g in range(2)]
          KEw = [sb("KEw%d" % g, [128, T], BF) for g in range(2)]
          for t_ in KEs + KEw:
              k.dma(t_[64:128, :], c_E_d)
          Vall = sb("Vall", [128, NT, 4, 66], BF)
          k.memset("dve", Vall[:], 1.0)
          KcT = [sb("KcT%d" % g, [64, NCT * 128], BF) for g in range(2)]
          Vc = [sb("Vc%d" % g, [128, NCT, 64], BF) for g in range(2)]
          gates = sb("gates", [128, NT, 24])
          rstd_g = sb("rstd_g", [128, NT])
          gq = sb("gq", [64, 1]); k.dma(gq[:], col(q_g_d, 64))
          gks8 = sb("gks8", [64, 1]); k.dma(gks8[:], col(ks_g_d, 64))
          gkw8 = sb("gkw8", [64, 1]); k.dma(gkw8[:], col(kw_g_d, 64))
          k.ts("dve", gks8[:], gks8[:], 8.0, None, ALU.mult)
          k.ts("dve", gkw8[:], gkw8[:], 8.0, None, ALU.mult)
          lng = sb("lng", [128, 512]); k.dma(lng[:], ln_g_d.partition_broadcast(128))
          lnb = sb("lnb", [128, 512]); k.dma(lnb[:], ln_b_d.partition_broadcast(128))
          bstab = sb("bstab", [128, 8]); k.dma(bstab[:], bs_d.rearrange("g t -> t g"), slow=True)

          def front(xt, hT, gidx_unused, junk, ssum, rs, hb, tsl):
              k.act(junk[:, 0:1024], xt, AF.Square, accum=ssum[:])
              k.rsq(rs[:], ssum[:], float(D_MODEL * EPS))
              k.ts("dve", hb[:], xt, rs[:, 0:1], 32.0, ALU.mult, ALU.mult)
              for kk in range(8):
                  k.tr(psb[4][:, kk * 128:(kk + 1) * 128], hb[:, kk * 128:(kk + 1) * 128], ident[:])
              k.cp("act", hT[:, :, tsl], psb[4][:, 0:1024].rearrange("p (k t) -> p k t", k=8))

          def gelu_to(out, src_ps, n, junk_a, junk_b, accum=None, p0=0, p1=128, bias=None):
              xv = junk_a[p0:p1, 0:n]
              sq = junk_b[p0:p1, 0:n]
              if bias is None:
                  k.act(xv, src_ps, AF.Copy)
                  k.act(sq, src_ps, AF.Square)
              else:
                  k.ts("dve", xv, src_ps, bias, None, ALU.add)
                  k.act(sq, src_ps, AF.Square, bias=bias)
              k.ts("dve", sq, sq, 0.044715, 1.0, ALU.mult, ALU.add)
              k.tt("pool", sq, sq, xv, ALU.mult)
              k.act(sq, sq, AF.Exp, scale=-1.5957691216)
              k.ts("dve", sq, sq, 1.0, None, ALU.add)
              k.recip(sq, sq)
              if accum is None:
                  k.tt("dve", out, sq, xv, ALU.mult)
              else:
                  P.add("dve", lambda e: e.scalar_tensor_tensor(out, sq, 1.0, xv, ALU.mult, ALU.mult, accum_out=accum),
                        [out, accum], [sq, xv])

          SA = contextlib.ExitStack()
          with SA:
              w_inA = sb("w_inA", [128, 8, 768], BF, SA)
              kvT = sb("kvT", [128, 2, T], BF, SA)
              segsA = [(0, 384, 0), (384, 512, 512), (512, 640, 384), (640, 768, 640)]
              load_w(lambda kk, a, b: w_inA[:, kk, a:b], w_in_d[:, 512:1280], 8, 768, gcols[:, 0, :], segsA)
              ck(1, [w_inA[:, 0, 0:512], w_inA[:, 7, 256:768]])
              xtA = [sb("xtA%d" % i, [128, 1024], F32, SA) for i in range(2)]
              hTA = [sb("hTA%d" % i, [128, 8, 128], BF, SA) for i in range(2)]
              junkA = sb("junkA", [128, 1024], F32, SA)
              hbA = sb("hbA", [128, 1024], BF, SA)
              ssA = sb("ssA", [128, 1], F32, SA)
              rsA = sb("rsA", [128, 1], F32, SA)
              ssk = sb("ssk", [128, 4], F32, SA)
              zbA = sb("zbA", [128, 512], BF, SA)
              def front_a(tt_):
                  xt = xtA[tt_ % 2]
                  hT = hTA[tt_ % 2]
                  k.dma(xt[:], x_d[tt_ * 128:(tt_ + 1) * 128, :])
                  front(xt[:], hT, 0, junkA, ssA, rsA, hbA, slice(0, 128))
                  b0, b1 = ps[2 * (tt_ % 2)], ps[2 * (tt_ % 2) + 1]
                  for kk in range(8):
                      k.mm(b0[:, 0:512], hT[:, kk, :], w_inA[:, kk, 0:512], start=(kk == 0), stop=(kk == 7))
                      k.mm(b1[:, 0:256], hT[:, kk, :], w_inA[:, kk, 512:768], start=(kk == 0), stop=(kk == 7))

              def post_a(tt_):
                  b0, b1 = ps[2 * (tt_ % 2)], ps[2 * (tt_ % 2) + 1]
                  k.act(junkA[:, 0:256], b0[:, 256:512], AF.Square)
                  k.red(ssk[:], junkA[:, 0:256].rearrange("p (a d) -> p a d", d=64))
                  k.rsq(ssk[:], ssk[:], 64.0 * EPS)
                  k.cp("act", zbA[:, 0:256], b0[:, 0:256])
                  k.tt("dve", zbA[:, 256:512].rearrange("p (a d) -> p a d", d=64),
                       b0[:, 256:512].rearrange("p (a d) -> p a d", d=64),
                       ssk[:].unsqueeze(2).to_broadcast([128, 4, 64]), ALU.mult)
                  k.cp("act", Vall[:, tt_, :, 0:64], b1[:, 0:256].rearrange("p (a d) -> p a d", d=64))
                  k.tr(psb[5][:, 0:128], zbA[:, 0:128], ident[:])
                  k.tr(psb[5][:, 128:256], zbA[:, 128:256], ident[:])
                  for a in range(4):
                      k.tr(psb[5][0:64, 256 + a * 128:384 + a * 128], zbA[:, 256 + a * 64:320 + a * 64], ident[:])
                  tsl = slice(tt_ * 128, (tt_ + 1) * 128)
                  k.cp("dve", kvT[:, :, tsl], psb[5][:, 0:256].rearrange("p (a t) -> p a t", a=2))
                  for g in range(2):
                      k.ts("dve", KEs[g][0:64, tsl], psb[5][0:64, 256 + g * 128:384 + g * 128], gks8[:, 0:1], None, ALU.mult)
                      k.ts("dve", KEw[g][0:64, tsl], psb[5][0:64, 512 + g * 128:640 + g * 128], gkw8[:, 0:1], None, ALU.mult)

              front_a(0)
              for tt_ in range(NT):
                  if tt_ + 1 < NT:
                      front_a(tt_ + 1)
                  post_a(tt_)

              ck(2, [KEs[0][:, 0:1024], kvT[:, 0, 0:1024], KEw[1][:, 0:1024], Vall[:, 3, :, :].rearrange("p a d -> p (a d)")])
              SC = contextlib.ExitStack()
              with SC:
                  w1r = sb("w1r", [128, 32, 256], BF, SC)
                  w2b = sb("w2b", [128, 2, 64], BF, SC)
                  posT = sb("posT", [128, 32], BF, SC)
                  posf = sb("posf", [128, 32], F32, SC)
                  b1c = sb("b1c", [128, 2], F32, SC)
                  bias1 = sb("bias1", [128, 2], F32, SC)
                  b2t = sb("b2t", [128, 64], F32, SC)
                  gkc = sb("gkc", [128, 64], F32, SC)
                  hdn = sb("hdn", [128, 2, NCT * 128], BF, SC)
                  ja = sb("cja", [128, 256], F32, SC)
                  jb = sb("cjb", [128, 256], F32, SC)
                  kcf = sb("kcf", [128, 64], F32, SC)
                  kcb = sb("kcb", [128, 64], BF, SC)
                  ssc = sb("ssc", [128, 1], F32, SC)
                  k.dma(gkc[:], kc_g_d.partition_broadcast(128))
                  k.memset("dve", hdn[:], 0.0)
                  for kv in range(2):
                      w1v = cw1_d[kv].rearrange("(l d) h -> d l h", d=64)
                      for half in range(2):
                          for lc in range(4):
                              s = stg[cnt[0] % 2]
                              cnt[0] += 1
                              k.dma(s[half * 64:(half + 1) * 64, :].rearrange("p (l h) -> p l h", h=256),
                                    w1v[:, lc * 8:(lc + 1) * 8, :])
                              k.cp(("pool", "dve")[lc % 2],
                                   w1r[half * 64:(half + 1) * 64, lc * 8:(lc + 1) * 8, :],
                                   s[half * 64:(half + 1) * 64, :].rearrange("p (l h) -> p l h", h=256))
                      s = stg[cnt[0] % 2]
                      cnt[0] += 1
                      k.dma(s[:, 0:128].rearrange("p (c o) -> p c o", c=2), cw2_d[kv].rearrange("(c p) o -> p c o", p=128))
                      k.cp("dve", w2b[:], s[:, 0:128].rearrange("p (c o) -> p c o", c=2))
                      pos_d = (pos_k_d, pos_v_d)[kv]
                      for half in range(2):
                          k.dma(posf[half * 64:(half + 1) * 64, :], pos_d.rearrange("l d -> d l"), slow=True)
                      k.cp("dve", posT[:], posf[:])
                      k.dma(b1c[:], cb1_d[kv].rearrange("(c p) -> p c", p=128), slow=True)
                      k.dma(b2t[:], cb2_d[kv].partition_broadcast(128))
                      for hh in range(2):
                          for l in range(32):
                              k.mm(ps[6][:, hh:hh + 1], w1r[0:64, l, hh * 128:(hh + 1) * 128], posT[0:64, l:l + 1],
                                   start=(l == 0), stop=(l == 31))
                      k.tt("dve", bias1[:], ps[6][:, 0:2], b1c[:], ALU.add)
                      for g in range(2):
                          pr = slice(g * 64, (g + 1) * 64)
                          for hh in range(2):
                              for l in range(32):
                                  k.mm(ps[hh][:, 0:NC], w1r[pr, l, hh * 128:(hh + 1) * 128],
                                       kvT[pr, kv, l:l + 16 * (NC - 1) + 1:16], start=(l == 0), stop=(l == 31))
                          for hh in range(2):
                              for c0 in range(0, NC, 256):
                                  c1 = min(NC, c0 + 256)
                                  gelu_to(hdn[:, hh, c0:c1], ps[hh][:, c0:c1], c1 - c0, ja, jb, bias=bias1[:, hh:hh + 1])
                          for ct in range(NCT):
                              ncv = min(128, NC - ct * 128)
                              for hh in range(2):
                                  k.mm(ps[2][0:ncv, 0:64], hdn[:, hh, ct * 128:ct * 128 + ncv], w2b[:, hh, :],
                                       start=(hh == 0), stop=(hh == 1))
                              k.tt("dve", kcf[0:ncv, :], ps[2][0:ncv, 0:64], b2t[0:ncv, :], ALU.add)
                              if kv == 1:
                                  if ncv < 128:
                                      k.memset("dve", Vc[g][:, ct, :], 0.0)
                                  k.cp("dve", Vc[g][0:ncv, ct, :], kcf[0:ncv, :])
                              else:
                                  k.act(ja[0:ncv, 0:64], kcf[0:ncv, :], AF.Square, accum=ssc[0:ncv, :])
                                  k.rsq(ssc[0:ncv, :], ssc[0:ncv, :], 64.0 * EPS)
                                  k.ts("dve", kcf[0:ncv, :], kcf[0:ncv, :], ssc[0:ncv, 0:1], 8.0, ALU.mult, ALU.mult)
                                  if ncv < 128:
                                      k.memset("dve", kcb[:], 0.0)
                                  k.tt("dve", kcb[0:ncv, :], kcf[0:ncv, :], gkc[0:ncv, :], ALU.mult)
                                  k.tr(psb[5][0:64, 0:128], kcb[:, :], ident[:])
                                  k.cp("dve", KcT[g][:, ct * 128:(ct + 1) * 128], psb[5][0:64, 0:128])

          P.barrier()
          ck(3, [KcT[0][:, 0:128], KcT[1][:, 0:128], Vc[0][:, 0, :], Vc[1][:, 0, :]])
          SB_ = contextlib.ExitStack()
          with SB_:
              w_inB = sb("w_inB", [128, 8, 1560], BF, SB_)
              load_w(lambda kk, a, b: w_inB[:, kk, a:b], w_in_d[:, 0:512], 8, 512, gcols[:, 0, :], [(0, 512, 0)])
              load_w(lambda kk, a, b: w_inB[:, kk, a:b], w_in_d[:, 1280:2328], 8, 1048, gcols[:, 0, :],
                     [(0, 24, 1536), (24, 1048, 512)])
              w_outb = sb("w_outb", [128, 8, 1024], BF, SB_)
              load_w(lambda kk, a, b: w_outb[:, kk, a:b], w_out_d, 4, 1024, gcols[:, 3, 0:4])
              load_w(lambda kk, a, b: w_outb[:, 4 + kk, a:b], w_out_d[512:1024, :], 4, 1024, gcols[:, 3, 4:8])
              WmT = sb("WmT", [128, 8, 128], BF, SB_)
              wsf = sb("wsf", [128, 128], F32, SB_)
              wsb = sb("wsb", [128, 128], BF, SB_)
              xs = sb("xs", [128, 4, 1024], F32, SB_)
              hTB = sb("hTB", [128, 8, 128], BF, SB_)
              junkB = stg[0][:, 1024:2048]
              junkC = stg[1][:, 1024:2048]
              hbB = sb("hbB", [128, 1024], BF, SB_)
              ssB = sb("ssB", [128, 1], F32, SB_)
              rsB = sb("rsB", [128, 1], F32, SB_)
              ssq = sb("ssq", [128, 8], F32, SB_)
              qnb = sb("qnb", [128, 512], BF, SB_)
              R = sb("R", [128, 8, 512], BF, SB_)
              u_sb = sb("u_sb", [128, 512], F32, SB_)
              v_sb = sb("v_sb", [128, 512], F32, SB_)
              vnb = sb("vnb", [128, 512], BF, SB_)
              st1 = sb("st1", [128, 4], F32, SB_)
              gof = v_sb
              gob = sb("gob", [128, 512], BF, SB_)
              goT = sb("goT", [128, 4, 512], BF, SB_)
              aoT = sb("aoT", [128, 4, 128], BF, SB_)
              ao = sb("ao", [128, 4, 512], F32, SB_)
              aob = qnb
              rstd_a = sb("rstd_a", [128, 1], F32, SB_)
              sc = sb("sc", [128, 2, 256], F32, SB_)
              ef = sb("ef", [128, 2, 8, 256], BF, SB_)
              ebT = sb("ebT", [128, 2, NCT, 128], BF, SB_)
              csum = sb("csum", [128, 2, 8], F32, SB_)
              crin = sb("crin", [128, 2, 8], F32, SB_)
              icp = sb("icp", [128, 2, NCP + 1], F32, SB_)
              imp = sb("imp", [128, 2, 64], F32, SB_)
              impw = sb("impw", [128, 2, 64], F32, SB_)
              m8 = sb("m8", [128, 2, 16], F32, SB_)
              bm = sb("bm", [128, 2, 128], BF, SB_)
              mbT = sb("mbT", [128, 2, 512], F32, SB_)
              ocs = sb("ocs", [128, 4, 8, 64], F32, SB_)
              pti = [0]
              PT = [sb("PT%d" % i, [128, 512], BF, SB_) for i in range(3)]
              fsum = sb("fsum", [128, 2, 4], F32, SB_)
              coef = sb("coef", [128, 3, 4], F32, SB_)
              tmpo = u_sb[:, 0:256].rearrange("p (a d) -> p a d", d=64)
              x1t = [stg[0][:, 0:1024], stg[1][:, 0:1024]]
              k.memset("dve", icp[:], 0.0)
              k.memset("dve", bm[:], 0.0)
              k.memset("dve", ef[:], 0.0)
              for g in range(8):
                  k.dma(wsf[:], ws_d[g])
                  k.tr(psb[5][:, 0:128], trile[:], ident[:])
                  k.tt("dve", wsb[:], wsf[:], psb[5][:, 0:128], ALU.mult)
                  k.tr(psb[5][:, 128:256], wsb[:], ident[:])
                  k.cp("dve", WmT[:, g, :], psb[5][:, 128:256])

              for Q in range(NQ):
                  ju_a, ju_b = stg[0][:, 0:512], stg[0][:, 512:1024]
                  jv_a, jv_b = stg[0][:, 1024:1536], stg[0][:, 1536:2048]
                  fjunk, qsq, dummy = stg[1][:, 0:1024], stg[1][:, 1024:1536], stg[1][:, 1536:2048]

                  def front_b(qs):
                      tt_ = Q * 4 + qs
                      xt = xs[:, qs, :]
                      k.dma(xt, x_d[tt_ * 128:(tt_ + 1) * 128, :])
                      front(xt, hTB, 0, fjunk, ssB, rsB, hbB, slice(0, 128))
                      for kk in range(8):
                          for cb in range(3):
                              k.mm(ps[cb][:, 0:512], hTB[:, kk, :], w_inB[:, kk, cb * 512:(cb + 1) * 512],
                                   start=(kk == 0), stop=(kk == 7))
                          k.mm(ps[3][:, 0:24], hTB[:, kk, :], w_inB[:, kk, 1536:1560], start=(kk == 0), stop=(kk == 7))

                  def post1_b(qs):
                      tt_ = Q * 4 + qs
                      k.act(qsq, ps[0][:, 0:512], AF.Square)
                      k.red(ssq[:], qsq.rearrange("p (a d) -> p a d", d=64))
                      k.rsq(ssq[:], ssq[:], 64.0 * EPS)
                      k.tt("dve", qnb[:].rearrange("p (a d) -> p a d", d=64),
                           ps[0][:, 0:512].rearrange("p (a d) -> p a d", d=64),
                           ssq[:].unsqueeze(2).to_broadcast([128, 8, 64]), ALU.mult)
                      k.act(gates[:, tt_, :], ps[3][:, 0:24], AF.Exp, scale=-1.0)
                      k.act(ju_a, ps[1][:, 0:512], AF.Copy)
                      k.act(ju_b, ps[1][:, 0:512], AF.Square)
                      k.act(jv_a, ps[2][:, 0:512], AF.Copy)
                      k.act(jv_b, ps[2][:, 0:512], AF.Square)

                  def post2_b(qs):
                      tt_ = Q * 4 + qs
                      k.ts("dve", gates[:, tt_, :], gates[:, tt_, :], 1.0, None, ALU.add)
                      k.recip(gates[:, tt_, :], gates[:, tt_, :])
                      prs = ((ju_a, ju_b), (jv_a, jv_b))
                      for xv, sq in prs:
                          k.ts("dve", sq, sq, 0.044715, 1.0, ALU.mult, ALU.add)
                      for xv, sq in prs:
                          k.tt("dve", sq, sq, xv, ALU.mult)
                      for xv, sq in prs:
                          k.act(sq, sq, AF.Exp, scale=-1.5957691216)
                      for xv, sq in prs:
                          k.act(sq, sq, AF.Ln, bias=1.0)
                      for xv, sq in prs:
                          k.act(sq, sq, AF.Exp, scale=-1.0)
                      k.tt("pool", u_sb[:], ju_b, ju_a, ALU.mult)
                      P.add("dve", lambda e: e.scalar_tensor_tensor(v_sb[:], jv_b, 1.0, jv_a, ALU.mult, ALU.mult, accum_out=st1[:, 0:1]),
                            [v_sb[:], st1[:, 0:1]], [jv_b, jv_a])
                      k.act(dummy, v_sb[:], AF.Square, accum=st1[:, 1:2])
                      k.ts("dve", st1[:, 2:3], st1[:, 0:1], 1.0 / 512, None, ALU.mult)
                      k.stt("dve", st1[:, 3:4], st1[:, 2:3], -1.0, st1[:, 2:3], ALU.mult, ALU.mult)
                      k.stt("dve", st1[:, 3:4], st1[:, 1:2], 1.0 / 512, st1[:, 3:4], ALU.mult, ALU.add)
                      k.rsq(st1[:, 3:4], st1[:, 3:4], EPS)
                      k.ts("dve", v_sb[:], v_sb[:], st1[:, 2:3], st1[:, 3:4], ALU.subtract, ALU.mult)
                      k.tt("dve", v_sb[:], v_sb[:], lng[:], ALU.mult)
                      k.tt("dve", vnb[:], v_sb[:], lnb[:], ALU.add)
                      for g in range(8):
                          k.mm(ps[6][:, g * 64:(g + 1) * 64], WmT[:, g, :], vnb[:, g * 64:(g + 1) * 64])
                      k.tt("dve", gof[:].rearrange("p (a d) -> p a d", d=64),
                           ps[6][:, 0:512].rearrange("p (a d) -> p a d", d=64),
                           bstab[:].unsqueeze(2).to_broadcast([128, 8, 64]), ALU.add)
                      k.tt("pool", gob[:], gof[:], u_sb[:], ALU.mult)
                      k.act(dummy, gob[:], AF.Square, accum=rstd_g[:, tt_:tt_ + 1])
                      k.rsq(rstd_g[:, tt_:tt_ + 1], rstd_g[:, tt_:tt_ + 1], 512.0 * EPS)
                      k.ts("dve", rstd_g[:, tt_:tt_ + 1], rstd_g[:, tt_:tt_ + 1], 22.627416998, None, ALU.mult)
                      for c4 in range(4):
                          k.tr(psb[7][:, c4 * 128:(c4 + 1) * 128], gob[:, c4 * 128:(c4 + 1) * 128], ident[:])
                      k.cp("act", goT[:, :, qs * 128:(qs + 1) * 128], psb[7][:, 0:512].rearrange("p (c t) -> p c t", c=4))
                      for h in range(8):
                          k.tr(psb[5][0:64, h * 128:(h + 1) * 128], qnb[:, h * 64:(h + 1) * 64], ident[:])
                      k.ts("dve", R[0:64, :, qs * 128:(qs + 1) * 128],
                           psb[5][0:64, 0:1024].rearrange("p (h t) -> p h t", h=8), gq[:, 0:1], None, ALU.mult)

                  front_b(0)
                  for qs in range(4):
                      post1_b(qs)
                      if qs + 1 < 4:
                          front_b(qs + 1)
                      post2_b(qs)

                  ck(4, [R[:, 0, :], R[:, 7, :], gof[:], goT[:, 0, :], gates[:, 0:4, :].rearrange("p a d -> p (a d)")])

                  def make_sel(qs):
                      qt = Q * 4 + qs
                      qp = qs % 2
                      ocb = ps[2 + qp]
                      th = []
                      th.append(lambda: k.ts("dve", csum[:, qp, :], csum[:, qp, :], 1e-30, None, ALU.max))
                      th.append(lambda: k.recip(crin[:, qp, :], csum[:, qp, :]))
                      th.append(lambda: k.tt("dve", ocs[:, qs, :, :], ocb[:, 0:512].rearrange("p (h d) -> p h d", h=8),
                                             crin[:, qp, :].unsqueeze(2).to_broadcast([128, 8, 64]), ALU.mult))
                      span = 4 * (NB - 1) + 1
                      for g in range(2):
                          th.append(lambda g=g: k.ts("dve", icp[:, g, 1:1 + NC], ef[:, qp, 4 * g, 0:NC],
                                                     crin[:, qp, 4 * g:4 * g + 1], None, ALU.mult))
                          for r in range(1, 4):
                              th.append(lambda g=g, h=4 * g + r: k.stt("dve", icp[:, g, 1:1 + NC], ef[:, qp, h, 0:NC],
                                                                       crin[:, qp, h:h + 1], icp[:, g, 1:1 + NC], ALU.mult, ALU.add))
                          th.append(lambda g=g: k.cp("dve", imp[:, g, 0:NB], icp[:, g, 0:span:4]))
                          for kk_, wk in ((1, 2.0), (2, 2.0), (3, 2.0), (4, 1.0)):
                              th.append(lambda g=g, kk_=kk_, wk=wk: k.stt("dve", imp[:, g, 0:NB], icp[:, g, kk_:kk_ + span:4], wk,
                                                                          imp[:, g, 0:NB], ALU.mult, ALU.add))
                      if NB < 64:
                          th.append(lambda: k.memset("dve", imp[:, :, NB:64], -1.0))
                      f0 = 2 * (NT - 1) - 2 * qt
                      for g in range(2):
                          th.append(lambda g=g: k.tt("dve", imp[:, g, 0:NB], imp[:, g, 0:NB], Fw[:, f0:f0 + NB], ALU.max))
                          th.append(lambda g=g: k.tt("dve", imp[:, g, 0:NB], imp[:, g, 0:NB], Uw[:, f0:f0 + NB], ALU.min))
                      th.append(lambda: k.memset("dve", imp[:, :, 0:1], 3.0e4))
                      for g in range(2):
                          th.append(lambda g=g: k.max8(m8[:, g, 0:8], imp[:, g, :]))
                          th.append(lambda g=g: k.mrep(impw[:, g, :], m8[:, g, 0:8], imp[:, g, :], -2.0))
                          th.append(lambda g=g: k.max8(m8[:, g, 8:16], impw[:, g, :]))
                          th.append(lambda g=g: k.ts("dve", impw[:, g, :], imp[:, g, :], m8[:, g, 15:16], None, ALU.is_ge))
                          th.append(lambda g=g: k.ts("dve", bm[:, g, 64:128], impw[:, g, :], -NEG, NEG, ALU.mult, ALU.add))
                          th.append(lambda g=g: k.tr(psb[5][:, g * 128:(g + 1) * 128], bm[:, g, :], ident[:]))
                          th.append(lambda g=g: k.cp("dve", mbT[64:128, g, qs * 128:(qs + 1) * 128],
                                                     psb[5][64:128, g * 128:(g + 1) * 128]))
                      return th

                  def heads(qs, pending):
                      qt = Q * 4 + qs
                      qp = qs % 2
                      nctq = min(NCT, (8 * qt + 7 + 127) // 128)
                      gs0 = 8 * (NT - 1) - 8 * qt
                      ocb = ps[2 + qp]

                      def stage_a(h):
                          g = h // 4
                          sbk = ps[0] if h % 2 == 0 else ps[6]
                          k.mm(sbk[:, 0:NC], R[0:64, h, qs * 128:(qs + 1) * 128], KcT[g][:, 0:NC])
                          k.stt("dve", sc[:, h % 2, 0:NC], Gw[:, gs0:gs0 + NC], SLOPES[h], sbk[:, 0:NC], ALU.mult, ALU.add)
                          k.act(ef[:, qp, h, 0:NC], sc[:, h % 2, 0:NC], AF.Exp, accum=csum[:, qp, h:h + 1])

                      def stage_b(h):
                          g = h // 4
                          tb = psb[1] if h % 2 == 0 else psb[7]
                          for ct in range(nctq):
                              k.tr(tb[:, ct * 128:(ct + 1) * 128], ef[:, qp, h, ct * 128:(ct + 1) * 128], ident[:])
                          k.cp("act", ebT[:, h % 2, 0:nctq, :], tb[:, 0:nctq * 128].rearrange("p (c t) -> p c t", c=nctq))
                          for ct in range(nctq):
                              k.mm(ocb[:, h * 64:(h + 1) * 64], ebT[:, h % 2, ct, :], Vc[g][:, ct, :],
                                   start=(ct == 0), stop=(ct == nctq - 1))

                      stage_a(0)
                      for h in range(8):
                          if h + 1 < 8:
                              stage_a(h + 1)
                          stage_b(h)
                          nd = (len(pending) + (7 - h)) // (8 - h)
                          for _ in range(nd):
                              pending.pop(0)()

                  pending = []
                  for qs in range(4):
                      heads(qs, pending)
                      assert not pending
                      pending = make_sel(qs)
                  for th_ in pending:
                      th_()

                  ck(5, [mbT[:, 0, :], mbT[:, 1, :], ocs[:, 3, :, :].rearrange("p a d -> p (a d)"), imp[:, 0, :]])
                  qcols = slice(Q * 512, (Q + 1) * 512)
                  gsl = slice(4 * Q, 4 * Q + 4)
                  tasks = []
                  for h in range(8):
                      for br in range(2):
                          kt_lo = max(0, 4 * Q - 4) if br == 0 else 0
                          kts = list(range(kt_lo, 4 * Q + 4))
                          for kt in kts:
                              tasks.append(dict(h=h, br=br, kt=kt, gfirst=(kt == kts[0]), glast=(kt == kts[-1]), n=len(tasks)))

                  def obank(h, br):
                      return (ps[6 + br] if h % 2 == 0 else ps[br])

                  def emit_S(t):
                      h, br, kt = t["h"], t["br"], t["kt"]
                      g = h // 4
                      if t["gfirst"]:
                          if br == 0:
                              k.ts("dve", R[64:128, h, :], Dt[64:128, qcols], SLOPES[h], None, ALU.mult)
                          else:
                              k.tt("dve", R[64:128, h, :], R[64:128, h, :], mbT[64:128, g, :], ALU.add)
                      i = kt - 4 * Q
                      c_lo = max(0, i)
                      c_hi = min(3, i + 4) if br == 0 else 3
                      t["c"] = (c_lo, c_hi)
                      n0, n1 = c_lo * 128, (c_hi + 1) * 128
                      KE = (KEw, KEs)[br][g]
                      sbank = (ps[3], ps[4], ps[5])[t["n"] % 3]
                      k.mm(sbank[:, n0:n1], KE[:, kt * 128:(kt + 1) * 128], R[:, h, n0:n1])

                  def emit_rest(t):
                      h, br, kt = t["h"], t["br"], t["kt"]
                      g = h // 4
                      i = kt - 4 * Q
                      c_lo, c_hi = t["c"]
                      n0, n1 = c_lo * 128, (c_hi + 1) * 128
                      sbank = (ps[3], ps[4], ps[5])[t["n"] % 3]
                      pt = PT[t["n"] % len(PT)]
                      Ob = obank(h, br)
                      vidx = (2 + g, g)[br]
                      k.act(pt[:, n0:n1], sbank[:, n0:n1], AF.Exp, bias=wb[:, h:h + 1])
                      if i >= 0:
                          k.tt("dve", pt[:, i * 128:(i + 1) * 128], pt[:, i * 128:(i + 1) * 128], trile[:], ALU.mult)
                      if br == 0 and 0 <= i + 4 <= 3:
                          cc = i + 4
                          k.tt("dve", pt[:, cc * 128:(cc + 1) * 128], pt[:, cc * 128:(cc + 1) * 128], trigt[:], ALU.mult)
                      for c in range(c_lo, c_hi + 1):
                          k.mm(Ob[:, c * 65:(c + 1) * 65], pt[:, c * 128:(c + 1) * 128], Vall[:, kt, vidx, 0:65],
                               start=(t["gfirst"] and c == c_lo), stop=(kt == 4 * Q + c), skip=True)
                      if br == 1 and t["glast"]:
                          Ow = obank(h, 0)[:, 0:260].rearrange("p (c d) -> p c d", d=65)
                          Os = obank(h, 1)[:, 0:260].rearrange("p (c d) -> p c d", d=65)
                          k.ts("dve", fsum[:, 0, :], Ow[:, :, 64], 1e-30, None, ALU.max)
                          k.ts("dve", fsum[:, 1, :], Os[:, :, 64], 1e-30, None, ALU.max)
                          k.recip(fsum[:], fsum[:])
                          k.tt("dve", coef[:, 0, :], fsum[:, 0, :], gates[:, gsl, 3 * h + 2], ALU.mult)
                          k.tt("dve", coef[:, 1, :], fsum[:, 1, :], gates[:, gsl, 3 * h + 1], ALU.mult)
                          dst = ao[:, :, h * 64:(h + 1) * 64]
                          k.tt("dve", dst, ocs[:, :, h, :], gates[:, gsl, 3 * h:3 * h + 1].to_broadcast([128, 4, 64]), ALU.mult)
                          k.tt("dve", tmpo, Ow[:, :, 0:64], coef[:, 0, :].unsqueeze(2).to_broadcast([128, 4, 64]), ALU.mult)
                          k.tt("dve", dst, dst, tmpo, ALU.add)
                          k.tt("dve", tmpo, Os[:, :, 0:64], coef[:, 1, :].unsqueeze(2).to_broadcast([128, 4, 64]), ALU.mult)
                          k.tt("dve", dst, dst, tmpo, ALU.add)

                  emit_S(tasks[0])
                  if len(tasks) > 1:
                      emit_S(tasks[1])
                  for n_, t in enumerate(tasks):
                      if n_ + 2 < len(tasks):
                          emit_S(tasks[n_ + 2])
                      emit_rest(t)

                  ck(6, [ao[:, 0, :], ao[:, 3, :]])
                  for qs in range(4):
                      tt_ = Q * 4 + qs
                      x1 = x1t[tt_ % 2]
                      k.act(junkB[:, 0:512], ao[:, qs, :], AF.Square, accum=rstd_a[:])
                      k.rsq(rstd_a[:], rstd_a[:], 512.0 * EPS)
                      k.ts("dve", rstd_a[:], rstd_a[:], 22.627416998, None, ALU.mult)
                      k.cp("pool", aob[:], ao[:, qs, :])
                      for c4 in range(4):
                          k.tr(psb[5][:, c4 * 128:(c4 + 1) * 128], aob[:, c4 * 128:(c4 + 1) * 128], ident[:])
                      k.cp("act", aoT[:], psb[5][:, 0:512].rearrange("p (c t) -> p c t", c=4))
                      for half in range(2):
                          hs = slice(half * 512, (half + 1) * 512)
                          for c4 in range(4):
                              k.mm(ps[half][:, :], aoT[:, c4, :], w_outb[:, c4, hs], start=(c4 == 0), stop=(c4 == 3))
                          for c4 in range(4):
                              k.mm(ps[2 + half][:, :], goT[:, c4, qs * 128:(qs + 1) * 128], w_outb[:, 4 + c4, hs],
                                   start=(c4 == 0), stop=(c4 == 3))
                          k.stt("dve", x1[:, hs], ps[half][:, :], rstd_a[:, 0:1], xs[:, qs, hs], ALU.mult, ALU.add)
                          k.stt("dve", x1[:, hs], ps[2 + half][:, :], rstd_g[:, tt_:tt_ + 1], x1[:, hs], ALU.mult, ALU.add)
                      k.dma(x1_d[tt_ * 128:(tt_ + 1) * 128, :], x1)
          SATT.close()
          cur[0] = ES
          P.barrier()

          SC3 = contextlib.ExitStack()
          with SC3:
              w1b = sb("w1b", [128, 8, 4096], BF, SC3)
              w2f = sb("w2f", [128, 32, 1024], BF, SC3)
              wpg = sb("wpg", [128, 8, 1024], BF, SC3)
              wpl = sb("wpl", [128, 2, 1024], BF, SC3)
              fT = sb("fT", [128, 32, 256], BF, SC3)
              fTf = fT[:].rearrange("p a b -> p (a b)").bitcast(F32)
              st4 = [stg[0], stg[1], fTf[:, 0:2048], fTf[:, 2048:4096]]
              load_w(lambda kk, a, b: w1b[:, kk, a:b], w_ff1_d, 8, D_FF, gcols[:, 1, :], stages=st4)
              load_w(lambda kk, a, b: w2f[:, kk, a:b], w_ff2_d, 32, D_MODEL, stages=st4)
              load_w(lambda kk, a, b: wpg[:, kk, a:b], w_pg_d, 8, D_MODEL, gcols[:, 2, :], stages=st4)
              load_w(lambda kk, a, b: wpl[:, kk, a:b], w_ple_d, 2, D_MODEL, stages=st4)
              xc = sb("xc", [128, 2, 1024], F32, SC3)
              x2 = sb("x2", [128, 2, 1024], F32, SC3)
              hTC = sb("hTC", [128, 8, 256], BF, SC3)
              h3T = sb("h3T", [128, 8, 128], BF, SC3)
              junkD = stg[0][:, 0:1024]
              hbC = sb("hbC", [128, 1024], BF, SC3)
              ssC = sb("ssC", [128, 1], F32, SC3)
              rsC = sb("rsC", [128, 1], F32, SC3)
              rls = [stg[1][:, 1024:1280], stg[1][:, 1536:1792]]
              ptf = stg[1][:, 1280:1536]
              ptb = sb("ptb", [128, 256], BF, SC3)
              pT = sb("pT", [128, 2, 128], BF, SC3)
              th = stg[0][:, 1024:1536]
              outt = stg[1][:, 0:1024]
              xcs = [xc, x2]
              th2 = [stg[0][:, 1024:1536], stg[0][:, 1536:2048]]
              NBT = T // 256

              def s1(b):
                  for j in range(2):
                      tt_ = 2 * b + j
                      k.dma(xcs[b % 2][:, j, :], x1_d[tt_ * 128:(tt_ + 1) * 128, :])
                      front(xcs[b % 2][:, j, :], hTC, 0, junkD, ssC, rsC, hbC, slice(j * 128, (j + 1) * 128))

              def drain(pend, slots_left):
                  nd = (len(pend) + slots_left - 1) // max(1, slots_left)
                  for _ in range(min(nd, len(pend))):
                      pend.pop(0)()

              def s2_s3(b, pend):
                  x2c = xcs[b % 2]
                  for fc in range(32):
                      bank = ps[fc % 2]
                      for kk in range(8):
                          k.mm(bank[:, 0:256], w1b[:, kk, fc * 128:(fc + 1) * 128], hTC[:, kk, :], start=(kk == 0), stop=(kk == 7))
                      rl = rls[fc % 2]
                      k.act(rl, bank[:, 0:256], AF.Relu)
                      k.tt(("pool", "dve")[fc % 2], fT[:, fc, :], rl, rl, ALU.mult)
                      drain(pend, 36 - fc)
                  for gi in range(4):
                      j, half = gi // 2, gi % 2
                      hs = slice(half * 512, (half + 1) * 512)
                      bank = ps[2 + gi % 2]
                      for fc in range(32):
                          k.mm(bank[:, :], fT[:, fc, j * 128:(j + 1) * 128], w2f[:, fc, hs], start=(fc == 0), stop=(fc == 31))
                      k.tt("dve", x2c[:, j, hs], bank[:, :], x2c[:, j, hs], ALU.add)
                      drain(pend, 4 - gi)

              def ple_thunks(b, j):
                  tt_ = 2 * b + j
                  xin = xcs[b % 2][:, j, :]
                  tl = []

                  def f_a():
                      k.act(junkD[:, 0:1024], xin, AF.Square, accum=ssC[:])
                      k.rsq(rsC[:], ssC[:], float(D_MODEL * EPS))
                      k.ts("dve", hbC[:], xin, rsC[:, 0:1], 32.0, ALU.mult, ALU.mult)
                      for kk in range(8):
                          k.tr(psb[4][:, kk * 128:(kk + 1) * 128], hbC[:, kk * 128:(kk + 1) * 128], ident[:])

                  def f_p():
                      k.dma(ptf, p_d[tt_ * 128:(tt_ + 1) * 128, :])
                      k.cp("pool", ptb[:], ptf)

                  def f_ptr():
                      for c2 in range(2):
                          k.tr(psb[5][:, c2 * 128:(c2 + 1) * 128], ptb[:, c2 * 128:(c2 + 1) * 128], ident[:])

                  tl.append(f_a)
                  tl.append(f_p)
                  tl.append(lambda: k.cp("act", h3T[:, :, 0:128], psb[4][:, 0:1024].rearrange("p (k t) -> p k t", k=8)))
                  tl.append(f_ptr)
                  tl.append(lambda: k.cp("act", pT[:], psb[5][:, 0:256].rearrange("p (c t) -> p c t", c=2)))
                  for half in range(2):
                      hs = slice(half * 512, (half + 1) * 512)
                      thh = th2[half]

                      def f_g(hs=hs):
                          for kk in range(8):
                              k.mm(ps[6][:, :], h3T[:, kk, :], wpg[:, kk, hs], start=(kk == 0), stop=(kk == 7))

                      def f_w(hs=hs):
                          for c2 in range(2):
                              k.mm(ps[7][:, :], pT[:, c2, :], wpl[:, c2, hs], start=(c2 == 0), stop=(c2 == 1))

                      tl.append(f_g)
                      tl.append(f_w)
                      tl.append(lambda thh=thh: k.act(thh, ps[6][:, :], AF.Exp, scale=-1.0))
                      tl.append(lambda thh=thh: k.act(thh, thh, AF.Ln, bias=1.0))
                      tl.append(lambda thh=thh: k.act(thh, thh, AF.Exp, scale=-1.0))
                      tl.append(lambda thh=thh: k.tt("dve", thh, thh, ps[7][:, :], ALU.mult))
                      tl.append(lambda thh=thh, hs=hs: k.tt("pool", outt[:, hs], thh, xin[:, hs], ALU.add))
                  tl.append(lambda: k.dma(out_d[tt_ * 128:(tt_ + 1) * 128, :], outt))
                  return tl

              pend = []
              s1(0)
              for b in range(NBT):
                  s2_s3(b, pend)
                  assert not pend
                  if b + 1 < NBT:
                      s1(b + 1)
                  pend = ple_thunks(b, 0) + ple_thunks(b, 1)
              for t_ in pend:
                  t_()
    except _Stop:
        pass
    P.emit(nc)
    return nc


_NC_CACHE = {}


def _core_inputs(inp, b, consts):
    sq = lambda a: np.ascontiguousarray(np.asarray(a)[0], dtype=np.float32)
    m = {
        "x": np.ascontiguousarray(np.asarray(inp["x"])[b], dtype=np.float32),
        "p": np.ascontiguousarray(np.asarray(inp["p"])[0, b], dtype=np.float32),
    }
    for name in ("g_mix", "w_in", "q_norm_g", "kc_norm_g", "ks_norm_g", "kw_norm_g", "cmp_pos_k", "cmp_pos_v",
                 "cmp_k_w1", "cmp_k_b1", "cmp_k_w2", "cmp_k_b2", "cmp_v_w1", "cmp_v_b1", "cmp_v_w2", "cmp_v_b2",
                 "gmlp_ln_g", "gmlp_ln_b", "gmlp_ws", "gmlp_bs", "out_g_nsa", "out_g_gmlp", "w_out",
                 "g_ff", "w_ff1", "w_ff2", "g_ple", "w_ple_gate", "w_ple"):
        m[name] = sq(inp[name])
    m.update(consts)
    return m


def kernel(_stop=None, **inputs):
    x = np.asarray(inputs["x"])
    B, T = x.shape[0], x.shape[1]
    if T not in _NC_CACHE:
        _NC_CACHE[T] = build_nc(T, _stop)
    nc = _NC_CACHE[T]
    consts = make_consts(T)
    in_maps = [_core_inputs(inputs, b, consts) for b in range(B)]
    res = run_bass_kernel_spmd(nc, in_maps, core_ids=list(range(B)))
    return np.stack([np.asarray(r["out"], dtype=np.float32) for r in res.results], axis=0)
```

```python
import contextlib
import math
import numpy as np
import ml_dtypes
import concourse.bass as bass
import concourse.mybir as mybir
from concourse.bass_utils import run_bass_kernel_spmd

F32 = mybir.dt.float32
BF = mybir.dt.bfloat16
ALU = mybir.AluOpType
AF = mybir.ActivationFunctionType
AX = mybir.AxisListType
DSZ = {F32: 4, BF: 2}

D_MODEL = 1024
IN_COLS = 2328
D_FF = 4096
D_PLE = 256
EPS = 1e-6
NDS = 24
SLOPES = [2.0 ** (-(h + 1)) for h in range(8)]
NEG = -30000.0


class Prog:
    ENGS = ("pe", "act", "dve", "pool", "sp")

    def __init__(self):
        self.ops = {e: [] for e in self.ENGS}
        self.all = []
        self.track = {}
        self.seen = {e: {} for e in self.ENGS}
        self.seen_dma = {e: set() for e in self.ENGS}
        self.ndma = 0
        self.dram = set()
        self.psum = set()
        self.pending = {}

    def barrier(self):
        lasts = {E: self.ops[E][-1]["idx"] for E in self.ENGS if self.ops[E]}
        dmas = {}
        for op in self.all:
            if op["dma"]:
                dmas[op["dsem"]] = op["idx"]
        for X in self.ENGS:
            lst = self.pending.setdefault(X, [])
            for E, d in lasts.items():
                if E != X:
                    lst.append(d)
            lst.extend(dmas.values())

    def box(self, ap):
        name = ap.name
        a = ap.ap
        off = int(ap.offset)
        esz = DSZ.get(ap.dtype, 4)
        if name in self.dram:
            ext = 1
            for st, cnt in a:
                ext += (cnt - 1) * abs(st)
            return name, (0, 1, off * esz, (off + ext) * esz)
        if name in self.psum:
            return name, (0, 128, 0, 2048)
        pstride = a[0][0]
        if pstride == 0:
            p0, f0 = 0, off
        else:
            p0, f0 = off // pstride, off % pstride
        ext = 1
        for st, cnt in a[1:]:
            ext += (cnt - 1) * abs(st)
        return name, (p0, p0 + a[0][1], f0 * esz, (f0 + ext) * esz)

    @staticmethod
    def _ov(a, b):
        return a[0] < b[1] and b[0] < a[1] and a[2] < b[3] and b[2] < a[3]

    @staticmethod
    def _inside(a, b):
        return a[0] >= b[0] and a[1] <= b[1] and a[2] >= b[2] and a[3] <= b[3]

    def add(self, eng, fn, outs, ins, dma=False):
        idx = len(self.all)
        op = dict(eng=eng, fn=fn, waits=[], sig=False, dma=dma, seq=len(self.ops[eng]) + 1, idx=idx)
        deps = {}
        for d in self.pending.pop(eng, []):
            deps[d] = True
        inb = [self.box(a) for a in ins]
        outb = [self.box(a) for a in outs]
        for name, b in inb:
            isps = name in self.psum
            for key, ent in self.track.get(name, {}).items():
                if not self._ov(key[0], b):
                    continue
                if key[2] == "w":
                    deps[ent] = True
                elif isps and key[1] != eng:
                    deps.setdefault(ent, False)
        for name, b in outb:
            tr = self.track.setdefault(name, {})
            for key in list(tr.keys()):
                if self._ov(key[0], b):
                    deps.setdefault(tr[key], False)
                    if self._inside(key[0], b):
                        del tr[key]
        for name, b in inb:
            self.track.setdefault(name, {})[(b, eng, "r")] = idx
        for name, b in outb:
            self.track.setdefault(name, {})[(b, eng, "w")] = idx
        for d in sorted(deps):
            raw = deps[d]
            dop = self.all[d]
            if dop["dma"]:
                if d in self.seen_dma[eng]:
                    continue
                self.seen_dma[eng].add(d)
                op["waits"].append(("d", d))
            else:
                E = dop["eng"]
                if E == eng and not dma:
                    if eng == "pe":
                        continue
                if self.seen[eng].get(E, 0) >= dop["seq"]:
                    continue
                self.seen[eng][E] = dop["seq"]
                dop["sig"] = True
                op["waits"].append(("c", d))
        if dma:
            j = self.ndma
            self.ndma += 1
            op["dsem"] = j % NDS
            op["dval"] = 16 * (j // NDS + 1)
            op["dprev"] = 16 * (j // NDS)
        self.all.append(op)
        self.ops[eng].append(op)
        return op

    def emit(self, nc):
        with contextlib.ExitStack() as st:
            esem = {e: st.enter_context(nc.semaphore("s_" + e)) for e in self.ENGS}
            dsem = [st.enter_context(nc.semaphore("d%d" % i)) for i in range(NDS)]
            for e in self.ENGS:
                c = 0
                for op in self.ops[e]:
                    if op["sig"] and not op["dma"]:
                        c += 1
                        op["cnt"] = c
            dfinal = [0] * NDS
            for op in self.all:
                if op["dma"]:
                    dfinal[op["dsem"]] = max(dfinal[op["dsem"]], op["dval"])
            block = st.enter_context(nc.Block())

            def run(e, eng):
                for op in self.ops[e]:
                    if op["dma"] and op["dprev"] > 0:
                        eng.wait_ge(dsem[op["dsem"]], op["dprev"])
                    for kind, d in op["waits"]:
                        dop = self.all[d]
                        if kind == "d":
                            eng.wait_ge(dsem[dop["dsem"]], dop["dval"])
                        else:
                            eng.wait_ge(esem[dop["eng"]], dop["cnt"])
                    ins = op["fn"](eng)
                    if op["dma"]:
                        ins.then_inc(dsem[op["dsem"]], 16)
                    elif op["sig"]:
                        ins.then_inc(esem[e], 1)
                if e == "sp":
                    for i in range(NDS):
                        if dfinal[i] > 0:
                            eng.wait_ge(dsem[i], dfinal[i])

            block.tensor(lambda eng: run("pe", eng))
            block.scalar(lambda eng: run("act", eng))
            block.vector(lambda eng: run("dve", eng))
            block.gpsimd(lambda eng: run("pool", eng))
            block.sync(lambda eng: run("sp", eng))


def _aps(*xs):
    return [x for x in xs if x is not None and not isinstance(x, (int, float))]


class K:
    def __init__(self, P):
        self.P = P
        self.consts = {}

    def eps_ap(self, val, like):
        t = self.consts[round(float(val), 12)]
        p0 = like.base_partition()
        return t[p0:p0 + like.partition_size(), 0:1]

    def mm(self, out, lhsT, rhs, start=True, stop=True, skip=False):
        if skip:
            self.P.add("pe", lambda e: e.matmul(out, lhsT, rhs, start=start, stop=stop, skip_group_check=True), [out], [lhsT, rhs])
        else:
            self.P.add("pe", lambda e: e.matmul(out, lhsT, rhs, start=start, stop=stop), [out], [lhsT, rhs])

    def tr(self, out, in_, ident):
        self.P.add("pe", lambda e: e.transpose(out, in_, ident), [out], [in_, ident])

    def act(self, out, in_, func, bias=None, scale=None, accum=None):
        kw = {}
        if bias is not None:
            kw["bias"] = bias
        if scale is not None:
            kw["scale"] = scale
        if accum is not None:
            kw["accum_out"] = accum
        self.P.add("act", lambda e: e.activation(out, in_, func, **kw), _aps(out, accum), _aps(in_, bias, scale))

    def ts(self, eng, out, in0, s1, s2, op0, op1=None):
        if op1 is None:
            self.P.add(eng, lambda e: e.tensor_scalar(out, in0, s1, None, op0), [out], _aps(in0, s1))
        else:
            self.P.add(eng, lambda e: e.tensor_scalar(out, in0, s1, s2, op0, op1), [out], _aps(in0, s1, s2))

    def tt(self, eng, out, a, b, op):
        self.P.add(eng, lambda e: e.tensor_tensor(out, a, b, op), [out], [a, b])

    def stt(self, eng, out, in0, scalar, in1, op0, op1):
        self.P.add(eng, lambda e: e.scalar_tensor_tensor(out, in0, scalar, in1, op0, op1), [out], _aps(in0, scalar, in1))

    def rsq(self, out, in_, eps, mul=1.0):
        self.act(out, in_, AF.Ln, bias=float(eps))
        self.act(out, out, AF.Exp, scale=-0.5)

    def cp(self, eng, out, in_):
        if eng == "act":
            self.P.add("act", lambda e: e.copy(out, in_), [out], [in_])
        else:
            self.P.add(eng, lambda e: e.tensor_copy(out, in_), [out], [in_])

    def memset(self, eng, out, val):
        self.P.add(eng, lambda e: e.memset(out, val), [out], [])

    def red(self, out, in_, op=ALU.add):
        self.P.add("dve", lambda e: e.tensor_reduce(out, in_, AX.X, op), [out], [in_])

    def recip(self, out, in_):
        self.P.add("dve", lambda e: e.reciprocal(out, in_), [out], [in_])

    def max8(self, out, in_):
        self.P.add("dve", lambda e: e.max(out, in_), [out], [in_])

    def mrep(self, out, rep, vals, imm):
        self.P.add("dve", lambda e: e.match_replace(out, rep, vals, imm), [out], [rep, vals])

    def dma(self, out, in_, slow=False, eng="sp"):
        if slow:
            self.P.add(eng, lambda e: e.dma_start(out=out, in_=in_, allow_slow_non_contiguous=True), [out], [in_], dma=True)
        else:
            self.P.add(eng, lambda e: e.dma_start(out=out, in_=in_), [out], [in_], dma=True)


def make_consts(T):
    NT = T // 128
    NB = T // 64
    NC = T // 16 - 1
    bf = ml_dtypes.bfloat16
    c = {}
    c["c_ident"] = np.eye(128, dtype=np.float32).astype(bf)
    key = np.arange(T)
    E = (key[None, :] // 64 == np.arange(64)[:, None]).astype(np.float32)
    c["c_E"] = E.astype(bf)
    D = 64.0 * (np.arange(64)[:, None] - (key[None, :] // 64))
    c["c_D"] = D.astype(np.float32).astype(bf)
    p = np.arange(128)[:, None]
    f = np.arange(128)[None, :]
    c["c_trile"] = (p <= f).astype(np.float32).astype(bf)
    c["c_trigt"] = (p > f).astype(np.float32).astype(bf)
    W = NC + 8 * (NT - 1)
    m = np.arange(W)[None, :] - 8 * (NT - 1)
    G = np.where(16 * m + 31 <= p, -(p - 16.0 * m - 15.5), -1.0e6)
    c["c_G"] = G.astype(np.float32)
    W2 = NB + 2 * (NT - 1)
    jp = np.arange(W2)[None, :] - 2 * (NT - 1)
    cur = (p >= 64).astype(np.int64)
    Fw = np.where(jp == cur, 2.0e4, np.where(jp == cur - 1, 1.0e4, 0.0))
    Uw = np.where(jp <= cur, 1.0e9, -1.0)
    c["c_Fw"] = Fw.astype(np.float32)
    c["c_Uw"] = Uw.astype(np.float32)
    wb = np.zeros((128, 8), np.float32)
    for h in range(8):
        wb[:, h] = SLOPES[h] * (np.arange(128) % 64)
    c["c_wb"] = wb
    return c


class _Stop(Exception):
    pass


def build_nc(T, stop=None):
    NT = T // 128
    NQ = T // 512
    NB = T // 64
    NC = T // 16 - 1
    NCP = T // 16
    NCT = (NCP + 127) // 128
    WG = NC + 8 * (NT - 1)
    W2 = NB + 2 * (NT - 1)
    nc = bass.Bass("TRN2", target_bir_lowering=False)
    P = Prog()
    k = K(P)

    def din(name, shape, dt=F32):
        P.dram.add(name)
        return nc.dram_tensor(name, list(shape), dt, kind="ExternalInput").ap()

    x_d = din("x", [T, D_MODEL])
    p_d = din("p", [T, D_PLE])
    g_mix_d = din("g_mix", [D_MODEL])
    w_in_d = din("w_in", [D_MODEL, IN_COLS])
    q_g_d = din("q_norm_g", [64])
    kc_g_d = din("kc_norm_g", [64])
    ks_g_d = din("ks_norm_g", [64])
    kw_g_d = din("kw_norm_g", [64])
    pos_k_d = din("cmp_pos_k", [32, 64])
    pos_v_d = din("cmp_pos_v", [32, 64])
    cw1_d = [din("cmp_k_w1", [2048, 256]), din("cmp_v_w1", [2048, 256])]
    cb1_d = [din("cmp_k_b1", [256]), din("cmp_v_b1", [256])]
    cw2_d = [din("cmp_k_w2", [256, 64]), din("cmp_v_w2", [256, 64])]
    cb2_d = [din("cmp_k_b2", [64]), din("cmp_v_b2", [64])]
    ln_g_d = din("gmlp_ln_g", [512])
    ln_b_d = din("gmlp_ln_b", [512])
    ws_d = din("gmlp_ws", [8, 128, 128])
    bs_d = din("gmlp_bs", [8, 128])
    og_nsa_d = din("out_g_nsa", [512])
    og_gmlp_d = din("out_g_gmlp", [512])
    w_out_d = din("w_out", [D_MODEL, D_MODEL])
    g_ff_d = din("g_ff", [D_MODEL])
    w_ff1_d = din("w_ff1", [D_MODEL, D_FF])
    w_ff2_d = din("w_ff2", [D_FF, D_MODEL])
    g_ple_d = din("g_ple", [D_MODEL])
    w_pg_d = din("w_ple_gate", [D_MODEL, D_MODEL])
    w_ple_d = din("w_ple", [D_PLE, D_MODEL])
    c_ident_d = din("c_ident", [128, 128], BF)
    c_E_d = din("c_E", [64, T], BF)
    c_D_d = din("c_D", [64, T], BF)
    c_trile_d = din("c_trile", [128, 128], BF)
    c_trigt_d = din("c_trigt", [128, 128], BF)
    c_G_d = din("c_G", [128, WG])
    c_Fw_d = din("c_Fw", [128, W2])
    c_Uw_d = din("c_Uw", [128, W2])
    c_wb_d = din("c_wb", [128, 8])
    P.dram.add("x1s")
    x1_d = nc.dram_tensor("x1s", [T, D_MODEL], F32, kind="Internal").ap()
    P.dram.add("out")
    out_d = nc.dram_tensor("out", [T, D_MODEL], F32, kind="ExternalOutput").ap()

    ES = contextlib.ExitStack()

    def ck(stage, aps):
        if stop != stage:
            return
        for i, a in enumerate(aps):
            n = a.shape[-1] if len(a.shape) == 2 else None
            d = stg[i % 2]
            k.cp("dve", d[0:a.shape[0], 0:n], a)
            k.dma(out_d[i * 128:i * 128 + a.shape[0], 0:n], d[0:a.shape[0], 0:n])
        raise _Stop()

    cur = [ES]

    def sb(name, shape, dt=F32, st=None):
        return (st or cur[0]).enter_context(nc.sbuf_tensor(name, list(shape), dt))

    def col(d_ap, n):
        return d_ap.rearrange("(p o) -> p o", o=1)

    try:
      with ES:
          ps = [ES.enter_context(nc.psum_tensor("ps%d" % i, [128, 512], F32)) for i in range(8)]
          for i in range(8):
              P.psum.add("ps%d" % i)
          psb = [t[:].bitcast(BF) for t in ps]

          ident = sb("ident", [128, 128], BF)
          k.dma(ident[:], c_ident_d)
          trile = sb("trile", [128, 128], BF)
          k.dma(trile[:], c_trile_d)
          trigt = sb("trigt", [128, 128], BF)
          k.dma(trigt[:], c_trigt_d)
          wb = sb("wb", [128, 8])
          k.dma(wb[:], c_wb_d)
          gcols = sb("gcols", [128, 4, 8])
          k.dma(gcols[:, 0, :], g_mix_d.rearrange("(k p) -> p k", p=128), slow=True)
          k.dma(gcols[:, 1, :], g_ff_d.rearrange("(k p) -> p k", p=128), slow=True)
          k.dma(gcols[:, 2, :], g_ple_d.rearrange("(k p) -> p k", p=128), slow=True)
          k.dma(gcols[:, 3, 0:4], og_nsa_d.rearrange("(k p) -> p k", p=128), slow=True)
          k.dma(gcols[:, 3, 4:8], og_gmlp_d.rearrange("(k p) -> p k", p=128), slow=True)
          eps_c = sb("eps_c", [128, 1]); k.memset("dve", eps_c[:], EPS)
          stg = [sb("stg%d" % i, [128, 2048]) for i in range(2)]
          cnt = [0]

          def load_w(dst_fn, src, nk, ncols, gcol=None, segs=None, defer=None):
              if segs is None:
                  segs = [(0, ncols, 0)]
              for kk in range(nk):
                  for c0 in range(0, ncols, 2048):
                      c1 = min(ncols, c0 + 2048)

                      def unit(kk=kk, c0=c0, c1=c1):
                          s = stg[cnt[0] % 2]
                          cnt[0] += 1
                          k.dma(s[:, 0:c1 - c0], src[kk * 128:(kk + 1) * 128, c0:c1])
                          for (a0, a1, d0) in segs:
                              lo, hi = max(a0, c0), min(a1, c1)
                              if lo >= hi:
                                  continue
                              dst = dst_fn(kk, d0 + lo - a0, d0 + hi - a0)
                              if gcol is None:
                                  k.cp(("act", "dve")[cnt[0] % 2], dst, s[:, lo - c0:hi - c0])
                              else:
                                  k.act(dst, s[:, lo - c0:hi - c0], AF.Copy, scale=gcol[:, kk:kk + 1])

                      if defer is None:
                          unit()
                      else:
                          defer.append(unit)

          SATT = contextlib.ExitStack()
          cur[0] = SATT
          Gw = sb("Gw", [128, WG])
          k.dma(Gw[:], c_G_d)
          Fw = sb("Fw", [128, W2])
          k.dma(Fw[:], c_Fw_d)
          Uw = sb("Uw", [128, W2])
          k.dma(Uw[:], c_Uw_d)
          Dt = sb("Dt", [128, T], BF)
          k.dma(Dt[64:128, :], c_D_d)
          KEs = [sb("KEs%d" % g, [128, T], BF) for g in range(2)]
          KEw = [sb("KEw%d" % g, [128, T], BF) for g in range(2)]
          for t_ in KEs + KEw:
              k.dma(t_[64:128, :], c_E_d)
          Vall = sb("Vall", [128, NT, 4, 66], BF)
          k.memset("dve", Vall[:], 1.0)
          KcT = [sb("KcT%d" % g, [64, NCT * 128], BF) for g in range(2)]
          Vc = [sb("Vc%d" % g, [128, NCT, 64], BF) for g in range(2)]
          gates = sb("gates", [128, NT, 24])
          rstd_g = sb("rstd_g", [128, NT])
          gq = sb("gq", [64, 1]); k.dma(gq[:], col(q_g_d, 64))
          gks8 = sb("gks8", [64, 1]); k.dma(gks8[:], col(ks_g_d, 64))
          gkw8 = sb("gkw8", [64, 1]); k.dma(gkw8[:], col(kw_g_d, 64))
          k.ts("dve", gks8[:], gks8[:], 8.0, None, ALU.mult)
          k.ts("dve", gkw8[:], gkw8[:], 8.0, None, ALU.mult)
          lng = sb("lng", [128, 512]); k.dma(lng[:], ln_g_d.partition_broadcast(128))
          lnb = sb("lnb", [128, 512]); k.dma(lnb[:], ln_b_d.partition_broadcast(128))
          bstab = sb("bstab", [128, 8]); k.dma(bstab[:], bs_d.rearrange("g t -> t g"), slow=True)

          def front(xt, hT, gidx_unused, junk, ssum, rs, hb, tsl):
              k.act(junk[:, 0:1024], xt, AF.Square, accum=ssum[:])
              k.rsq(rs[:], ssum[:], float(D_MODEL * EPS))
              k.ts("dve", hb[:], xt, rs[:, 0:1], 32.0, ALU.mult, ALU.mult)
              for kk in range(8):
                  k.tr(psb[4][:, kk * 128:(kk + 1) * 128], hb[:, kk * 128:(kk + 1) * 128], ident[:])
              k.cp("act", hT[:, :, tsl], psb[4][:, 0:1024].rearrange("p (k t) -> p k t", k=8))

          def gelu_to(out, src_ps, n, junk_a, junk_b, accum=None, p0=0, p1=128, bias=None):
              xv = junk_a[p0:p1, 0:n]
              sq = junk_b[p0:p1, 0:n]
              if bias is None:
                  k.act(xv, src_ps, AF.Copy)
                  k.act(sq, src_ps, AF.Square)
              else:
                  k.ts("dve", xv, src_ps, bias, None, ALU.add)
                  k.act(sq, src_ps, AF.Square, bias=bias)
              k.ts("dve", sq, sq, 0.044715, 1.0, ALU.mult, ALU.add)
              k.tt("pool", sq, sq, xv, ALU.mult)
              k.act(sq, sq, AF.Exp, scale=-1.5957691216)
              k.ts("dve", sq, sq, 1.0, None, ALU.add)
              k.recip(sq, sq)
              if accum is None:
                  k.tt("dve", out, sq, xv, ALU.mult)
              else:
                  P.add("dve", lambda e: e.scalar_tensor_tensor(out, sq, 1.0, xv, ALU.mult, ALU.mult, accum_out=accum),
                        [out, accum], [sq, xv])

          w_inB = sb("w_inB", [128, 8, 1560], BF)
          w_outb = sb("w_outb", [128, 8, 1024], BF)
          SA = contextlib.ExitStack()
          with SA:
              w_inA = sb("w_inA", [128, 8, 768], BF, SA)
              kvT = sb("kvT", [128, 2, T], BF, SA)
              segsA = [(0, 384, 0), (384, 512, 512), (512, 640, 384), (640, 768, 640)]
              load_w(lambda kk, a, b: w_inA[:, kk, a:b], w_in_d[:, 512:1280], 8, 768, gcols[:, 0, :], segsA)
              wq = []
              load_w(lambda kk, a, b: w_inB[:, kk, a:b], w_in_d[:, 0:512], 8, 512, gcols[:, 0, :], [(0, 512, 0)], defer=wq)
              load_w(lambda kk, a, b: w_inB[:, kk, a:b], w_in_d[:, 1280:2328], 8, 1048, gcols[:, 0, :],
                     [(0, 24, 1536), (24, 1048, 512)], defer=wq)
              load_w(lambda kk, a, b: w_outb[:, kk, a:b], w_out_d, 4, 1024, gcols[:, 3, 0:4], defer=wq)
              load_w(lambda kk, a, b: w_outb[:, 4 + kk, a:b], w_out_d[512:1024, :], 4, 1024, gcols[:, 3, 4:8], defer=wq)
              ck(1, [w_inA[:, 0, 0:512], w_inA[:, 7, 256:768]])
              w1r2 = [sb("w1r%d" % i, [128, 32, 256], BF, SA) for i in range(2)]
              w2b2 = [sb("w2b%d" % i, [128, 2, 64], BF, SA) for i in range(2)]
              posT2 = [sb("posT%d" % i, [128, 32], BF, SA) for i in range(2)]
              b1c2 = [sb("b1c%d" % i, [128, 2], F32, SA) for i in range(2)]
              b2t2 = [sb("b2t%d" % i, [128, 64], F32, SA) for i in range(2)]
              posf2 = [sb("posf%d" % i, [128, 32], F32, SA) for i in range(2)]
              for kv in range(2):
                  w1v = cw1_d[kv].rearrange("(l d) h -> d l h", d=64)
                  for half in range(2):
                      for lc in range(4):
                          s_ = stg[cnt[0] % 2]
                          cnt[0] += 1
                          k.dma(s_[half * 64:(half + 1) * 64, :].rearrange("p (l h) -> p l h", h=256),
                                w1v[:, lc * 8:(lc + 1) * 8, :])
                          k.cp(("act", "dve")[lc % 2],
                               w1r2[kv][half * 64:(half + 1) * 64, lc * 8:(lc + 1) * 8, :],
                               s_[half * 64:(half + 1) * 64, :].rearrange("p (l h) -> p l h", h=256))
                  s_ = stg[cnt[0] % 2]
                  cnt[0] += 1
                  k.dma(s_[:, 0:128].rearrange("p (c o) -> p c o", c=2), cw2_d[kv].rearrange("(c p) o -> p c o", p=128))
                  k.cp("dve", w2b2[kv][:], s_[:, 0:128].rearrange("p (c o) -> p c o", c=2))
                  pos_d = (pos_k_d, pos_v_d)[kv]
                  for half in range(2):
                      k.dma(posf2[kv][half * 64:(half + 1) * 64, :], pos_d.rearrange("l d -> d l"), slow=True)
                  k.cp("dve", posT2[kv][:], posf2[kv][:])
                  k.dma(b1c2[kv][:], cb1_d[kv].rearrange("(c p) -> p c", p=128), slow=True)
                  k.dma(b2t2[kv][:], cb2_d[kv].partition_broadcast(128))
              xtA = [sb("xtA%d" % i, [128, 1024], F32, SA) for i in range(2)]
              hTA = [sb("hTA%d" % i, [128, 8, 128], BF, SA) for i in range(2)]
              junkA = sb("junkA", [128, 1024], F32, SA)
              hbA = sb("hbA", [128, 1024], BF, SA)
              ssA = sb("ssA", [128, 1], F32, SA)
              rsA = sb("rsA", [128, 1], F32, SA)
              ssk = sb("ssk", [128, 4], F32, SA)
              zbA = sb("zbA", [128, 512], BF, SA)
              def front_a(tt_):
                  xt = xtA[tt_ % 2]
                  hT = hTA[tt_ % 2]
                  k.dma(xt[:], x_d[tt_ * 128:(tt_ + 1) * 128, :])
                  front(xt[:], hT, 0, junkA, ssA, rsA, hbA, slice(0, 128))
                  b0, b1 = ps[2 * (tt_ % 2)], ps[2 * (tt_ % 2) + 1]
                  for kk in range(8):
                      k.mm(b0[:, 0:512], hT[:, kk, :], w_inA[:, kk, 0:512], start=(kk == 0), stop=(kk == 7))
                      k.mm(b1[:, 0:256], hT[:, kk, :], w_inA[:, kk, 512:768], start=(kk == 0), stop=(kk == 7))

              def post_a(tt_):
                  b0, b1 = ps[2 * (tt_ % 2)], ps[2 * (tt_ % 2) + 1]
                  k.act(junkA[:, 0:256], b0[:, 256:512], AF.Square)
                  k.red(ssk[:], junkA[:, 0:256].rearrange("p (a d) -> p a d", d=64))
                  k.rsq(ssk[:], ssk[:], 64.0 * EPS)
                  k.cp("act", zbA[:, 0:256], b0[:, 0:256])
                  k.tt("dve", zbA[:, 256:512].rearrange("p (a d) -> p a d", d=64),
                       b0[:, 256:512].rearrange("p (a d) -> p a d", d=64),
                       ssk[:].unsqueeze(2).to_broadcast([128, 4, 64]), ALU.mult)
                  k.cp("act", Vall[:, tt_, :, 0:64], b1[:, 0:256].rearrange("p (a d) -> p a d", d=64))
                  k.tr(psb[5][:, 0:128], zbA[:, 0:128], ident[:])
                  k.tr(psb[5][:, 128:256], zbA[:, 128:256], ident[:])
                  for a in range(4):
                      k.tr(psb[5][0:64, 256 + a * 128:384 + a * 128], zbA[:, 256 + a * 64:320 + a * 64], ident[:])
                  tsl = slice(tt_ * 128, (tt_ + 1) * 128)
                  k.cp("dve", kvT[:, :, tsl], psb[5][:, 0:256].rearrange("p (a t) -> p a t", a=2))
                  for g in range(2):
                      k.ts("dve", KEs[g][0:64, tsl], psb[5][0:64, 256 + g * 128:384 + g * 128], gks8[:, 0:1], None, ALU.mult)
                      k.ts("dve", KEw[g][0:64, tsl], psb[5][0:64, 512 + g * 128:640 + g * 128], gkw8[:, 0:1], None, ALU.mult)

              front_a(0)
              for tt_ in range(NT):
                  if tt_ + 1 < NT:
                      front_a(tt_ + 1)
                  post_a(tt_)
                  nd = (len(wq) + (NT - 1 - tt_)) // (NT - tt_)
                  for _ in range(nd):
                      wq.pop(0)()
              assert not wq

              ck(2, [KEs[0][:, 0:1024], kvT[:, 0, 0:1024], KEw[1][:, 0:1024], Vall[:, 3, :, :].rearrange("p a d -> p (a d)")])
              SC = contextlib.ExitStack()
              with SC:
                  bias1 = sb("bias1", [128, 2], F32, SC)
                  gkc = sb("gkc", [128, 64], F32, SC)
                  hdn = sb("hdn", [128, 2, NCT * 128], BF, SC)
                  ja = stg[0]
                  jb = stg[1]
                  kcf = sb("kcf", [128, 64], F32, SC)
                  kcb = sb("kcb", [128, 64], BF, SC)
                  ssc = sb("ssc", [128, 1], F32, SC)
                  k.dma(gkc[:], kc_g_d.partition_broadcast(128))
                  k.memset("dve", hdn[:], 0.0)
                  for kv in range(2):
                      w1r, w2b, posT, b1c, b2t = w1r2[kv], w2b2[kv], posT2[kv], b1c2[kv], b2t2[kv]
                      for hh in range(2):
                          for l in range(32):
                              k.mm(ps[6][:, hh:hh + 1], w1r[0:64, l, hh * 128:(hh + 1) * 128], posT[0:64, l:l + 1],
                                   start=(l == 0), stop=(l == 31))
                      k.tt("dve", bias1[:], ps[6][:, 0:2], b1c[:], ALU.add)
                      for g in range(2):
                          pr = slice(g * 64, (g + 1) * 64)
                          for hh in range(2):
                              for l in range(32):
                                  k.mm(ps[hh][:, 0:NC], w1r[pr, l, hh * 128:(hh + 1) * 128],
                                       kvT[pr, kv, l:l + 16 * (NC - 1) + 1:16], start=(l == 0), stop=(l == 31))
                          for hh in range(2):
                              for c0 in range(0, NC, 256):
                                  c1 = min(NC, c0 + 256)
                                  gelu_to(hdn[:, hh, c0:c1], ps[hh][:, c0:c1], c1 - c0, ja, jb, bias=bias1[:, hh:hh + 1])
                          for ct in range(NCT):
                              ncv = min(128, NC - ct * 128)
                              for hh in range(2):
                                  k.mm(ps[2][0:ncv, 0:64], hdn[:, hh, ct * 128:ct * 128 + ncv], w2b[:, hh, :],
                                       start=(hh == 0), stop=(hh == 1))
                              k.tt("dve", kcf[0:ncv, :], ps[2][0:ncv, 0:64], b2t[0:ncv, :], ALU.add)
                              if kv == 1:
                                  if ncv < 128:
                                      k.memset("dve", Vc[g][:, ct, :], 0.0)
                                  k.cp("dve", Vc[g][0:ncv, ct, :], kcf[0:ncv, :])
                              else:
                                  k.act(ja[0:ncv, 0:64], kcf[0:ncv, :], AF.Square, accum=ssc[0:ncv, :])
                                  k.rsq(ssc[0:ncv, :], ssc[0:ncv, :], 64.0 * EPS)
                                  k.ts("dve", kcf[0:ncv, :], kcf[0:ncv, :], ssc[0:ncv, 0:1], 8.0, ALU.mult, ALU.mult)
                                  if ncv < 128:
                                      k.memset("dve", kcb[:], 0.0)
                                  k.tt("dve", kcb[0:ncv, :], kcf[0:ncv, :], gkc[0:ncv, :], ALU.mult)
                                  k.tr(psb[5][0:64, 0:128], kcb[:, :], ident[:])
                                  k.cp("dve", KcT[g][:, ct * 128:(ct + 1) * 128], psb[5][0:64, 0:128])

          P.barrier()
          ck(3, [KcT[0][:, 0:128], KcT[1][:, 0:128], Vc[0][:, 0, :], Vc[1][:, 0, :]])
          SB_ = contextlib.ExitStack()
          with SB_:
              WmT = sb("WmT", [128, 8, 128], BF, SB_)
              wsf = sb("wsf", [128, 128], F32, SB_)
              wsb = sb("wsb", [128, 128], BF, SB_)
              xs = sb("xs", [128, 4, 1024], F32, SB_)
              hTB = sb("hTB", [128, 8, 128], BF, SB_)
              junkB = stg[0][:, 1024:2048]
              junkC = stg[1][:, 1024:2048]
              hbB = sb("hbB", [128, 1024], BF, SB_)
              ssB = sb("ssB", [128, 1], F32, SB_)
              rsB = sb("rsB", [128, 1], F32, SB_)
              ssq = sb("ssq", [128, 8], F32, SB_)
              qnb = sb("qnb", [128, 512], BF, SB_)
              R = sb("R", [128, 8, 512], BF, SB_)
              u_sb = sb("u_sb", [128, 512], F32, SB_)
              v_sb = sb("v_sb", [128, 512], F32, SB_)
              vnb = sb("vnb", [128, 512], BF, SB_)
              st1 = sb("st1", [128, 4], F32, SB_)
              gof = v_sb
              gob = sb("gob", [128, 512], BF, SB_)
              goT = sb("goT", [128, 4, 512], BF, SB_)
              aoT = sb("aoT", [128, 4, 128], BF, SB_)
              ao = sb("ao", [128, 4, 512], F32, SB_)
              aob = qnb
              rstd_a = sb("rstd_a", [128, 1], F32, SB_)
              sc = sb("sc", [128, 2, 256], F32, SB_)
              ef = sb("ef", [128, 2, 8, 256], BF, SB_)
              ebT = sb("ebT", [128, 2, NCT, 128], BF, SB_)
              csum = sb("csum", [128, 2, 8], F32, SB_)
              crin = sb("crin", [128, 2, 8], F32, SB_)
              icp = sb("icp", [128, 2, NCP + 1], F32, SB_)
              imp = sb("imp", [128, 2, 64], F32, SB_)
              impw = sb("impw", [128, 2, 64], F32, SB_)
              m8 = sb("m8", [128, 2, 16], F32, SB_)
              bm = sb("bm", [128, 2, 128], BF, SB_)
              mbT = sb("mbT", [128, 2, 512], F32, SB_)
              ocs = sb("ocs", [128, 4, 8, 64], F32, SB_)
              pti = [0]
              PT = [sb("PT%d" % i, [128, 512], BF, SB_) for i in range(3)]
              fsum = sb("fsum", [128, 2, 4], F32, SB_)
              coef = sb("coef", [128, 3, 4], F32, SB_)
              tmpo = u_sb[:, 0:256].rearrange("p (a d) -> p a d", d=64)
              x1t = [stg[0][:, 0:1024], stg[1][:, 0:1024]]
              k.memset("dve", icp[:], 0.0)
              k.memset("dve", bm[:], 0.0)
              k.memset("dve", ef[:], 0.0)
              for g in range(8):
                  k.dma(wsf[:], ws_d[g])
                  k.tr(psb[5][:, 0:128], trile[:], ident[:])
                  k.tt("dve", wsb[:], wsf[:], psb[5][:, 0:128], ALU.mult)
                  k.tr(psb[5][:, 128:256], wsb[:], ident[:])
                  k.cp("dve", WmT[:, g, :], psb[5][:, 128:256])

              for Q in range(NQ):
                  ju_a, ju_b = stg[0][:, 0:512], stg[0][:, 512:1024]
                  jv_a, jv_b = stg[0][:, 1024:1536], stg[0][:, 1536:2048]
                  fjunk, qsq, dummy = stg[1][:, 0:1024], stg[1][:, 1024:1536], stg[1][:, 1536:2048]

                  def front_b(qs):
                      tt_ = Q * 4 + qs
                      xt = xs[:, qs, :]
                      k.dma(xt, x_d[tt_ * 128:(tt_ + 1) * 128, :])
                      front(xt, hTB, 0, fjunk, ssB, rsB, hbB, slice(0, 128))
                      for kk in range(8):
                          for cb in range(3):
                              k.mm(ps[cb][:, 0:512], hTB[:, kk, :], w_inB[:, kk, cb * 512:(cb + 1) * 512],
                                   start=(kk == 0), stop=(kk == 7))
                          k.mm(ps[3][:, 0:24], hTB[:, kk, :], w_inB[:, kk, 1536:1560], start=(kk == 0), stop=(kk == 7))

                  def post1_b(qs):
                      tt_ = Q * 4 + qs
                      k.act(qsq, ps[0][:, 0:512], AF.Square)
                      k.red(ssq[:], qsq.rearrange("p (a d) -> p a d", d=64))
                      k.rsq(ssq[:], ssq[:], 64.0 * EPS)
                      k.tt("dve", qnb[:].rearrange("p (a d) -> p a d", d=64),
                           ps[0][:, 0:512].rearrange("p (a d) -> p a d", d=64),
                           ssq[:].unsqueeze(2).to_broadcast([128, 8, 64]), ALU.mult)
                      k.act(gates[:, tt_, :], ps[3][:, 0:24], AF.Exp, scale=-1.0)
                      k.act(ju_a, ps[1][:, 0:512], AF.Copy)
                      k.act(ju_b, ps[1][:, 0:512], AF.Square)
                      k.act(jv_a, ps[2][:, 0:512], AF.Copy)
                      k.act(jv_b, ps[2][:, 0:512], AF.Square)

                  def post2_b(qs):
                      tt_ = Q * 4 + qs
                      k.ts("dve", gates[:, tt_, :], gates[:, tt_, :], 1.0, None, ALU.add)
                      k.recip(gates[:, tt_, :], gates[:, tt_, :])
                      prs = ((ju_a, ju_b), (jv_a, jv_b))
                      for xv, sq in prs:
                          k.ts("dve", sq, sq, 0.044715, 1.0, ALU.mult, ALU.add)
                      for xv, sq in prs:
                          k.tt("dve", sq, sq, xv, ALU.mult)
                      for xv, sq in prs:
                          k.act(sq, sq, AF.Exp, scale=-1.5957691216)
                      for xv, sq in prs:
                          k.act(sq, sq, AF.Ln, bias=1.0)
                      for xv, sq in prs:
                          k.act(sq, sq, AF.Exp, scale=-1.0)
                      k.tt("pool", u_sb[:], ju_b, ju_a, ALU.mult)
                      P.add("dve", lambda e: e.scalar_tensor_tensor(v_sb[:], jv_b, 1.0, jv_a, ALU.mult, ALU.mult, accum_out=st1[:, 0:1]),
                            [v_sb[:], st1[:, 0:1]], [jv_b, jv_a])
                      k.act(dummy, v_sb[:], AF.Square, accum=st1[:, 1:2])
                      k.ts("dve", st1[:, 2:3], st1[:, 0:1], 1.0 / 512, None, ALU.mult)
                      k.stt("dve", st1[:, 3:4], st1[:, 2:3], -1.0, st1[:, 2:3], ALU.mult, ALU.mult)
                      k.stt("dve", st1[:, 3:4], st1[:, 1:2], 1.0 / 512, st1[:, 3:4], ALU.mult, ALU.add)
                      k.rsq(st1[:, 3:4], st1[:, 3:4], EPS)
                      k.ts("dve", v_sb[:], v_sb[:], st1[:, 2:3], st1[:, 3:4], ALU.subtract, ALU.mult)
                      k.tt("dve", v_sb[:], v_sb[:], lng[:], ALU.mult)
                      k.tt("dve", vnb[:], v_sb[:], lnb[:], ALU.add)
                      for g in range(8):
                          k.mm(ps[6][:, g * 64:(g + 1) * 64], WmT[:, g, :], vnb[:, g * 64:(g + 1) * 64])
                      k.tt("dve", gof[:].rearrange("p (a d) -> p a d", d=64),
                           ps[6][:, 0:512].rearrange("p (a d) -> p a d", d=64),
                           bstab[:].unsqueeze(2).to_broadcast([128, 8, 64]), ALU.add)
                      k.tt("pool", gob[:], gof[:], u_sb[:], ALU.mult)
                      k.act(dummy, gob[:], AF.Square, accum=rstd_g[:, tt_:tt_ + 1])
                      k.rsq(rstd_g[:, tt_:tt_ + 1], rstd_g[:, tt_:tt_ + 1], 512.0 * EPS)
                      k.ts("dve", rstd_g[:, tt_:tt_ + 1], rstd_g[:, tt_:tt_ + 1], 22.627416998, None, ALU.mult)
                      for c4 in range(4):
                          k.tr(psb[7][:, c4 * 128:(c4 + 1) * 128], gob[:, c4 * 128:(c4 + 1) * 128], ident[:])
                      k.cp("act", goT[:, :, qs * 128:(qs + 1) * 128], psb[7][:, 0:512].rearrange("p (c t) -> p c t", c=4))
                      for h in range(8):
                          k.tr(psb[5][0:64, h * 128:(h + 1) * 128], qnb[:, h * 64:(h + 1) * 64], ident[:])
                      k.ts("dve", R[0:64, :, qs * 128:(qs + 1) * 128],
                           psb[5][0:64, 0:1024].rearrange("p (h t) -> p h t", h=8), gq[:, 0:1], None, ALU.mult)

                  front_b(0)
                  for qs in range(4):
                      post1_b(qs)
                      if qs + 1 < 4:
                          front_b(qs + 1)
                      post2_b(qs)

                  ck(4, [R[:, 0, :], R[:, 7, :], gof[:], goT[:, 0, :], gates[:, 0:4, :].rearrange("p a d -> p (a d)")])

                  def make_sel(qs):
                      qt = Q * 4 + qs
                      qp = qs % 2
                      ocb = ps[2 + qp]
                      th = []
                      th.append(lambda: k.ts("dve", csum[:, qp, :], csum[:, qp, :], 1e-30, None, ALU.max))
                      th.append(lambda: k.recip(crin[:, qp, :], csum[:, qp, :]))
                      th.append(lambda: k.tt("dve", ocs[:, qs, :, :], ocb[:, 0:512].rearrange("p (h d) -> p h d", h=8),
                                             crin[:, qp, :].unsqueeze(2).to_broadcast([128, 8, 64]), ALU.mult))
                      span = 4 * (NB - 1) + 1
                      for g in range(2):
                          th.append(lambda g=g: k.ts("dve", icp[:, g, 1:1 + NC], ef[:, qp, 4 * g, 0:NC],
                                                     crin[:, qp, 4 * g:4 * g + 1], None, ALU.mult))
                          for r in range(1, 4):
                              th.append(lambda g=g, h=4 * g + r: k.stt("dve", icp[:, g, 1:1 + NC], ef[:, qp, h, 0:NC],
                                                                       crin[:, qp, h:h + 1], icp[:, g, 1:1 + NC], ALU.mult, ALU.add))
                          th.append(lambda g=g: k.cp("dve", imp[:, g, 0:NB], icp[:, g, 0:span:4]))
                          for kk_, wk in ((1, 2.0), (2, 2.0), (3, 2.0), (4, 1.0)):
                              th.append(lambda g=g, kk_=kk_, wk=wk: k.stt("dve", imp[:, g, 0:NB], icp[:, g, kk_:kk_ + span:4], wk,
                                                                          imp[:, g, 0:NB], ALU.mult, ALU.add))
                      if NB < 64:
                          th.append(lambda: k.memset("dve", imp[:, :, NB:64], -1.0))
                      f0 = 2 * (NT - 1) - 2 * qt
                      for g in range(2):
                          th.append(lambda g=g: k.tt("dve", imp[:, g, 0:NB], imp[:, g, 0:NB], Fw[:, f0:f0 + NB], ALU.max))
                          th.append(lambda g=g: k.tt("dve", imp[:, g, 0:NB], imp[:, g, 0:NB], Uw[:, f0:f0 + NB], ALU.min))
                      th.append(lambda: k.memset("dve", imp[:, :, 0:1], 3.0e4))
                      for g in range(2):
                          th.append(lambda g=g: k.max8(m8[:, g, 0:8], imp[:, g, :]))
                          th.append(lambda g=g: k.mrep(impw[:, g, :], m8[:, g, 0:8], imp[:, g, :], -2.0))
                          th.append(lambda g=g: k.max8(m8[:, g, 8:16], impw[:, g, :]))
                          th.append(lambda g=g: k.ts("dve", impw[:, g, :], imp[:, g, :], m8[:, g, 15:16], None, ALU.is_ge))
                          th.append(lambda g=g: k.ts("dve", bm[:, g, 64:128], impw[:, g, :], -NEG, NEG, ALU.mult, ALU.add))
                          th.append(lambda g=g: k.tr(psb[5][:, g * 128:(g + 1) * 128], bm[:, g, :], ident[:]))
                          th.append(lambda g=g: k.cp("dve", mbT[64:128, g, qs * 128:(qs + 1) * 128],
                                                     psb[5][64:128, g * 128:(g + 1) * 128]))
                      return th

                  def heads(qs, pending):
                      qt = Q * 4 + qs
                      qp = qs % 2
                      nctq = min(NCT, (8 * qt + 7 + 127) // 128)
                      gs0 = 8 * (NT - 1) - 8 * qt
                      ocb = ps[2 + qp]

                      def stage_a(h):
                          g = h // 4
                          sbk = ps[0] if h % 2 == 0 else ps[6]
                          k.mm(sbk[:, 0:NC], R[0:64, h, qs * 128:(qs + 1) * 128], KcT[g][:, 0:NC])
                          k.stt("dve", sc[:, h % 2, 0:NC], Gw[:, gs0:gs0 + NC], SLOPES[h], sbk[:, 0:NC], ALU.mult, ALU.add)
                          k.act(ef[:, qp, h, 0:NC], sc[:, h % 2, 0:NC], AF.Exp, accum=csum[:, qp, h:h + 1])

                      def stage_b(h):
                          g = h // 4
                          tb = psb[1] if h % 2 == 0 else psb[7]
                          for ct in range(nctq):
                              k.tr(tb[:, ct * 128:(ct + 1) * 128], ef[:, qp, h, ct * 128:(ct + 1) * 128], ident[:])
                          k.cp("act", ebT[:, h % 2, 0:nctq, :], tb[:, 0:nctq * 128].rearrange("p (c t) -> p c t", c=nctq))
                          for ct in range(nctq):
                              k.mm(ocb[:, h * 64:(h + 1) * 64], ebT[:, h % 2, ct, :], Vc[g][:, ct, :],
                                   start=(ct == 0), stop=(ct == nctq - 1))

                      stage_a(0)
                      for h in range(8):
                          if h + 1 < 8:
                              stage_a(h + 1)
                          stage_b(h)
                          nd = (len(pending) + (7 - h)) // (8 - h)
                          for _ in range(nd):
                              pending.pop(0)()

                  pending = []
                  for qs in range(4):
                      heads(qs, pending)
                      assert not pending
                      pending = make_sel(qs)
                  for th_ in pending:
                      th_()

                  ck(5, [mbT[:, 0, :], mbT[:, 1, :], ocs[:, 3, :, :].rearrange("p a d -> p (a d)"), imp[:, 0, :]])
                  qcols = slice(Q * 512, (Q + 1) * 512)
                  gsl = slice(4 * Q, 4 * Q + 4)
                  tasks = []
                  for h in range(8):
                      for br in range(2):
                          kt_lo = max(0, 4 * Q - 4) if br == 0 else 0
                          kts = list(range(kt_lo, 4 * Q + 4))
                          for kt in kts:
                              tasks.append(dict(h=h, br=br, kt=kt, gfirst=(kt == kts[0]), glast=(kt == kts[-1]), n=len(tasks)))

                  def obank(h, br):
                      return (ps[6 + br] if h % 2 == 0 else ps[br])

                  def emit_S(t):
                      h, br, kt = t["h"], t["br"], t["kt"]
                      g = h // 4
                      if t["gfirst"]:
                          if br == 0:
                              k.ts("dve", R[64:128, h, :], Dt[64:128, qcols], SLOPES[h], None, ALU.mult)
                          else:
                              k.tt("dve", R[64:128, h, :], R[64:128, h, :], mbT[64:128, g, :], ALU.add)
                      i = kt - 4 * Q
                      c_lo = max(0, i)
                      c_hi = min(3, i + 4) if br == 0 else 3
                      t["c"] = (c_lo, c_hi)
                      n0, n1 = c_lo * 128, (c_hi + 1) * 128
                      KE = (KEw, KEs)[br][g]
                      sbank = (ps[3], ps[4], ps[5])[t["n"] % 3]
                      k.mm(sbank[:, n0:n1], KE[:, kt * 128:(kt + 1) * 128], R[:, h, n0:n1])

                  def emit_rest(t):
                      h, br, kt = t["h"], t["br"], t["kt"]
                      g = h // 4
                      i = kt - 4 * Q
                      c_lo, c_hi = t["c"]
                      n0, n1 = c_lo * 128, (c_hi + 1) * 128
                      sbank = (ps[3], ps[4], ps[5])[t["n"] % 3]
                      pt = PT[t["n"] % len(PT)]
                      Ob = obank(h, br)
                      vidx = (2 + g, g)[br]
                      k.act(pt[:, n0:n1], sbank[:, n0:n1], AF.Exp, bias=wb[:, h:h + 1])
                      if i >= 0:
                          k.tt("dve", pt[:, i * 128:(i + 1) * 128], pt[:, i * 128:(i + 1) * 128], trile[:], ALU.mult)
                      if br == 0 and 0 <= i + 4 <= 3:
                          cc = i + 4
                          k.tt("dve", pt[:, cc * 128:(cc + 1) * 128], pt[:, cc * 128:(cc + 1) * 128], trigt[:], ALU.mult)
                      for c in range(c_lo, c_hi + 1):
                          k.mm(Ob[:, c * 65:(c + 1) * 65], pt[:, c * 128:(c + 1) * 128], Vall[:, kt, vidx, 0:65],
                               start=(t["gfirst"] and c == c_lo), stop=(kt == 4 * Q + c), skip=True)
                      if br == 1 and t["glast"]:
                          Ow = obank(h, 0)[:, 0:260].rearrange("p (c d) -> p c d", d=65)
                          Os = obank(h, 1)[:, 0:260].rearrange("p (c d) -> p c d", d=65)
                          k.ts("dve", fsum[:, 0, :], Ow[:, :, 64], 1e-30, None, ALU.max)
                          k.ts("dve", fsum[:, 1, :], Os[:, :, 64], 1e-30, None, ALU.max)
                          k.recip(fsum[:], fsum[:])
                          k.tt("dve", coef[:, 0, :], fsum[:, 0, :], gates[:, gsl, 3 * h + 2], ALU.mult)
                          k.tt("dve", coef[:, 1, :], fsum[:, 1, :], gates[:, gsl, 3 * h + 1], ALU.mult)
                          dst = ao[:, :, h * 64:(h + 1) * 64]
                          k.tt("dve", dst, ocs[:, :, h, :], gates[:, gsl, 3 * h:3 * h + 1].to_broadcast([128, 4, 64]), ALU.mult)
                          k.tt("dve", tmpo, Ow[:, :, 0:64], coef[:, 0, :].unsqueeze(2).to_broadcast([128, 4, 64]), ALU.mult)
                          k.tt("dve", dst, dst, tmpo, ALU.add)
                          k.tt("dve", tmpo, Os[:, :, 0:64], coef[:, 1, :].unsqueeze(2).to_broadcast([128, 4, 64]), ALU.mult)
                          k.tt("dve", dst, dst, tmpo, ALU.add)

                  emit_S(tasks[0])
                  if len(tasks) > 1:
                      emit_S(tasks[1])
                  for n_, t in enumerate(tasks):
                      if n_ + 2 < len(tasks):
                          emit_S(tasks[n_ + 2])
                      emit_rest(t)

                  ck(6, [ao[:, 0, :], ao[:, 3, :]])
                  for qs in range(4):
                      tt_ = Q * 4 + qs
                      x1 = x1t[tt_ % 2]
                      k.act(junkB[:, 0:512], ao[:, qs, :], AF.Square, accum=rstd_a[:])
                      k.rsq(rstd_a[:], rstd_a[:], 512.0 * EPS)
                      k.ts("dve", rstd_a[:], rstd_a[:], 22.627416998, None, ALU.mult)
                      k.cp("pool", aob[:], ao[:, qs, :])
                      for c4 in range(4):
                          k.tr(psb[5][:, c4 * 128:(c4 + 1) * 128], aob[:, c4 * 128:(c4 + 1) * 128], ident[:])
                      k.cp("act", aoT[:], psb[5][:, 0:512].rearrange("p (c t) -> p c t", c=4))
                      for half in range(2):
                          hs = slice(half * 512, (half + 1) * 512)
                          for c4 in range(4):
                              k.mm(ps[half][:, :], aoT[:, c4, :], w_outb[:, c4, hs], start=(c4 == 0), stop=(c4 == 3))
                          for c4 in range(4):
                              k.mm(ps[2 + half][:, :], goT[:, c4, qs * 128:(qs + 1) * 128], w_outb[:, 4 + c4, hs],
                                   start=(c4 == 0), stop=(c4 == 3))
                          k.stt("dve", x1[:, hs], ps[half][:, :], rstd_a[:, 0:1], xs[:, qs, hs], ALU.mult, ALU.add)
                          k.stt("dve", x1[:, hs], ps[2 + half][:, :], rstd_g[:, tt_:tt_ + 1], x1[:, hs], ALU.mult, ALU.add)
                      k.dma(x1_d[tt_ * 128:(tt_ + 1) * 128, :], x1)
          SATT.close()
          cur[0] = ES
          P.barrier()

          SC3 = contextlib.ExitStack()
          with SC3:
              w2f = sb("w2f", [128, 32, 1024], BF, SC3)
              load_w(lambda kk, a, b: w2f[:, kk, a:b], w_ff2_d, 32, D_MODEL)
              w1b = sb("w1b", [128, 8, 4096], BF, SC3)
              load_w(lambda kk, a, b: w1b[:, kk, a:b], w_ff1_d, 8, D_FF, gcols[:, 1, :])
              wpg = sb("wpg", [128, 8, 1024], BF, SC3)
              load_w(lambda kk, a, b: wpg[:, kk, a:b], w_pg_d, 8, D_MODEL, gcols[:, 2, :])
              wpl = sb("wpl", [128, 2, 1024], BF, SC3)
              load_w(lambda kk, a, b: wpl[:, kk, a:b], w_ple_d, 2, D_MODEL)
              xc = sb("xc", [128, 2, 1024], F32, SC3)
              x2 = sb("x2", [128, 2, 1024], F32, SC3)
              hTC = sb("hTC", [128, 8, 256], BF, SC3)
              h3T = sb("h3T", [128, 8, 128], BF, SC3)
              fT = sb("fT", [128, 32, 256], BF, SC3)
              junkD = stg[0][:, 0:1024]
              hbC = sb("hbC", [128, 1024], BF, SC3)
              ssC = sb("ssC", [128, 1], F32, SC3)
              rsC = sb("rsC", [128, 1], F32, SC3)
              rls = [stg[1][:, 1024:1280], stg[1][:, 1536:1792]]
              ptf = stg[1][:, 1280:1536]
              ptb = sb("ptb", [128, 256], BF, SC3)
              pT = sb("pT", [128, 2, 128], BF, SC3)
              th = stg[0][:, 1024:1536]
              outt = stg[1][:, 0:1024]
              xcs = [xc, x2]
              th2 = [stg[0][:, 1024:1536], stg[0][:, 1536:2048]]
              NBT = T // 256

              def s1(b):
                  for j in range(2):
                      tt_ = 2 * b + j
                      k.dma(xcs[b % 2][:, j, :], x1_d[tt_ * 128:(tt_ + 1) * 128, :])
                      front(xcs[b % 2][:, j, :], hTC, 0, junkD, ssC, rsC, hbC, slice(j * 128, (j + 1) * 128))

              def drain(pend, slots_left):
                  nd = (len(pend) + slots_left - 1) // max(1, slots_left)
                  for _ in range(min(nd, len(pend))):
                      pend.pop(0)()

              def s2_s3(b, pend):
                  x2c = xcs[b % 2]
                  for fc in range(32):
                      bank = ps[fc % 2]
                      for kk in range(8):
                          k.mm(bank[:, 0:256], w1b[:, kk, fc * 128:(fc + 1) * 128], hTC[:, kk, :], start=(kk == 0), stop=(kk == 7))
                      rl = rls[fc % 2]
                      k.act(rl, bank[:, 0:256], AF.Relu)
                      k.tt(("pool", "dve")[fc % 2], fT[:, fc, :], rl, rl, ALU.mult)
                      drain(pend, 36 - fc)
                  for gi in range(4):
                      j, half = gi // 2, gi % 2
                      hs = slice(half * 512, (half + 1) * 512)
                      bank = ps[2 + gi % 2]
                      for fc in range(32):
                          k.mm(bank[:, :], fT[:, fc, j * 128:(j + 1) * 128], w2f[:, fc, hs], start=(fc == 0), stop=(fc == 31))
                      k.tt("dve", x2c[:, j, hs], bank[:, :], x2c[:, j, hs], ALU.add)
                      drain(pend, 4 - gi)

              def ple_thunks(b, j):
                  tt_ = 2 * b + j
                  xin = xcs[b % 2][:, j, :]
                  tl = []

                  def f_a():
                      k.act(junkD[:, 0:1024], xin, AF.Square, accum=ssC[:])
                      k.rsq(rsC[:], ssC[:], float(D_MODEL * EPS))
                      k.ts("dve", hbC[:], xin, rsC[:, 0:1], 32.0, ALU.mult, ALU.mult)
                      for kk in range(8):
                          k.tr(psb[4][:, kk * 128:(kk + 1) * 128], hbC[:, kk * 128:(kk + 1) * 128], ident[:])

                  def f_p():
                      k.dma(ptf, p_d[tt_ * 128:(tt_ + 1) * 128, :])
                      k.cp("pool", ptb[:], ptf)

                  def f_ptr():
                      for c2 in range(2):
                          k.tr(psb[5][:, c2 * 128:(c2 + 1) * 128], ptb[:, c2 * 128:(c2 + 1) * 128], ident[:])

                  tl.append(f_a)
                  tl.append(f_p)
                  tl.append(lambda: k.cp("act", h3T[:, :, 0:128], psb[4][:, 0:1024].rearrange("p (k t) -> p k t", k=8)))
                  tl.append(f_ptr)
                  tl.append(lambda: k.cp("act", pT[:], psb[5][:, 0:256].rearrange("p (c t) -> p c t", c=2)))
                  for half in range(2):
                      hs = slice(half * 512, (half + 1) * 512)
                      thh = th2[half]

                      def f_g(hs=hs):
                          for kk in range(8):
                              k.mm(ps[6][:, :], h3T[:, kk, :], wpg[:, kk, hs], start=(kk == 0), stop=(kk == 7))

                      def f_w(hs=hs):
                          for c2 in range(2):
                              k.mm(ps[7][:, :], pT[:, c2, :], wpl[:, c2, hs], start=(c2 == 0), stop=(c2 == 1))

                      tl.append(f_g)
                      tl.append(f_w)
                      tl.append(lambda thh=thh: k.act(thh, ps[6][:, :], AF.Exp, scale=-1.0))
                      tl.append(lambda thh=thh: k.act(thh, thh, AF.Ln, bias=1.0))
                      tl.append(lambda thh=thh: k.act(thh, thh, AF.Exp, scale=-1.0))
                      tl.append(lambda thh=thh: k.tt("dve", thh, thh, ps[7][:, :], ALU.mult))
                      tl.append(lambda thh=thh, hs=hs: k.tt("pool", outt[:, hs], thh, xin[:, hs], ALU.add))
                  tl.append(lambda: k.dma(out_d[tt_ * 128:(tt_ + 1) * 128, :], outt))
                  return tl

              pend = []
              s1(0)
              for b in range(NBT):
                  s2_s3(b, pend)
                  assert not pend
                  if b + 1 < NBT:
                      s1(b + 1)
                  pend = ple_thunks(b, 0) + ple_thunks(b, 1)
              for t_ in pend:
                  t_()
    except _Stop:
        pass
    P.emit(nc)
    return nc


_NC_CACHE = {}


def _core_inputs(inp, b, consts):
    sq = lambda a: np.ascontiguousarray(np.asarray(a)[0], dtype=np.float32)
    m = {
        "x": np.ascontiguousarray(np.asarray(inp["x"])[b], dtype=np.float32),
        "p": np.ascontiguousarray(np.asarray(inp["p"])[0, b], dtype=np.float32),
    }
    for name in ("g_mix", "w_in", "q_norm_g", "kc_norm_g", "ks_norm_g", "kw_norm_g", "cmp_pos_k", "cmp_pos_v",
                 "cmp_k_w1", "cmp_k_b1", "cmp_k_w2", "cmp_k_b2", "cmp_v_w1", "cmp_v_b1", "cmp_v_w2", "cmp_v_b2",
                 "gmlp_ln_g", "gmlp_ln_b", "gmlp_ws", "gmlp_bs", "out_g_nsa", "out_g_gmlp", "w_out",
                 "g_ff", "w_ff1", "w_ff2", "g_ple", "w_ple_gate", "w_ple"):
        m[name] = sq(inp[name])
    m.update(consts)
    return m


def kernel(_stop=None, **inputs):
    x = np.asarray(inputs["x"])
    B, T = x.shape[0], x.shape[1]
    if T not in _NC_CACHE:
        _NC_CACHE[T] = build_nc(T, _stop)
    nc = _NC_CACHE[T]
    consts = make_consts(T)
    in_maps = [_core_inputs(inputs, b, consts) for b in range(B)]
    res = run_bass_kernel_spmd(nc, in_maps, core_ids=list(range(B)))
    return np.stack([np.asarray(r["out"], dtype=np.float32) for r in res.results], axis=0)
```

```python
import contextlib
import math
import numpy as np
import ml_dtypes
import concourse.bass as bass
import concourse.mybir as mybir
from concourse.bass_utils import run_bass_kernel_spmd

F32 = mybir.dt.float32
BF = mybir.dt.bfloat16
ALU = mybir.AluOpType
AF = mybir.ActivationFunctionType
AX = mybir.AxisListType
DSZ = {F32: 4, BF: 2}

D_MODEL = 1024
IN_COLS = 2328
D_FF = 4096
D_PLE = 256
EPS = 1e-6
NDS = 24
SLOPES = [2.0 ** (-(h + 1)) for h in range(8)]
NEG = -30000.0


class Prog:
    ENGS = ("pe", "act", "dve", "pool", "sp")

    def __init__(self):
        self.ops = {e: [] for e in self.ENGS}
        self.all = []
        self.track = {}
        self.seen = {e: {} for e in self.ENGS}
        self.seen_dma = {e: set() for e in self.ENGS}
        self.ndma = 0
        self.dram = set()
        self.psum = set()
        self.pending = {}

    def barrier(self):
        lasts = {E: self.ops[E][-1]["idx"] for E in self.ENGS if self.ops[E]}
        dmas = {}
        for op in self.all:
            if op["dma"]:
                dmas[op["dsem"]] = op["idx"]
        for X in self.ENGS:
            lst = self.pending.setdefault(X, [])
            for E, d in lasts.items():
                if E != X:
                    lst.append(d)
            lst.extend(dmas.values())

    def box(self, ap):
        name = ap.name
        a = ap.ap
        off = int(ap.offset)
        esz = DSZ.get(ap.dtype, 4)
        if name in self.dram:
            ext = 1
            for st, cnt in a:
                ext += (cnt - 1) * abs(st)
            return name, (0, 1, off * esz, (off + ext) * esz)
        if name in self.psum:
            return name, (0, 128, 0, 2048)
        pstride = a[0][0]
        if pstride == 0:
            p0, f0 = 0, off
        else:
            p0, f0 = off // pstride, off % pstride
        ext = 1
        for st, cnt in a[1:]:
            ext += (cnt - 1) * abs(st)
        return name, (p0, p0 + a[0][1], f0 * esz, (f0 + ext) * esz)

    @staticmethod
    def _ov(a, b):
        return a[0] < b[1] and b[0] < a[1] and a[2] < b[3] and b[2] < a[3]

    @staticmethod
    def _inside(a, b):
        return a[0] >= b[0] and a[1] <= b[1] and a[2] >= b[2] and a[3] <= b[3]

    def add(self, eng, fn, outs, ins, dma=False):
        idx = len(self.all)
        op = dict(eng=eng, fn=fn, waits=[], sig=False, dma=dma, seq=len(self.ops[eng]) + 1, idx=idx)
        deps = {}
        for d in self.pending.pop(eng, []):
            deps[d] = True
        inb = [self.box(a) for a in ins]
        outb = [self.box(a) for a in outs]
        for name, b in inb:
            isps = name in self.psum
            for key, ent in self.track.get(name, {}).items():
                if not self._ov(key[0], b):
                    continue
                if key[2] == "w":
                    deps[ent] = True
                elif isps and key[1] != eng:
                    deps.setdefault(ent, False)
        for name, b in outb:
            tr = self.track.setdefault(name, {})
            for key in list(tr.keys()):
                if self._ov(key[0], b):
                    deps.setdefault(tr[key], False)
                    if self._inside(key[0], b):
                        del tr[key]
        for name, b in inb:
            self.track.setdefault(name, {})[(b, eng, "r")] = idx
        for name, b in outb:
            self.track.setdefault(name, {})[(b, eng, "w")] = idx
        for d in sorted(deps):
            raw = deps[d]
            dop = self.all[d]
            if dop["dma"]:
                if d in self.seen_dma[eng]:
                    continue
                self.seen_dma[eng].add(d)
                op["waits"].append(("d", d))
            else:
                E = dop["eng"]
                if E == eng and not dma:
                    if eng == "pe":
                        continue
                if self.seen[eng].get(E, 0) >= dop["seq"]:
                    continue
                self.seen[eng][E] = dop["seq"]
                dop["sig"] = True
                op["waits"].append(("c", d))
        if dma:
            j = self.ndma
            self.ndma += 1
            op["dsem"] = j % NDS
            op["dval"] = 16 * (j // NDS + 1)
            op["dprev"] = 16 * (j // NDS)
        self.all.append(op)
        self.ops[eng].append(op)
        return op

    def emit(self, nc):
        with contextlib.ExitStack() as st:
            esem = {e: st.enter_context(nc.semaphore("s_" + e)) for e in self.ENGS}
            dsem = [st.enter_context(nc.semaphore("d%d" % i)) for i in range(NDS)]
            for e in self.ENGS:
                c = 0
                for op in self.ops[e]:
                    if op["sig"] and not op["dma"]:
                        c += 1
                        op["cnt"] = c
            dfinal = [0] * NDS
            for op in self.all:
                if op["dma"]:
                    dfinal[op["dsem"]] = max(dfinal[op["dsem"]], op["dval"])
            block = st.enter_context(nc.Block())

            def run(e, eng):
                for op in self.ops[e]:
                    if op["dma"] and op["dprev"] > 0:
                        eng.wait_ge(dsem[op["dsem"]], op["dprev"])
                    for kind, d in op["waits"]:
                        dop = self.all[d]
                        if kind == "d":
                            eng.wait_ge(dsem[dop["dsem"]], dop["dval"])
                        else:
                            eng.wait_ge(esem[dop["eng"]], dop["cnt"])
                    ins = op["fn"](eng)
                    if op["dma"]:
                        ins.then_inc(dsem[op["dsem"]], 16)
                    elif op["sig"]:
                        ins.then_inc(esem[e], 1)
                if e == "sp":
                    for i in range(NDS):
                        if dfinal[i] > 0:
                            eng.wait_ge(dsem[i], dfinal[i])

            block.tensor(lambda eng: run("pe", eng))
            block.scalar(lambda eng: run("act", eng))
            block.vector(lambda eng: run("dve", eng))
            block.gpsimd(lambda eng: run("pool", eng))
            block.sync(lambda eng: run("sp", eng))


def _aps(*xs):
    return [x for x in xs if x is not None and not isinstance(x, (int, float))]


class K:
    def __init__(self, P):
        self.P = P
        self.consts = {}

    def eps_ap(self, val, like):
        t = self.consts[round(float(val), 12)]
        p0 = like.base_partition()
        return t[p0:p0 + like.partition_size(), 0:1]

    def mm(self, out, lhsT, rhs, start=True, stop=True, skip=False):
        if skip:
            self.P.add("pe", lambda e: e.matmul(out, lhsT, rhs, start=start, stop=stop, skip_group_check=True), [out], [lhsT, rhs])
        else:
            self.P.add("pe", lambda e: e.matmul(out, lhsT, rhs, start=start, stop=stop), [out], [lhsT, rhs])

    def tr(self, out, in_, ident):
        self.P.add("pe", lambda e: e.transpose(out, in_, ident), [out], [in_, ident])

    def act(self, out, in_, func, bias=None, scale=None, accum=None):
        kw = {}
        if bias is not None:
            kw["bias"] = bias
        if scale is not None:
            kw["scale"] = scale
        if accum is not None:
            kw["accum_out"] = accum
        self.P.add("act", lambda e: e.activation(out, in_, func, **kw), _aps(out, accum), _aps(in_, bias, scale))

    def ts(self, eng, out, in0, s1, s2, op0, op1=None):
        if op1 is None:
            self.P.add(eng, lambda e: e.tensor_scalar(out, in0, s1, None, op0), [out], _aps(in0, s1))
        else:
            self.P.add(eng, lambda e: e.tensor_scalar(out, in0, s1, s2, op0, op1), [out], _aps(in0, s1, s2))

    def tt(self, eng, out, a, b, op):
        self.P.add(eng, lambda e: e.tensor_tensor(out, a, b, op), [out], [a, b])

    def stt(self, eng, out, in0, scalar, in1, op0, op1):
        self.P.add(eng, lambda e: e.scalar_tensor_tensor(out, in0, scalar, in1, op0, op1), [out], _aps(in0, scalar, in1))

    def rsq(self, out, in_, eps, mul=1.0):
        self.act(out, in_, AF.Ln, bias=float(eps))
        self.act(out, out, AF.Exp, scale=-0.5)

    def cp(self, eng, out, in_):
        if eng == "act":
            self.P.add("act", lambda e: e.copy(out, in_), [out], [in_])
        else:
            self.P.add(eng, lambda e: e.tensor_copy(out, in_), [out], [in_])

    def memset(self, eng, out, val):
        self.P.add(eng, lambda e: e.memset(out, val), [out], [])

    def red(self, out, in_, op=ALU.add):
        self.P.add("dve", lambda e: e.tensor_reduce(out, in_, AX.X, op), [out], [in_])

    def recip(self, out, in_):
        self.P.add("dve", lambda e: e.reciprocal(out, in_), [out], [in_])

    def max8(self, out, in_):
        self.P.add("dve", lambda e: e.max(out, in_), [out], [in_])

    def mrep(self, out, rep, vals, imm):
        self.P.add("dve", lambda e: e.match_replace(out, rep, vals, imm), [out], [rep, vals])

    def dma(self, out, in_, slow=False, eng="sp"):
        if slow:
            self.P.add(eng, lambda e: e.dma_start(out=out, in_=in_, allow_slow_non_contiguous=True), [out], [in_], dma=True)
        else:
            self.P.add(eng, lambda e: e.dma_start(out=out, in_=in_), [out], [in_], dma=True)


def make_consts(T):
    NT = T // 128
    NB = T // 64
    NC = T // 16 - 1
    bf = ml_dtypes.bfloat16
    c = {}
    c["c_ident"] = np.eye(128, dtype=np.float32).astype(bf)
    key = np.arange(T)
    E = (key[None, :] // 64 == np.arange(64)[:, None]).astype(np.float32)
    c["c_E"] = E.astype(bf)
    D = 64.0 * (np.arange(64)[:, None] - (key[None, :] // 64))
    c["c_D"] = D.astype(np.float32).astype(bf)
    p = np.arange(128)[:, None]
    f = np.arange(128)[None, :]
    c["c_trile"] = (p <= f).astype(np.float32).astype(bf)
    c["c_trigt"] = (p > f).astype(np.float32).astype(bf)
    W = NC + 8 * (NT - 1)
    m = np.arange(W)[None, :] - 8 * (NT - 1)
    G = np.where(16 * m + 31 <= p, -(p - 16.0 * m - 15.5), -1.0e6)
    c["c_G"] = G.astype(np.float32)
    W2 = NB + 2 * (NT - 1)
    jp = np.arange(W2)[None, :] - 2 * (NT - 1)
    cur = (p >= 64).astype(np.int64)
    Fw = np.where(jp == cur, 2.0e4, np.where(jp == cur - 1, 1.0e4, 0.0))
    Uw = np.where(jp <= cur, 1.0e9, -1.0)
    c["c_Fw"] = Fw.astype(np.float32)
    c["c_Uw"] = Uw.astype(np.float32)
    wb = np.zeros((128, 8), np.float32)
    for h in range(8):
        wb[:, h] = SLOPES[h] * (np.arange(128) % 64)
    c["c_wb"] = wb
    return c


class _Stop(Exception):
    pass


def build_nc(T, stop=None):
    NT = T // 128
    NQ = T // 512
    NB = T // 64
    NC = T // 16 - 1
    NCP = T // 16
    NCT = (NCP + 127) // 128
    WG = NC + 8 * (NT - 1)
    W2 = NB + 2 * (NT - 1)
    nc = bass.Bass("TRN2", target_bir_lowering=False)
    P = Prog()
    k = K(P)

    def din(name, shape, dt=F32):
        P.dram.add(name)
        return nc.dram_tensor(name, list(shape), dt, kind="ExternalInput").ap()

    x_d = din("x", [T, D_MODEL])
    p_d = din("p", [T, D_PLE])
    g_mix_d = din("g_mix", [D_MODEL])
    w_in_d = din("w_in", [D_MODEL, IN_COLS])
    q_g_d = din("q_norm_g", [64])
    kc_g_d = din("kc_norm_g", [64])
    ks_g_d = din("ks_norm_g", [64])
    kw_g_d = din("kw_norm_g", [64])
    pos_k_d = din("cmp_pos_k", [32, 64])
    pos_v_d = din("cmp_pos_v", [32, 64])
    cw1_d = [din("cmp_k_w1", [2048, 256]), din("cmp_v_w1", [2048, 256])]
    cb1_d = [din("cmp_k_b1", [256]), din("cmp_v_b1", [256])]
    cw2_d = [din("cmp_k_w2", [256, 64]), din("cmp_v_w2", [256, 64])]
    cb2_d = [din("cmp_k_b2", [64]), din("cmp_v_b2", [64])]
    ln_g_d = din("gmlp_ln_g", [512])
    ln_b_d = din("gmlp_ln_b", [512])
    ws_d = din("gmlp_ws", [8, 128, 128])
    bs_d = din("gmlp_bs", [8, 128])
    og_nsa_d = din("out_g_nsa", [512])
    og_gmlp_d = din("out_g_gmlp", [512])
    w_out_d = din("w_out", [D_MODEL, D_MODEL])
    g_ff_d = din("g_ff", [D_MODEL])
    w_ff1_d = din("w_ff1", [D_MODEL, D_FF])
    w_ff2_d = din("w_ff2", [D_FF, D_MODEL])
    g_ple_d = din("g_ple", [D_MODEL])
    w_pg_d = din("w_ple_gate", [D_MODEL, D_MODEL])
    w_ple_d = din("w_ple", [D_PLE, D_MODEL])
    c_ident_d = din("c_ident", [128, 128], BF)
    c_E_d = din("c_E", [64, T], BF)
    c_D_d = din("c_D", [64, T], BF)
    c_trile_d = din("c_trile", [128, 128], BF)
    c_trigt_d = din("c_trigt", [128, 128], BF)
    c_G_d = din("c_G", [128, WG])
    c_Fw_d = din("c_Fw", [128, W2])
    c_Uw_d = din("c_Uw", [128, W2])
    c_wb_d = din("c_wb", [128, 8])
    P.dram.add("x1s")
    x1_d = nc.dram_tensor("x1s", [T, D_MODEL], F32, kind="Internal").ap()
    P.dram.add("out")
    out_d = nc.dram_tensor("out", [T, D_MODEL], F32, kind="ExternalOutput").ap()

    ES = contextlib.ExitStack()

    def ck(stage, aps):
        if stop != stage:
            return
        for i, a in enumerate(aps):
            n = a.shape[-1] if len(a.shape) == 2 else None
            d = stg[i % 2]
            k.cp("dve", d[0:a.shape[0], 0:n], a)
            k.dma(out_d[i * 128:i * 128 + a.shape[0], 0:n], d[0:a.shape[0], 0:n])
        raise _Stop()

    cur = [ES]

    def sb(name, shape, dt=F32, st=None):
        return (st or cur[0]).enter_context(nc.sbuf_tensor(name, list(shape), dt))

    def col(d_ap, n):
        return d_ap.rearrange("(p o) -> p o", o=1)

    try:
      with ES:
          ps = [ES.enter_context(nc.psum_tensor("ps%d" % i, [128, 512], F32)) for i in range(8)]
          for i in range(8):
              P.psum.add("ps%d" % i)
          psb = [t[:].bitcast(BF) for t in ps]

          ident = sb("ident", [128, 128], BF)
          k.dma(ident[:], c_ident_d)
          trile = sb("trile", [128, 128], BF)
          k.dma(trile[:], c_trile_d)
          trigt = sb("trigt", [128, 128], BF)
          k.dma(trigt[:], c_trigt_d)
          wb = sb("wb", [128, 8])
          k.dma(wb[:], c_wb_d)
          gcols = sb("gcols", [128, 4, 8])
          k.dma(gcols[:, 0, :], g_mix_d.rearrange("(k p) -> p k", p=128), slow=True)
          k.dma(gcols[:, 1, :], g_ff_d.rearrange("(k p) -> p k", p=128), slow=True)
          k.dma(gcols[:, 2, :], g_ple_d.rearrange("(k p) -> p k", p=128), slow=True)
          k.dma(gcols[:, 3, 0:4], og_nsa_d.rearrange("(k p) -> p k", p=128), slow=True)
          k.dma(gcols[:, 3, 4:8], og_gmlp_d.rearrange("(k p) -> p k", p=128), slow=True)
          eps_c = sb("eps_c", [128, 1]); k.memset("dve", eps_c[:], EPS)
          stg = [sb("stg%d" % i, [128, 2048]) for i in range(2)]
          cnt = [0]

          def load_w(dst_fn, src, nk, ncols, gcol=None, segs=None, stages=None):
              if segs is None:
                  segs = [(0, ncols, 0)]
              if stages is None:
                  stages = stg
              for kk in range(nk):
                  for c0 in range(0, ncols, 2048):
                      c1 = min(ncols, c0 + 2048)
                      s = stages[cnt[0] % len(stages)]
                      cnt[0] += 1
                      k.dma(s[:, 0:c1 - c0], src[kk * 128:(kk + 1) * 128, c0:c1])
                      for (a0, a1, d0) in segs:
                          lo, hi = max(a0, c0), min(a1, c1)
                          if lo >= hi:
                              continue
                          dst = dst_fn(kk, d0 + lo - a0, d0 + hi - a0)
                          if gcol is None:
                              k.cp(("act", "dve")[cnt[0] % 2], dst, s[:, lo - c0:hi - c0])
                          else:
                              k.act(dst, s[:, lo - c0:hi - c0], AF.Copy, scale=gcol[:, kk:kk + 1])

          SATT = contextlib.ExitStack()
          cur[0] = SATT
          Gw = sb("Gw", [128, WG])
          k.dma(Gw[:], c_G_d)
          Fw = sb("Fw", [128, W2])
          k.dma(Fw[:], c_Fw_d)
          Uw = sb("Uw", [128, W2])
          k.dma(Uw[:], c_Uw_d)
          Dt = sb("Dt", [128, T], BF)
          k.dma(Dt[64:128, :], c_D_d)
          KEs = [sb("KEs%d" % g, [128, T], BF) for g in range(2)]
          KEw = [sb("KEw%d" % g, [128, T], BF) for g in range(2)]
          for t_ in KEs + KEw:
              k.dma(t_[64:128, :], c_E_d)
          Vall = sb("Vall", [128, NT, 4, 66], BF)
          k.memset("dve", Vall[:], 1.0)
          KcT = [sb("KcT%d" % g, [64, NCT * 128], BF) for g in range(2)]
          Vc = [sb("Vc%d" % g, [128, NCT, 64], BF) for g in range(2)]
          gates = sb("gates", [128, NT, 24])
          rstd_g = sb("rstd_g", [128, NT])
          gq = sb("gq", [64, 1]); k.dma(gq[:], col(q_g_d, 64))
          gks8 = sb("gks8", [64, 1]); k.dma(gks8[:], col(ks_g_d, 64))
          gkw8 = sb("gkw8", [64, 1]); k.dma(gkw8[:], col(kw_g_d, 64))
          k.ts("dve", gks8[:], gks8[:], 8.0, None, ALU.mult)
          k.ts("dve", gkw8[:], gkw8[:], 8.0, None, ALU.mult)
          lng = sb("lng", [128, 512]); k.dma(lng[:], ln_g_d.partition_broadcast(128))
          lnb = sb("lnb", [128, 512]); k.dma(lnb[:], ln_b_d.partition_broadcast(128))
          bstab = sb("bstab", [128, 8]); k.dma(bstab[:], bs_d.rearrange("g t -> t g"), slow=True)

          def front(xt, hT, gidx_unused, junk, ssum, rs, hb, tsl):
              k.act(junk[:, 0:1024], xt, AF.Square, accum=ssum[:])
              k.rsq(rs[:], ssum[:], float(D_MODEL * EPS))
              k.ts("dve", hb[:], xt, rs[:, 0:1], 32.0, ALU.mult, ALU.mult)
              for kk in range(8):
                  k.tr(psb[4][:, kk * 128:(kk + 1) * 128], hb[:, kk * 128:(kk + 1) * 128], ident[:])
              k.cp("act", hT[:, :, tsl], psb[4][:, 0:1024].rearrange("p (k t) -> p k t", k=8))

          def gelu_to(out, src_ps, n, junk_a, junk_b, accum=None, p0=0, p1=128, bias=None):
              xv = junk_a[p0:p1, 0:n]
              sq = junk_b[p0:p1, 0:n]
              if bias is None:
                  k.act(xv, src_ps, AF.Copy)
                  k.act(sq, src_ps, AF.Square)
              else:
                  k.ts("dve", xv, src_ps, bias, None, ALU.add)
                  k.act(sq, src_ps, AF.Square, bias=bias)
              k.ts("dve", sq, sq, 0.044715, 1.0, ALU.mult, ALU.add)
              k.tt("pool", sq, sq, xv, ALU.mult)
              k.act(sq, sq, AF.Exp, scale=-1.5957691216)
              k.ts("dve", sq, sq, 1.0, None, ALU.add)
              k.recip(sq, sq)
              if accum is None:
                  k.tt("dve", out, sq, xv, ALU.mult)
              else:
                  P.add("dve", lambda e: e.scalar_tensor_tensor(out, sq, 1.0, xv, ALU.mult, ALU.mult, accum_out=accum),
                        [out, accum], [sq, xv])

          SA = contextlib.ExitStack()
          with SA:
              w_inA = sb("w_inA", [128, 8, 768], BF, SA)
              kvT = sb("kvT", [128, 2, T], BF, SA)
              segsA = [(0, 384, 0), (384, 512, 512), (512, 640, 384), (640, 768, 640)]
              load_w(lambda kk, a, b: w_inA[:, kk, a:b], w_in_d[:, 512:1280], 8, 768, gcols[:, 0, :], segsA)
              ck(1, [w_inA[:, 0, 0:512], w_inA[:, 7, 256:768]])
              xtA = [sb("xtA%d" % i, [128, 1024], F32, SA) for i in range(2)]
              hTA = [sb("hTA%d" % i, [128, 8, 128], BF, SA) for i in range(2)]
              junkA = sb("junkA", [128, 1024], F32, SA)
              hbA = sb("hbA", [128, 1024], BF, SA)
              ssA = sb("ssA", [128, 1], F32, SA)
              rsA = sb("rsA", [128, 1], F32, SA)
              ssk = sb("ssk", [128, 4], F32, SA)
              zbA = sb("zbA", [128, 512], BF, SA)
              def front_a(tt_):
                  xt = xtA[tt_ % 2]
                  hT = hTA[tt_ % 2]
                  k.dma(xt[:], x_d[tt_ * 128:(tt_ + 1) * 128, :])
                  front(xt[:], hT, 0, junkA, ssA, rsA, hbA, slice(0, 128))
                  b0, b1 = ps[2 * (tt_ % 2)], ps[2 * (tt_ % 2) + 1]
                  for kk in range(8):
                      k.mm(b0[:, 0:512], hT[:, kk, :], w_inA[:, kk, 0:512], start=(kk == 0), stop=(kk == 7))
                      k.mm(b1[:, 0:256], hT[:, kk, :], w_inA[:, kk, 512:768], start=(kk == 0), stop=(kk == 7))

              def post_a(tt_):
                  b0, b1 = ps[2 * (tt_ % 2)], ps[2 * (tt_ % 2) + 1]
                  k.act(junkA[:, 0:256], b0[:, 256:512], AF.Square)
                  k.red(ssk[:], junkA[:, 0:256].rearrange("p (a d) -> p a d", d=64))
                  k.rsq(ssk[:], ssk[:], 64.0 * EPS)
                  k.cp("act", zbA[:, 0:256], b0[:, 0:256])
                  k.tt("dve", zbA[:, 256:512].rearrange("p (a d) -> p a d", d=64),
                       b0[:, 256:512].rearrange("p (a d) -> p a d", d=64),
                       ssk[:].unsqueeze(2).to_broadcast([128, 4, 64]), ALU.mult)
                  k.cp("act", Vall[:, tt_, :, 0:64], b1[:, 0:256].rearrange("p (a d) -> p a d", d=64))
                  k.tr(psb[5][:, 0:128], zbA[:, 0:128], ident[:])
                  k.tr(psb[5][:, 128:256], zbA[:, 128:256], ident[:])
                  for a in range(4):
                      k.tr(psb[5][0:64, 256 + a * 128:384 + a * 128], zbA[:, 256 + a * 64:320 + a * 64], ident[:])
                  tsl = slice(tt_ * 128, (tt_ + 1) * 128)
                  k.cp("dve", kvT[:, :, tsl], psb[5][:, 0:256].rearrange("p (a t) -> p a t", a=2))
                  for g in range(2):
                      k.ts("dve", KEs[g][0:64, tsl], psb[5][0:64, 256 + g * 128:384 + g * 128], gks8[:, 0:1], None, ALU.mult)
                      k.ts("dve", KEw[g][0:64, tsl], psb[5][0:64, 512 + g * 128:640 + g * 128], gkw8[:, 0:1], None, ALU.mult)

              front_a(0)
              for tt_ in range(NT):
                  if tt_ + 1 < NT:
                      front_a(tt_ + 1)
                  post_a(tt_)

              ck(2, [KEs[0][:, 0:1024], kvT[:, 0, 0:1024], KEw[1][:, 0:1024], Vall[:, 3, :, :].rearrange("p a d -> p (a d)")])
              SC = contextlib.ExitStack()
              with SC:
                  w1r = sb("w1r", [128, 32, 256], BF, SC)
                  w2b = sb("w2b", [128, 2, 64], BF, SC)
                  posT = sb("posT", [128, 32], BF, SC)
                  posf = sb("posf", [128, 32], F32, SC)
                  b1c = sb("b1c", [128, 2], F32, SC)
                  bias1 = sb("bias1", [128, 2], F32, SC)
                  b2t = sb("b2t", [128, 64], F32, SC)
                  gkc = sb("gkc", [128, 64], F32, SC)
                  hdn = sb("hdn", [128, 2, NCT * 128], BF, SC)
                  ja = sb("cja", [128, 256], F32, SC)
                  jb = sb("cjb", [128, 256], F32, SC)
                  kcf = sb("kcf", [128, 64], F32, SC)
                  kcb = sb("kcb", [128, 64], BF, SC)
                  ssc = sb("ssc", [128, 1], F32, SC)
                  k.dma(gkc[:], kc_g_d.partition_broadcast(128))
                  k.memset("dve", hdn[:], 0.0)
                  for kv in range(2):
                      w1v = cw1_d[kv].rearrange("(l d) h -> d l h", d=64)
                      for half in range(2):
                          for lc in range(4):
                              s = stg[cnt[0] % 2]
                              cnt[0] += 1
                              k.dma(s[half * 64:(half + 1) * 64, :].rearrange("p (l h) -> p l h", h=256),
                                    w1v[:, lc * 8:(lc + 1) * 8, :])
                              k.cp(("pool", "dve")[lc % 2],
                                   w1r[half * 64:(half + 1) * 64, lc * 8:(lc + 1) * 8, :],
                                   s[half * 64:(half + 1) * 64, :].rearrange("p (l h) -> p l h", h=256))
                      s = stg[cnt[0] % 2]
                      cnt[0] += 1
                      k.dma(s[:, 0:128].rearrange("p (c o) -> p c o", c=2), cw2_d[kv].rearrange("(c p) o -> p c o", p=128))
                      k.cp("dve", w2b[:], s[:, 0:128].rearrange("p (c o) -> p c o", c=2))
                      pos_d = (pos_k_d, pos_v_d)[kv]
                      for half in range(2):
                          k.dma(posf[half * 64:(half + 1) * 64, :], pos_d.rearrange("l d -> d l"), slow=True)
                      k.cp("dve", posT[:], posf[:])
                      k.dma(b1c[:], cb1_d[kv].rearrange("(c p) -> p c", p=128), slow=True)
                      k.dma(b2t[:], cb2_d[kv].partition_broadcast(128))
                      for hh in range(2):
                          for l in range(32):
                              k.mm(ps[6][:, hh:hh + 1], w1r[0:64, l, hh * 128:(hh + 1) * 128], posT[0:64, l:l + 1],
                                   start=(l == 0), stop=(l == 31))
                      k.tt("dve", bias1[:], ps[6][:, 0:2], b1c[:], ALU.add)
                      for g in range(2):
                          pr = slice(g * 64, (g + 1) * 64)
                          for hh in range(2):
                              for l in range(32):
                                  k.mm(ps[hh][:, 0:NC], w1r[pr, l, hh * 128:(hh + 1) * 128],
                                       kvT[pr, kv, l:l + 16 * (NC - 1) + 1:16], start=(l == 0), stop=(l == 31))
                          for hh in range(2):
                              for c0 in range(0, NC, 256):
                                  c1 = min(NC, c0 + 256)
                                  gelu_to(hdn[:, hh, c0:c1], ps[hh][:, c0:c1], c1 - c0, ja, jb, bias=bias1[:, hh:hh + 1])
                          for ct in range(NCT):
                              ncv = min(128, NC - ct * 128)
                              for hh in range(2):
                                  k.mm(ps[2][0:ncv, 0:64], hdn[:, hh, ct * 128:ct * 128 + ncv], w2b[:, hh, :],
                                       start=(hh == 0), stop=(hh == 1))
                              k.tt("dve", kcf[0:ncv, :], ps[2][0:ncv, 0:64], b2t[0:ncv, :], ALU.add)
                              if kv == 1:
                                  if ncv < 128:
                                      k.memset("dve", Vc[g][:, ct, :], 0.0)
                                  k.cp("dve", Vc[g][0:ncv, ct, :], kcf[0:ncv, :])
                              else:
                                  k.act(ja[0:ncv, 0:64], kcf[0:ncv, :], AF.Square, accum=ssc[0:ncv, :])
                                  k.rsq(ssc[0:ncv, :], ssc[0:ncv, :], 64.0 * EPS)
                                  k.ts("dve", kcf[0:ncv, :], kcf[0:ncv, :], ssc[0:ncv, 0:1], 8.0, ALU.mult, ALU.mult)
                                  if ncv < 128:
                                      k.memset("dve", kcb[:], 0.0)
                                  k.tt("dve", kcb[0:ncv, :], kcf[0:ncv, :], gkc[0:ncv, :], ALU.mult)
                                  k.tr(psb[5][0:64, 0:128], kcb[:, :], ident[:])
                                  k.cp("dve", KcT[g][:, ct * 128:(ct + 1) * 128], psb[5][0:64, 0:128])

          P.barrier()
          ck(3, [KcT[0][:, 0:128], KcT[1][:, 0:128], Vc[0][:, 0, :], Vc[1][:, 0, :]])
          SB_ = contextlib.ExitStack()
          with SB_:
              w_inB = sb("w_inB", [128, 8, 1560], BF, SB_)
              load_w(lambda kk, a, b: w_inB[:, kk, a:b], w_in_d[:, 0:512], 8, 512, gcols[:, 0, :], [(0, 512, 0)])
              load_w(lambda kk, a, b: w_inB[:, kk, a:b], w_in_d[:, 1280:2328], 8, 1048, gcols[:, 0, :],
                     [(0, 24, 1536), (24, 1048, 512)])
              w_outb = sb("w_outb", [128, 8, 1024], BF, SB_)
              load_w(lambda kk, a, b: w_outb[:, kk, a:b], w_out_d, 4, 1024, gcols[:, 3, 0:4])
              load_w(lambda kk, a, b: w_outb[:, 4 + kk, a:b], w_out_d[512:1024, :], 4, 1024, gcols[:, 3, 4:8])
              WmT = sb("WmT", [128, 8, 128], BF, SB_)
              wsf = sb("wsf", [128, 128], F32, SB_)
              wsb = sb("wsb", [128, 128], BF, SB_)
              xs = sb("xs", [128, 4, 1024], F32, SB_)
              hTB = sb("hTB", [128, 8, 128], BF, SB_)
              junkB = stg[0][:, 1024:2048]
              junkC = stg[1][:, 1024:2048]
              hbB = sb("hbB", [128, 1024], BF, SB_)
              ssB = sb("ssB", [128, 1], F32, SB_)
              rsB = sb("rsB", [128, 1], F32, SB_)
              ssq = sb("ssq", [128, 8], F32, SB_)
              qnb = sb("qnb", [128, 512], BF, SB_)
              R = sb("R", [128, 8, 512], BF, SB_)
              u_sb = sb("u_sb", [128, 512], F32, SB_)
              v_sb = sb("v_sb", [128, 512], F32, SB_)
              vnb = sb("vnb", [128, 512], BF, SB_)
              st1 = sb("st1", [128, 4], F32, SB_)
              gof = v_sb
              gob = sb("gob", [128, 512], BF, SB_)
              goT = sb("goT", [128, 4, 512], BF, SB_)
              aoT = sb("aoT", [128, 4, 128], BF, SB_)
              ao = sb("ao", [128, 4, 512], F32, SB_)
              aob = qnb
              rstd_a = sb("rstd_a", [128, 1], F32, SB_)
              sc = sb("sc", [128, 2, 256], F32, SB_)
              ef = sb("ef", [128, 2, 8, 256], BF, SB_)
              ebT = sb("ebT", [128, 2, NCT, 128], BF, SB_)
              csum = sb("csum", [128, 2, 8], F32, SB_)
              crin = sb("crin", [128, 2, 8], F32, SB_)
              icp = sb("icp", [128, 2, NCP + 1], F32, SB_)
              imp = sb("imp", [128, 2, 64], F32, SB_)
              impw = sb("impw", [128, 2, 64], F32, SB_)
              m8 = sb("m8", [128, 2, 16], F32, SB_)
              bm = sb("bm", [128, 2, 128], BF, SB_)
              mbT = sb("mbT", [128, 2, 512], F32, SB_)
              ocs = sb("ocs", [128, 4, 8, 64], F32, SB_)
              pti = [0]
              PT = [sb("PT%d" % i, [128, 512], BF, SB_) for i in range(3)]
              fsum = sb("fsum", [128, 2, 4], F32, SB_)
              coef = sb("coef", [128, 3, 4], F32, SB_)
              tmpo = u_sb[:, 0:256].rearrange("p (a d) -> p a d", d=64)
              x1t = [stg[0][:, 0:1024], stg[1][:, 0:1024]]
              k.memset("dve", icp[:], 0.0)
              k.memset("dve", bm[:], 0.0)
              k.memset("dve", ef[:], 0.0)
              for g in range(8):
                  k.dma(wsf[:], ws_d[g])
                  k.tr(psb[5][:, 0:128], trile[:], ident[:])
                  k.tt("dve", wsb[:], wsf[:], psb[5][:, 0:128], ALU.mult)
                  k.tr(psb[5][:, 128:256], wsb[:], ident[:])
                  k.cp("dve", WmT[:, g, :], psb[5][:, 128:256])

              for Q in range(NQ):
                  ju_a, ju_b = stg[0][:, 0:512], stg[0][:, 512:1024]
                  jv_a, jv_b = stg[0][:, 1024:1536], stg[0][:, 1536:2048]
                  fjunk, qsq, dummy = stg[1][:, 0:1024], stg[1][:, 1024:1536], stg[1][:, 1536:2048]

                  def front_b(qs):
                      tt_ = Q * 4 + qs
                      xt = xs[:, qs, :]
                      k.dma(xt, x_d[tt_ * 128:(tt_ + 1) * 128, :])
                      front(xt, hTB, 0, fjunk, ssB, rsB, hbB, slice(0, 128))
                      for kk in range(8):
                          for cb in range(3):
                              k.mm(ps[cb][:, 0:512], hTB[:, kk, :], w_inB[:, kk, cb * 512:(cb + 1) * 512],
                                   start=(kk == 0), stop=(kk == 7))
                          k.mm(ps[3][:, 0:24], hTB[:, kk, :], w_inB[:, kk, 1536:1560], start=(kk == 0), stop=(kk == 7))

                  def post1_b(qs):
                      tt_ = Q * 4 + qs
                      k.act(qsq, ps[0][:, 0:512], AF.Square)
                      k.red(ssq[:], qsq.rearrange("p (a d) -> p a d", d=64))
                      k.rsq(ssq[:], ssq[:], 64.0 * EPS)
                      k.tt("dve", qnb[:].rearrange("p (a d) -> p a d", d=64),
                           ps[0][:, 0:512].rearrange("p (a d) -> p a d", d=64),
                           ssq[:].unsqueeze(2).to_broadcast([128, 8, 64]), ALU.mult)
                      k.act(gates[:, tt_, :], ps[3][:, 0:24], AF.Exp, scale=-1.0)
                      k.act(ju_a, ps[1][:, 0:512], AF.Copy)
                      k.act(ju_b, ps[1][:, 0:512], AF.Square)
                      k.act(jv_a, ps[2][:, 0:512], AF.Copy)
                      k.act(jv_b, ps[2][:, 0:512], AF.Square)

                  def post2_b(qs):
                      tt_ = Q * 4 + qs
                      k.ts("dve", gates[:, tt_, :], gates[:, tt_, :], 1.0, None, ALU.add)
                      k.recip(gates[:, tt_, :], gates[:, tt_, :])
                      prs = ((ju_a, ju_b), (jv_a, jv_b))
                      for xv, sq in prs:
                          k.ts("dve", sq, sq, 0.044715, 1.0, ALU.mult, ALU.add)
                      for xv, sq in prs:
                          k.tt("dve", sq, sq, xv, ALU.mult)
                      for xv, sq in prs:
                          k.act(sq, sq, AF.Exp, scale=-1.5957691216)
                      for xv, sq in prs:
                          k.act(sq, sq, AF.Ln, bias=1.0)
                      for xv, sq in prs:
                          k.act(sq, sq, AF.Exp, scale=-1.0)
                      k.tt("pool", u_sb[:], ju_b, ju_a, ALU.mult)
                      P.add("dve", lambda e: e.scalar_tensor_tensor(v_sb[:], jv_b, 1.0, jv_a, ALU.mult, ALU.mult, accum_out=st1[:, 0:1]),
                            [v_sb[:], st1[:, 0:1]], [jv_b, jv_a])
                      k.act(dummy, v_sb[:], AF.Square, accum=st1[:, 1:2])
                      k.ts("dve", st1[:, 2:3], st1[:, 0:1], 1.0 / 512, None, ALU.mult)
                      k.stt("dve", st1[:, 3:4], st1[:, 2:3], -1.0, st1[:, 2:3], ALU.mult, ALU.mult)
                      k.stt("dve", st1[:, 3:4], st1[:, 1:2], 1.0 / 512, st1[:, 3:4], ALU.mult, ALU.add)
                      k.rsq(st1[:, 3:4], st1[:, 3:4], EPS)
                      k.ts("dve", v_sb[:], v_sb[:], st1[:, 2:3], st1[:, 3:4], ALU.subtract, ALU.mult)
                      k.tt("dve", v_sb[:], v_sb[:], lng[:], ALU.mult)
                      k.tt("dve", vnb[:], v_sb[:], lnb[:], ALU.add)
                      for g in range(8):
                          k.mm(ps[6][:, g * 64:(g + 1) * 64], WmT[:, g, :], vnb[:, g * 64:(g + 1) * 64])
                      k.tt("dve", gof[:].rearrange("p (a d) -> p a d", d=64),
                           ps[6][:, 0:512].rearrange("p (a d) -> p a d", d=64),
                           bstab[:].unsqueeze(2).to_broadcast([128, 8, 64]), ALU.add)
                      k.tt("pool", gob[:], gof[:], u_sb[:], ALU.mult)
                      k.act(dummy, gob[:], AF.Square, accum=rstd_g[:, tt_:tt_ + 1])
                      k.rsq(rstd_g[:, tt_:tt_ + 1], rstd_g[:, tt_:tt_ + 1], 512.0 * EPS)
                      k.ts("dve", rstd_g[:, tt_:tt_ + 1], rstd_g[:, tt_:tt_ + 1], 22.627416998, None, ALU.mult)
                      for c4 in range(4):
                          k.tr(psb[7][:, c4 * 128:(c4 + 1) * 128], gob[:, c4 * 128:(c4 + 1) * 128], ident[:])
                      k.cp("act", goT[:, :, qs * 128:(qs + 1) * 128], psb[7][:, 0:512].rearrange("p (c t) -> p c t", c=4))
                      for h in range(8):
                          k.tr(psb[5][0:64, h * 128:(h + 1) * 128], qnb[:, h * 64:(h + 1) * 64], ident[:])
                      k.ts("dve", R[0:64, :, qs * 128:(qs + 1) * 128],
                           psb[5][0:64, 0:1024].rearrange("p (h t) -> p h t", h=8), gq[:, 0:1], None, ALU.mult)

                  front_b(0)
                  for qs in range(4):
                      post1_b(qs)
                      if qs + 1 < 4:
                          front_b(qs + 1)
                      post2_b(qs)

                  ck(4, [R[:, 0, :], R[:, 7, :], gof[:], goT[:, 0, :], gates[:, 0:4, :].rearrange("p a d -> p (a d)")])

                  def make_sel(qs):
                      qt = Q * 4 + qs
                      qp = qs % 2
                      ocb = ps[2 + qp]
                      th = []
                      th.append(lambda: k.ts("dve", csum[:, qp, :], csum[:, qp, :], 1e-30, None, ALU.max))
                      th.append(lambda: k.recip(crin[:, qp, :], csum[:, qp, :]))
                      th.append(lambda: k.tt("dve", ocs[:, qs, :, :], ocb[:, 0:512].rearrange("p (h d) -> p h d", h=8),
                                             crin[:, qp, :].unsqueeze(2).to_broadcast([128, 8, 64]), ALU.mult))
                      span = 4 * (NB - 1) + 1
                      for g in range(2):
                          th.append(lambda g=g: k.ts("dve", icp[:, g, 1:1 + NC], ef[:, qp, 4 * g, 0:NC],
                                                     crin[:, qp, 4 * g:4 * g + 1], None, ALU.mult))
                          for r in range(1, 4):
                              th.append(lambda g=g, h=4 * g + r: k.stt("dve", icp[:, g, 1:1 + NC], ef[:, qp, h, 0:NC],
                                                                       crin[:, qp, h:h + 1], icp[:, g, 1:1 + NC], ALU.mult, ALU.add))
                          th.append(lambda g=g: k.cp("dve", imp[:, g, 0:NB], icp[:, g, 0:span:4]))
                          for kk_, wk in ((1, 2.0), (2, 2.0), (3, 2.0), (4, 1.0)):
                              th.append(lambda g=g, kk_=kk_, wk=wk: k.stt("dve", imp[:, g, 0:NB], icp[:, g, kk_:kk_ + span:4], wk,
                                                                          imp[:, g, 0:NB], ALU.mult, ALU.add))
                      if NB < 64:
                          th.append(lambda: k.memset("dve", imp[:, :, NB:64], -1.0))
                      f0 = 2 * (NT - 1) - 2 * qt
                      for g in range(2):
                          th.append(lambda g=g: k.tt("dve", imp[:, g, 0:NB], imp[:, g, 0:NB], Fw[:, f0:f0 + NB], ALU.max))
                          th.append(lambda g=g: k.tt("dve", imp[:, g, 0:NB], imp[:, g, 0:NB], Uw[:, f0:f0 + NB], ALU.min))
                      th.append(lambda: k.memset("dve", imp[:, :, 0:1], 3.0e4))
                      for g in range(2):
                          th.append(lambda g=g: k.max8(m8[:, g, 0:8], imp[:, g, :]))
                          th.append(lambda g=g: k.mrep(impw[:, g, :], m8[:, g, 0:8], imp[:, g, :], -2.0))
                          th.append(lambda g=g: k.max8(m8[:, g, 8:16], impw[:, g, :]))
                          th.append(lambda g=g: k.ts("dve", impw[:, g, :], imp[:, g, :], m8[:, g, 15:16], None, ALU.is_ge))
                          th.append(lambda g=g: k.ts("dve", bm[:, g, 64:128], impw[:, g, :], -NEG, NEG, ALU.mult, ALU.add))
                          th.append(lambda g=g: k.tr(psb[5][:, g * 128:(g + 1) * 128], bm[:, g, :], ident[:]))
                          th.append(lambda g=g: k.cp("dve", mbT[64:128, g, qs * 128:(qs + 1) * 128],
                                                     psb[5][64:128, g * 128:(g + 1) * 128]))
                      return th

                  def heads(qs, pending):
                      qt = Q * 4 + qs
                      qp = qs % 2
                      nctq = min(NCT, (8 * qt + 7 + 127) // 128)
                      gs0 = 8 * (NT - 1) - 8 * qt
                      ocb = ps[2 + qp]

                      def stage_a(h):
                          g = h // 4
                          sbk = ps[0] if h % 2 == 0 else ps[6]
                          k.mm(sbk[:, 0:NC], R[0:64, h, qs * 128:(qs + 1) * 128], KcT[g][:, 0:NC])
                          k.stt("dve", sc[:, h % 2, 0:NC], Gw[:, gs0:gs0 + NC], SLOPES[h], sbk[:, 0:NC], ALU.mult, ALU.add)
                          k.act(ef[:, qp, h, 0:NC], sc[:, h % 2, 0:NC], AF.Exp, accum=csum[:, qp, h:h + 1])

                      def stage_b(h):
                          g = h // 4
                          tb = psb[1] if h % 2 == 0 else psb[7]
                          for ct in range(nctq):
                              k.tr(tb[:, ct * 128:(ct + 1) * 128], ef[:, qp, h, ct * 128:(ct + 1) * 128], ident[:])
                          k.cp("act", ebT[:, h % 2, 0:nctq, :], tb[:, 0:nctq * 128].rearrange("p (c t) -> p c t", c=nctq))
                          for ct in range(nctq):
                              k.mm(ocb[:, h * 64:(h + 1) * 64], ebT[:, h % 2, ct, :], Vc[g][:, ct, :],
                                   start=(ct == 0), stop=(ct == nctq - 1))

                      stage_a(0)
                      for h in range(8):
                          if h + 1 < 8:
                              stage_a(h + 1)
                          stage_b(h)
                          nd = (len(pending) + (7 - h)) // (8 - h)
                          for _ in range(nd):
                              pending.pop(0)()

                  pending = []
                  for qs in range(4):
                      heads(qs, pending)
                      assert not pending
                      pending = make_sel(qs)
                  for th_ in pending:
                      th_()

                  ck(5, [mbT[:, 0, :], mbT[:, 1, :], ocs[:, 3, :, :].rearrange("p a d -> p (a d)"), imp[:, 0, :]])
                  qcols = slice(Q * 512, (Q + 1) * 512)
                  gsl = slice(4 * Q, 4 * Q + 4)
                  tasks = []
                  for h in range(8):
                      for br in range(2):
                          kt_lo = max(0, 4 * Q - 4) if br == 0 else 0
                          kts = list(range(kt_lo, 4 * Q + 4))
                          for kt in kts:
                              tasks.append(dict(h=h, br=br, kt=kt, gfirst=(kt == kts[0]), glast=(kt == kts[-1]), n=len(tasks)))

                  def obank(h, br):
                      return (ps[6 + br] if h % 2 == 0 else ps[br])

                  def emit_S(t):
                      h, br, kt = t["h"], t["br"], t["kt"]
                      g = h // 4
                      if t["gfirst"]:
                          if br == 0:
                              k.ts("dve", R[64:128, h, :], Dt[64:128, qcols], SLOPES[h], None, ALU.mult)
                          else:
                              k.tt("dve", R[64:128, h, :], R[64:128, h, :], mbT[64:128, g, :], ALU.add)
                      i = kt - 4 * Q
                      c_lo = max(0, i)
                      c_hi = min(3, i + 4) if br == 0 else 3
                      t["c"] = (c_lo, c_hi)
                      n0, n1 = c_lo * 128, (c_hi + 1) * 128
                      KE = (KEw, KEs)[br][g]
                      sbank = (ps[3], ps[4], ps[5])[t["n"] % 3]
                      k.mm(sbank[:, n0:n1], KE[:, kt * 128:(kt + 1) * 128], R[:, h, n0:n1])

                  def emit_rest(t):
                      h, br, kt = t["h"], t["br"], t["kt"]
                      g = h // 4
                      i = kt - 4 * Q
                      c_lo, c_hi = t["c"]
                      n0, n1 = c_lo * 128, (c_hi + 1) * 128
                      sbank = (ps[3], ps[4], ps[5])[t["n"] % 3]
                      pt = PT[t["n"] % len(PT)]
                      Ob = obank(h, br)
                      vidx = (2 + g, g)[br]
                      k.act(pt[:, n0:n1], sbank[:, n0:n1], AF.Exp, bias=wb[:, h:h + 1])
                      if i >= 0:
                          k.tt("dve", pt[:, i * 128:(i + 1) * 128], pt[:, i * 128:(i + 1) * 128], trile[:], ALU.mult)
                      if br == 0 and 0 <= i + 4 <= 3:
                          cc = i + 4
                          k.tt("dve", pt[:, cc * 128:(cc + 1) * 128], pt[:, cc * 128:(cc + 1) * 128], trigt[:], ALU.mult)
                      for c in range(c_lo, c_hi + 1):
                          k.mm(Ob[:, c * 65:(c + 1) * 65], pt[:, c * 128:(c + 1) * 128], Vall[:, kt, vidx, 0:65],
                               start=(t["gfirst"] and c == c_lo), stop=(kt == 4 * Q + c), skip=True)
                      if br == 1 and t["glast"]:
                          Ow = obank(h, 0)[:, 0:260].rearrange("p (c d) -> p c d", d=65)
                          Os = obank(h, 1)[:, 0:260].rearrange("p (c d) -> p c d", d=65)
                          k.ts("dve", fsum[:, 0, :], Ow[:, :, 64], 1e-30, None, ALU.max)
                          k.ts("dve", fsum[:, 1, :], Os[:, :, 64], 1e-30, None, ALU.max)
                          k.recip(fsum[:], fsum[:])
                          k.tt("dve", coef[:, 0, :], fsum[:, 0, :], gates[:, gsl, 3 * h + 2], ALU.mult)
                          k.tt("dve", coef[:, 1, :], fsum[:, 1, :], gates[:, gsl, 3 * h + 1], ALU.mult)
                          dst = ao[:, :, h * 64:(h + 1) * 64]
                          k.tt("dve", dst, ocs[:, :, h, :], gates[:, gsl, 3 * h:3 * h + 1].to_broadcast([128, 4, 64]), ALU.mult)
                          k.tt("dve", tmpo, Ow[:, :, 0:64], coef[:, 0, :].unsqueeze(2).to_broadcast([128, 4, 64]), ALU.mult)
                          k.tt("dve", dst, dst, tmpo, ALU.add)
                          k.tt("dve", tmpo, Os[:, :, 0:64], coef[:, 1, :].unsqueeze(2).to_broadcast([128, 4, 64]), ALU.mult)
                          k.tt("dve", dst, dst, tmpo, ALU.add)

                  emit_S(tasks[0])
                  if len(tasks) > 1:
                      emit_S(tasks[1])
                  for n_, t in enumerate(tasks):
                      if n_ + 2 < len(tasks):
                          emit_S(tasks[n_ + 2])
                      emit_rest(t)

                  ck(6, [ao[:, 0, :], ao[:, 3, :]])
                  for qs in range(4):
                      tt_ = Q * 4 + qs
                      x1 = x1t[tt_ % 2]
                      k.act(junkB[:, 0:512], ao[:, qs, :], AF.Square, accum=rstd_a[:])
                      k.rsq(rstd_a[:], rstd_a[:], 512.0 * EPS)
                      k.ts("dve", rstd_a[:], rstd_a[:], 22.627416998, None, ALU.mult)
                      k.cp("pool", aob[:], ao[:, qs, :])
                      for c4 in range(4):
                          k.tr(psb[5][:, c4 * 128:(c4 + 1) * 128], aob[:, c4 * 128:(c4 + 1) * 128], ident[:])
                      k.cp("act", aoT[:], psb[5][:, 0:512].rearrange("p (c t) -> p c t", c=4))
                      for half in range(2):
                          hs = slice(half * 512, (half + 1) * 512)
                          for c4 in range(4):
                              k.mm(ps[half][:, :], aoT[:, c4, :], w_outb[:, c4, hs], start=(c4 == 0), stop=(c4 == 3))
                          for c4 in range(4):
                              k.mm(ps[2 + half][:, :], goT[:, c4, qs * 128:(qs + 1) * 128], w_outb[:, 4 + c4, hs],
                                   start=(c4 == 0), stop=(c4 == 3))
                          k.stt("dve", x1[:, hs], ps[half][:, :], rstd_a[:, 0:1], xs[:, qs, hs], ALU.mult, ALU.add)
                          k.stt("dve", x1[:, hs], ps[2 + half][:, :], rstd_g[:, tt_:tt_ + 1], x1[:, hs], ALU.mult, ALU.add)
                      k.dma(x1_d[tt_ * 128:(tt_ + 1) * 128, :], x1)
          SATT.close()
          cur[0] = ES
          P.barrier()

          SC3 = contextlib.ExitStack()
          with SC3:
              w1b = sb("w1b", [128, 8, 4096], BF, SC3)
              w2f = sb("w2f", [128, 32, 1024], BF, SC3)
              wpg = sb("wpg", [128, 8, 1024], BF, SC3)
              wpl = sb("wpl", [128, 2, 1024], BF, SC3)
              fT = sb("fT", [128, 32, 256], BF, SC3)
              fTf = fT[:].rearrange("p a b -> p (a b)").bitcast(F32)
              st4 = [stg[0], stg[1], fTf[:, 0:2048], fTf[:, 2048:4096]]
              load_w(lambda kk, a, b: w1b[:, kk, a:b], w_ff1_d, 8, D_FF, gcols[:, 1, :], stages=st4)
              for c8 in range(8):
                  k.dma(w2f[:, 4 * c8:4 * c8 + 4, :], w_ff2_d[c8 * 512:(c8 + 1) * 512, :].rearrange("(k p) c -> p k c", p=128), eng="pool")
              load_w(lambda kk, a, b: wpg[:, kk, a:b], w_pg_d, 8, D_MODEL, gcols[:, 2, :], stages=st4)
              k.dma(wpl[:], w_ple_d.rearrange("(k p) c -> p k c", p=128), eng="pool")
              xc = sb("xc", [128, 2, 1024], F32, SC3)
              x2 = sb("x2", [128, 2, 1024], F32, SC3)
              hTC = sb("hTC", [128, 8, 256], BF, SC3)
              h3T = sb("h3T", [128, 8, 128], BF, SC3)
              junkD = stg[0][:, 0:1024]
              hbC = sb("hbC", [128, 1024], BF, SC3)
              ssC = sb("ssC", [128, 1], F32, SC3)
              rsC = sb("rsC", [128, 1], F32, SC3)
              rls = [stg[1][:, 1024:1280], stg[1][:, 1536:1792]]
              ptf = stg[1][:, 1280:1536]
              ptb = sb("ptb", [128, 256], BF, SC3)
              pT = sb("pT", [128, 2, 128], BF, SC3)
              th = stg[0][:, 1024:1536]
              outt = stg[1][:, 0:1024]
              xcs = [xc, x2]
              th2 = [stg[0][:, 1024:1536], stg[0][:, 1536:2048]]
              NBT = T // 256

              def s1(b):
                  for j in range(2):
                      tt_ = 2 * b + j
                      k.dma(xcs[b % 2][:, j, :], x1_d[tt_ * 128:(tt_ + 1) * 128, :])
                      front(xcs[b % 2][:, j, :], hTC, 0, junkD, ssC, rsC, hbC, slice(j * 128, (j + 1) * 128))

              def drain(pend, slots_left):
                  nd = (len(pend) + slots_left - 1) // max(1, slots_left)
                  for _ in range(min(nd, len(pend))):
                      pend.pop(0)()

              def s2_s3(b, pend):
                  x2c = xcs[b % 2]
                  for fc in range(32):
                      bank = ps[fc % 2]
                      for kk in range(8):
                          k.mm(bank[:, 0:256], w1b[:, kk, fc * 128:(fc + 1) * 128], hTC[:, kk, :], start=(kk == 0), stop=(kk == 7))
                      rl = rls[fc % 2]
                      k.act(rl, bank[:, 0:256], AF.Relu)
                      k.tt(("pool", "dve")[fc % 2], fT[:, fc, :], rl, rl, ALU.mult)
                      drain(pend, 36 - fc)
                  for gi in range(4):
                      j, half = gi // 2, gi % 2
                      hs = slice(half * 512, (half + 1) * 512)
                      bank = ps[2 + gi % 2]
                      for fc in range(32):
                          k.mm(bank[:, :], fT[:, fc, j * 128:(j + 1) * 128], w2f[:, fc, hs], start=(fc == 0), stop=(fc == 31))
                      k.tt("dve", x2c[:, j, hs], bank[:, :], x2c[:, j, hs], ALU.add)
                      drain(pend, 4 - gi)

              def ple_thunks(b, j):
                  tt_ = 2 * b + j
                  xin = xcs[b % 2][:, j, :]
                  tl = []

                  def f_a():
                      k.act(junkD[:, 0:1024], xin, AF.Square, accum=ssC[:])
                      k.rsq(rsC[:], ssC[:], float(D_MODEL * EPS))
                      k.ts("dve", hbC[:], xin, rsC[:, 0:1], 32.0, ALU.mult, ALU.mult)
                      for kk in range(8):
                          k.tr(psb[4][:, kk * 128:(kk + 1) * 128], hbC[:, kk * 128:(kk + 1) * 128], ident[:])

                  def f_p():
                      k.dma(ptf, p_d[tt_ * 128:(tt_ + 1) * 128, :])
                      k.cp("pool", ptb[:], ptf)

                  def f_ptr():
                      for c2 in range(2):
                          k.tr(psb[5][:, c2 * 128:(c2 + 1) * 128], ptb[:, c2 * 128:(c2 + 1) * 128], ident[:])

                  tl.append(f_a)
                  tl.append(f_p)
                  tl.append(lambda: k.cp("act", h3T[:, :, 0:128], psb[4][:, 0:1024].rearrange("p (k t) -> p k t", k=8)))
                  tl.append(f_ptr)
                  tl.append(lambda: k.cp("act", pT[:], psb[5][:, 0:256].rearrange("p (c t) -> p c t", c=2)))
                  for half in range(2):
                      hs = slice(half * 512, (half + 1) * 512)
                      thh = th2[half]

                      def f_g(hs=hs):
                          for kk in range(8):
                              k.mm(ps[6][:, :], h3T[:, kk, :], wpg[:, kk, hs], start=(kk == 0), stop=(kk == 7))

                      def f_w(hs=hs):
                          for c2 in range(2):
                              k.mm(ps[7][:, :], pT[:, c2, :], wpl[:, c2, hs], start=(c2 == 0), stop=(c2 == 1))

                      tl.append(f_g)
                      tl.append(f_w)
                      tl.append(lambda thh=thh: k.act(thh, ps[6][:, :], AF.Exp, scale=-1.0))
                      tl.append(lambda thh=thh: k.act(thh, thh, AF.Ln, bias=1.0))
                      tl.append(lambda thh=thh: k.act(thh, thh, AF.Exp, scale=-1.0))
                      tl.append(lambda thh=thh: k.tt("dve", thh, thh, ps[7][:, :], ALU.mult))
                      tl.append(lambda thh=thh, hs=hs: k.tt("pool", outt[:, hs], thh, xin[:, hs], ALU.add))
                  tl.append(lambda: k.dma(out_d[tt_ * 128:(tt_ + 1) * 128, :], outt))
                  return tl

              pend = []
              s1(0)
              for b in range(NBT):
                  s2_s3(b, pend)
                  assert not pend
                  if b + 1 < NBT:
                      s1(b + 1)
                  pend = ple_thunks(b, 0) + ple_thunks(b, 1)
              for t_ in pend:
                  t_()
    except _Stop:
        pass
    P.emit(nc)
    return nc


_NC_CACHE = {}


def _core_inputs(inp, b, consts):
    sq = lambda a: np.ascontiguousarray(np.asarray(a)[0], dtype=np.float32)
    m = {
        "x": np.ascontiguousarray(np.asarray(inp["x"])[b], dtype=np.float32),
        "p": np.ascontiguousarray(np.asarray(inp["p"])[0, b], dtype=np.float32),
    }
    for name in ("g_mix", "w_in", "q_norm_g", "kc_norm_g", "ks_norm_g", "kw_norm_g", "cmp_pos_k", "cmp_pos_v",
                 "cmp_k_w1", "cmp_k_b1", "cmp_k_w2", "cmp_k_b2", "cmp_v_w1", "cmp_v_b1", "cmp_v_w2", "cmp_v_b2",
                 "gmlp_ln_g", "gmlp_ln_b", "gmlp_ws", "gmlp_bs", "out_g_nsa", "out_g_gmlp", "w_out",
                 "g_ff", "w_ff1", "w_ff2", "g_ple", "w_ple_gate", "w_ple"):
        m[name] = sq(inp[name])
    m.update(consts)
    return m


def kernel(_stop=None, **inputs):
    x = np.asarray(inputs["x"])
    B, T = x.shape[0], x.shape[1]
    if T not in _NC_CACHE:
        _NC_CACHE[T] = build_nc(T, _stop)
    nc = _NC_CACHE[T]
    consts = make_consts(T)
    in_maps = [_core_inputs(inputs, b, consts) for b in range(B)]
    res = run_bass_kernel_spmd(nc, in_maps, core_ids=list(range(B)))
    return np.stack([np.asarray(r["out"], dtype=np.float32) for r in res.results], axis=0)
```

```python
import contextlib
import math
import numpy as np
import ml_dtypes
import concourse.bass as bass
import concourse.mybir as mybir
from concourse.bass_utils import run_bass_kernel_spmd

F32 = mybir.dt.float32
BF = mybir.dt.bfloat16
ALU = mybir.AluOpType
AF = mybir.ActivationFunctionType
AX = mybir.AxisListType
DSZ = {F32: 4, BF: 2}

D_MODEL = 1024
IN_COLS = 2328
D_FF = 4096
D_PLE = 256
EPS = 1e-6
NDS = 24
SLOPES = [2.0 ** (-(h + 1)) for h in range(8)]
NEG = -30000.0


class Prog:
    ENGS = ("pe", "act", "dve", "pool", "sp")

    def __init__(self):
        self.ops = {e: [] for e in self.ENGS}
        self.all = []
        self.track = {}
        self.seen = {e: {} for e in self.ENGS}
        self.seen_dma = {e: set() for e in self.ENGS}
        self.ndma = 0
        self.dram = set()
        self.psum = set()
        self.pending = {}

    def barrier(self):
        lasts = {E: self.ops[E][-1]["idx"] for E in self.ENGS if self.ops[E]}
        dmas = {}
        for op in self.all:
            if op["dma"]:
                dmas[op["dsem"]] = op["idx"]
        for X in self.ENGS:
            lst = self.pending.setdefault(X, [])
            for E, d in lasts.items():
                if E != X:
                    lst.append(d)
            lst.extend(dmas.values())

    def box(self, ap):
        name = ap.name
        a = ap.ap
        off = int(ap.offset)
        esz = DSZ.get(ap.dtype, 4)
        if name in self.dram:
            ext = 1
            for st, cnt in a:
                ext += (cnt - 1) * abs(st)
            return name, (0, 1, off * esz, (off + ext) * esz)
        if name in self.psum:
            return name, (0, 128, 0, 2048)
        pstride = a[0][0]
        if pstride == 0:
            p0, f0 = 0, off
        else:
            p0, f0 = off // pstride, off % pstride
        ext = 1
        for st, cnt in a[1:]:
            ext += (cnt - 1) * abs(st)
        return name, (p0, p0 + a[0][1], f0 * esz, (f0 + ext) * esz)

    @staticmethod
    def _ov(a, b):
        return a[0] < b[1] and b[0] < a[1] and a[2] < b[3] and b[2] < a[3]

    @staticmethod
    def _inside(a, b):
        return a[0] >= b[0] and a[1] <= b[1] and a[2] >= b[2] and a[3] <= b[3]

    def add(self, eng, fn, outs, ins, dma=False):
        idx = len(self.all)
        op = dict(eng=eng, fn=fn, waits=[], sig=False, dma=dma, seq=len(self.ops[eng]) + 1, idx=idx)
        deps = {}
        for d in self.pending.pop(eng, []):
            deps[d] = True
        inb = [self.box(a) for a in ins]
        outb = [self.box(a) for a in outs]
        for name, b in inb:
            isps = name in self.psum
            for key, ent in self.track.get(name, {}).items():
                if not self._ov(key[0], b):
                    continue
                if key[2] == "w":
                    deps[ent] = True
                elif isps and key[1] != eng:
                    deps.setdefault(ent, False)
        for name, b in outb:
            tr = self.track.setdefault(name, {})
            for key in list(tr.keys()):
                if self._ov(key[0], b):
                    deps.setdefault(tr[key], False)
                    if self._inside(key[0], b):
                        del tr[key]
        for name, b in inb:
            self.track.setdefault(name, {})[(b, eng, "r")] = idx
        for name, b in outb:
            self.track.setdefault(name, {})[(b, eng, "w")] = idx
        for d in sorted(deps):
            raw = deps[d]
            dop = self.all[d]
            if dop["dma"]:
                if d in self.seen_dma[eng]:
                    continue
                self.seen_dma[eng].add(d)
                op["waits"].append(("d", d))
            else:
                E = dop["eng"]
                if E == eng and not dma:
                    if eng == "pe":
                        continue
                if self.seen[eng].get(E, 0) >= dop["seq"]:
                    continue
                self.seen[eng][E] = dop["seq"]
                dop["sig"] = True
                op["waits"].append(("c", d))
        if dma:
            j = self.ndma
            self.ndma += 1
            op["dsem"] = j % NDS
            op["dval"] = 16 * (j // NDS + 1)
            op["dprev"] = 16 * (j // NDS)
        self.all.append(op)
        self.ops[eng].append(op)
        return op

    def emit(self, nc):
        with contextlib.ExitStack() as st:
            esem = {e: st.enter_context(nc.semaphore("s_" + e)) for e in self.ENGS}
            dsem = [st.enter_context(nc.semaphore("d%d" % i)) for i in range(NDS)]
            for e in self.ENGS:
                c = 0
                for op in self.ops[e]:
                    if op["sig"] and not op["dma"]:
                        c += 1
                        op["cnt"] = c
            dfinal = [0] * NDS
            for op in self.all:
                if op["dma"]:
                    dfinal[op["dsem"]] = max(dfinal[op["dsem"]], op["dval"])
            block = st.enter_context(nc.Block())

            def run(e, eng):
                for op in self.ops[e]:
                    if op["dma"] and op["dprev"] > 0:
                        eng.wait_ge(dsem[op["dsem"]], op["dprev"])
                    for kind, d in op["waits"]:
                        dop = self.all[d]
                        if kind == "d":
                            eng.wait_ge(dsem[dop["dsem"]], dop["dval"])
                        else:
                            eng.wait_ge(esem[dop["eng"]], dop["cnt"])
                    ins = op["fn"](eng)
                    if op["dma"]:
                        ins.then_inc(dsem[op["dsem"]], 16)
                    elif op["sig"]:
                        ins.then_inc(esem[e], 1)
                if e == "sp":
                    for i in range(NDS):
                        if dfinal[i] > 0:
                            eng.wait_ge(dsem[i], dfinal[i])

            block.tensor(lambda eng: run("pe", eng))
            block.scalar(lambda eng: run("act", eng))
            block.vector(lambda eng: run("dve", eng))
            block.gpsimd(lambda eng: run("pool", eng))
            block.sync(lambda eng: run("sp", eng))


def _aps(*xs):
    return [x for x in xs if x is not None and not isinstance(x, (int, float))]


class K:
    def __init__(self, P):
        self.P = P
        self.consts = {}

    def eps_ap(self, val, like):
        t = self.consts[round(float(val), 12)]
        p0 = like.base_partition()
        return t[p0:p0 + like.partition_size(), 0:1]

    def mm(self, out, lhsT, rhs, start=True, stop=True, skip=False):
        if skip:
            self.P.add("pe", lambda e: e.matmul(out, lhsT, rhs, start=start, stop=stop, skip_group_check=True), [out], [lhsT, rhs])
        else:
            self.P.add("pe", lambda e: e.matmul(out, lhsT, rhs, start=start, stop=stop), [out], [lhsT, rhs])

    def tr(self, out, in_, ident):
        self.P.add("pe", lambda e: e.transpose(out, in_, ident), [out], [in_, ident])

    def act(self, out, in_, func, bias=None, scale=None, accum=None):
        kw = {}
        if bias is not None:
            kw["bias"] = bias
        if scale is not None:
            kw["scale"] = scale
        if accum is not None:
            kw["accum_out"] = accum
        self.P.add("act", lambda e: e.activation(out, in_, func, **kw), _aps(out, accum), _aps(in_, bias, scale))

    def ts(self, eng, out, in0, s1, s2, op0, op1=None):
        if op1 is None:
            self.P.add(eng, lambda e: e.tensor_scalar(out, in0, s1, None, op0), [out], _aps(in0, s1))
        else:
            self.P.add(eng, lambda e: e.tensor_scalar(out, in0, s1, s2, op0, op1), [out], _aps(in0, s1, s2))

    def tt(self, eng, out, a, b, op):
        self.P.add(eng, lambda e: e.tensor_tensor(out, a, b, op), [out], [a, b])

    def stt(self, eng, out, in0, scalar, in1, op0, op1):
        self.P.add(eng, lambda e: e.scalar_tensor_tensor(out, in0, scalar, in1, op0, op1), [out], _aps(in0, scalar, in1))

    def rsq(self, out, in_, eps, mul=1.0):
        self.act(out, in_, AF.Ln, bias=float(eps))
        self.act(out, out, AF.Exp, scale=-0.5)

    def cp(self, eng, out, in_):
        if eng == "act":
            self.P.add("act", lambda e: e.copy(out, in_), [out], [in_])
        else:
            self.P.add(eng, lambda e: e.tensor_copy(out, in_), [out], [in_])

    def memset(self, eng, out, val):
        self.P.add(eng, lambda e: e.memset(out, val), [out], [])

    def red(self, out, in_, op=ALU.add):
        self.P.add("dve", lambda e: e.tensor_reduce(out, in_, AX.X, op), [out], [in_])

    def recip(self, out, in_):
        self.P.add("dve", lambda e: e.reciprocal(out, in_), [out], [in_])

    def max8(self, out, in_):
        self.P.add("dve", lambda e: e.max(out, in_), [out], [in_])

    def mrep(self, out, rep, vals, imm):
        self.P.add("dve", lambda e: e.match_replace(out, rep, vals, imm), [out], [rep, vals])

    def dma(self, out, in_, slow=False):
        if slow:
            self.P.add("sp", lambda e: e.dma_start(out=out, in_=in_, allow_slow_non_contiguous=True), [out], [in_], dma=True)
        else:
            self.P.add("sp", lambda e: e.dma_start(out=out, in_=in_), [out], [in_], dma=True)


def make_consts(T):
    NT = T // 128
    NB = T // 64
    NC = T // 16 - 1
    bf = ml_dtypes.bfloat16
    c = {}
    c["c_ident"] = np.eye(128, dtype=np.float32).astype(bf)
    key = np.arange(T)
    E = (key[None, :] // 64 == np.arange(64)[:, None]).astype(np.float32)
    c["c_E"] = E.astype(bf)
    D = 64.0 * (np.arange(64)[:, None] - (key[None, :] // 64))
    c["c_D"] = D.astype(np.float32).astype(bf)
    p = np.arange(128)[:, None]
    f = np.arange(128)[None, :]
    c["c_trile"] = (p <= f).astype(np.float32).astype(bf)
    c["c_trigt"] = (p > f).astype(np.float32).astype(bf)
    W = NC + 8 * (NT - 1)
    m = np.arange(W)[None, :] - 8 * (NT - 1)
    G = np.where(16 * m + 31 <= p, -(p - 16.0 * m - 15.5), -1.0e6)
    c["c_G"] = G.astype(np.float32)
    W2 = NB + 2 * (NT - 1)
    jp = np.arange(W2)[None, :] - 2 * (NT - 1)
    cur = (p >= 64).astype(np.int64)
    Fw = np.where(jp == cur, 2.0e4, np.where(jp == cur - 1, 1.0e4, 0.0))
    Uw = np.where(jp <= cur, 1.0e9, -1.0)
    c["c_Fw"] = Fw.astype(np.float32)
    c["c_Uw"] = Uw.astype(np.float32)
    wb = np.zeros((128, 8), np.float32)
    for h in range(8):
        wb[:, h] = SLOPES[h] * (np.arange(128) % 64)
    c["c_wb"] = wb
    return c


class _Stop(Exception):
    pass


def build_nc(T, stop=None):
    NT = T // 128
    NQ = T // 512
    NB = T // 64
    NC = T // 16 - 1
    NCP = T // 16
    NCT = (NCP + 127) // 128
    WG = NC + 8 * (NT - 1)
    W2 = NB + 2 * (NT - 1)
    nc = bass.Bass("TRN2", target_bir_lowering=False)
    P = Prog()
    k = K(P)

    def din(name, shape, dt=F32):
        P.dram.add(name)
        return nc.dram_tensor(name, list(shape), dt, kind="ExternalInput").ap()

    x_d = din("x", [T, D_MODEL])
    p_d = din("p", [T, D_PLE])
    g_mix_d = din("g_mix", [D_MODEL])
    w_in_d = din("w_in", [D_MODEL, IN_COLS])
    q_g_d = din("q_norm_g", [64])
    kc_g_d = din("kc_norm_g", [64])
    ks_g_d = din("ks_norm_g", [64])
    kw_g_d = din("kw_norm_g", [64])
    pos_k_d = din("cmp_pos_k", [32, 64])
    pos_v_d = din("cmp_pos_v", [32, 64])
    cw1_d = [din("cmp_k_w1", [2048, 256]), din("cmp_v_w1", [2048, 256])]
    cb1_d = [din("cmp_k_b1", [256]), din("cmp_v_b1", [256])]
    cw2_d = [din("cmp_k_w2", [256, 64]), din("cmp_v_w2", [256, 64])]
    cb2_d = [din("cmp_k_b2", [64]), din("cmp_v_b2", [64])]
    ln_g_d = din("gmlp_ln_g", [512])
    ln_b_d = din("gmlp_ln_b", [512])
    ws_d = din("gmlp_ws", [8, 128, 128])
    bs_d = din("gmlp_bs", [8, 128])
    og_nsa_d = din("out_g_nsa", [512])
    og_gmlp_d = din("out_g_gmlp", [512])
    w_out_d = din("w_out", [D_MODEL, D_MODEL])
    g_ff_d = din("g_ff", [D_MODEL])
    w_ff1_d = din("w_ff1", [D_MODEL, D_FF])
    w_ff2_d = din("w_ff2", [D_FF, D_MODEL])
    g_ple_d = din("g_ple", [D_MODEL])
    w_pg_d = din("w_ple_gate", [D_MODEL, D_MODEL])
    w_ple_d = din("w_ple", [D_PLE, D_MODEL])
    c_ident_d = din("c_ident", [128, 128], BF)
    c_E_d = din("c_E", [64, T], BF)
    c_D_d = din("c_D", [64, T], BF)
    c_trile_d = din("c_trile", [128, 128], BF)
    c_trigt_d = din("c_trigt", [128, 128], BF)
    c_G_d = din("c_G", [128, WG])
    c_Fw_d = din("c_Fw", [128, W2])
    c_Uw_d = din("c_Uw", [128, W2])
    c_wb_d = din("c_wb", [128, 8])
    P.dram.add("x1s")
    x1_d = nc.dram_tensor("x1s", [T, D_MODEL], F32, kind="Internal").ap()
    P.dram.add("out")
    out_d = nc.dram_tensor("out", [T, D_MODEL], F32, kind="ExternalOutput").ap()

    ES = contextlib.ExitStack()

    def ck(stage, aps):
        if stop != stage:
            return
        for i, a in enumerate(aps):
            n = a.shape[-1] if len(a.shape) == 2 else None
            d = stg[i % 2]
            k.cp("dve", d[0:a.shape[0], 0:n], a)
            k.dma(out_d[i * 128:i * 128 + a.shape[0], 0:n], d[0:a.shape[0], 0:n])
        raise _Stop()

    cur = [ES]

    def sb(name, shape, dt=F32, st=None):
        return (st or cur[0]).enter_context(nc.sbuf_tensor(name, list(shape), dt))

    def col(d_ap, n):
        return d_ap.rearrange("(p o) -> p o", o=1)

    try:
      with ES:
          ps = [ES.enter_context(nc.psum_tensor("ps%d" % i, [128, 512], F32)) for i in range(8)]
          for i in range(8):
              P.psum.add("ps%d" % i)
          psb = [t[:].bitcast(BF) for t in ps]

          ident = sb("ident", [128, 128], BF)
          k.dma(ident[:], c_ident_d)
          trile = sb("trile", [128, 128], BF)
          k.dma(trile[:], c_trile_d)
          trigt = sb("trigt", [128, 128], BF)
          k.dma(trigt[:], c_trigt_d)
          wb = sb("wb", [128, 8])
          k.dma(wb[:], c_wb_d)
          gcols = sb("gcols", [128, 4, 8])
          k.dma(gcols[:, 0, :], g_mix_d.rearrange("(k p) -> p k", p=128), slow=True)
          k.dma(gcols[:, 1, :], g_ff_d.rearrange("(k p) -> p k", p=128), slow=True)
          k.dma(gcols[:, 2, :], g_ple_d.rearrange("(k p) -> p k", p=128), slow=True)
          k.dma(gcols[:, 3, 0:4], og_nsa_d.rearrange("(k p) -> p k", p=128), slow=True)
          k.dma(gcols[:, 3, 4:8], og_gmlp_d.rearrange("(k p) -> p k", p=128), slow=True)
          eps_c = sb("eps_c", [128, 1]); k.memset("dve", eps_c[:], EPS)
          stg = [sb("stg%d" % i, [128, 2048]) for i in range(2)]
          cnt = [0]

          def load_w(dst_fn, src, nk, ncols, gcol=None, segs=None, stages=None):
              if segs is None:
                  segs = [(0, ncols, 0)]
              if stages is None:
                  stages = stg
              for kk in range(nk):
                  for c0 in range(0, ncols, 2048):
                      c1 = min(ncols, c0 + 2048)
                      s = stages[cnt[0] % len(stages)]
                      cnt[0] += 1
                      k.dma(s[:, 0:c1 - c0], src[kk * 128:(kk + 1) * 128, c0:c1])
                      for (a0, a1, d0) in segs:
                          lo, hi = max(a0, c0), min(a1, c1)
                          if lo >= hi:
                              continue
                          dst = dst_fn(kk, d0 + lo - a0, d0 + hi - a0)
                          if gcol is None:
                              k.cp(("act", "dve")[cnt[0] % 2], dst, s[:, lo - c0:hi - c0])
                          else:
                              k.act(dst, s[:, lo - c0:hi - c0], AF.Copy, scale=gcol[:, kk:kk + 1])

          SATT = contextlib.ExitStack()
          cur[0] = SATT
          Gw = sb("Gw", [128, WG])
          k.dma(Gw[:], c_G_d)
          Fw = sb("Fw", [128, W2])
          k.dma(Fw[:], c_Fw_d)
          Uw = sb("Uw", [128, W2])
          k.dma(Uw[:], c_Uw_d)
          Dt = sb("Dt", [128, T], BF)
          k.dma(Dt[64:128, :], c_D_d)
          KEs = [sb("KEs%d" % g, [128, T], BF) for g in range(2)]
          KEw = [sb("KEw%d" % g, [128, T], BF) for g in range(2)]
          for t_ in KEs + KEw:
              k.dma(t_[64:128, :], c_E_d)
          Vall = sb("Vall", [128, NT, 4, 66], BF)
          k.memset("dve", Vall[:], 1.0)
          KcT = [sb("KcT%d" % g, [64, NCT * 128], BF) for g in range(2)]
          Vc = [sb("Vc%d" % g, [128, NCT, 64], BF) for g in range(2)]
          gates = sb("gates", [128, NT, 24])
          rstd_g = sb("rstd_g", [128, NT])
          gq = sb("gq", [64, 1]); k.dma(gq[:], col(q_g_d, 64))
          gks8 = sb("gks8", [64, 1]); k.dma(gks8[:], col(ks_g_d, 64))
          gkw8 = sb("gkw8", [64, 1]); k.dma(gkw8[:], col(kw_g_d, 64))
          k.ts("dve", gks8[:], gks8[:], 8.0, None, ALU.mult)
          k.ts("dve", gkw8[:], gkw8[:], 8.0, None, ALU.mult)
          lng = sb("lng", [128, 512]); k.dma(lng[:], ln_g_d.partition_broadcast(128))
          lnb = sb("lnb", [128, 512]); k.dma(lnb[:], ln_b_d.partition_broadcast(128))
          bstab = sb("bstab", [128, 8]); k.dma(bstab[:], bs_d.rearrange("g t -> t g"), slow=True)

          def front(xt, hT, gidx_unused, junk, ssum, rs, hb, tsl):
              k.act(junk[:, 0:1024], xt, AF.Square, accum=ssum[:])
              k.rsq(rs[:], ssum[:], float(D_MODEL * EPS))
              k.ts("dve", hb[:], xt, rs[:, 0:1], 32.0, ALU.mult, ALU.mult)
              for kk in range(8):
                  k.tr(psb[4][:, kk * 128:(kk + 1) * 128], hb[:, kk * 128:(kk + 1) * 128], ident[:])
              k.cp("act", hT[:, :, tsl], psb[4][:, 0:1024].rearrange("p (k t) -> p k t", k=8))

          def gelu_to(out, src_ps, n, junk_a, junk_b, accum=None, p0=0, p1=128, bias=None):
              xv = junk_a[p0:p1, 0:n]
              sq = junk_b[p0:p1, 0:n]
              if bias is None:
                  k.act(xv, src_ps, AF.Copy)
                  k.act(sq, src_ps, AF.Square)
              else:
                  k.ts("dve", xv, src_ps, bias, None, ALU.add)
                  k.act(sq, src_ps, AF.Square, bias=bias)
              k.ts("dve", sq, sq, 0.044715, 1.0, ALU.mult, ALU.add)
              k.tt("pool", sq, sq, xv, ALU.mult)
              k.act(sq, sq, AF.Exp, scale=-1.5957691216)
              k.ts("dve", sq, sq, 1.0, None, ALU.add)
              k.recip(sq, sq)
              if accum is None:
                  k.tt("dve", out, sq, xv, ALU.mult)
              else:
                  P.add("dve", lambda e: e.scalar_tensor_tensor(out, sq, 1.0, xv, ALU.mult, ALU.mult, accum_out=accum),
                        [out, accum], [sq, xv])

          SA = contextlib.ExitStack()
          with SA:
              w_inA = sb("w_inA", [128, 8, 768], BF, SA)
              kvT = sb("kvT", [128, 2, T], BF, SA)
              segsA = [(0, 384, 0), (384, 512, 512), (512, 640, 384), (640, 768, 640)]
              load_w(lambda kk, a, b: w_inA[:, kk, a:b], w_in_d[:, 512:1280], 8, 768, gcols[:, 0, :], segsA)
              ck(1, [w_inA[:, 0, 0:512], w_inA[:, 7, 256:768]])
              xtA = [sb("xtA%d" % i, [128, 1024], F32, SA) for i in range(2)]
              hTA = [sb("hTA%d" % i, [128, 8, 128], BF, SA) for i in range(2)]
              junkA = sb("junkA", [128, 1024], F32, SA)
              hbA = sb("hbA", [128, 1024], BF, SA)
              ssA = sb("ssA", [128, 1], F32, SA)
              rsA = sb("rsA", [128, 1], F32, SA)
              ssk = sb("ssk", [128, 4], F32, SA)
              zbA = sb("zbA", [128, 512], BF, SA)
              def front_a(tt_):
                  xt = xtA[tt_ % 2]
                  hT = hTA[tt_ % 2]
                  k.dma(xt[:], x_d[tt_ * 128:(tt_ + 1) * 128, :])
                  front(xt[:], hT, 0, junkA, ssA, rsA, hbA, slice(0, 128))
                  b0, b1 = ps[2 * (tt_ % 2)], ps[2 * (tt_ % 2) + 1]
                  for kk in range(8):
                      k.mm(b0[:, 0:512], hT[:, kk, :], w_inA[:, kk, 0:512], start=(kk == 0), stop=(kk == 7))
                      k.mm(b1[:, 0:256], hT[:, kk, :], w_inA[:, kk, 512:768], start=(kk == 0), stop=(kk == 7))

              def post_a(tt_):
                  b0, b1 = ps[2 * (tt_ % 2)], ps[2 * (tt_ % 2) + 1]
                  k.act(junkA[:, 0:256], b0[:, 256:512], AF.Square)
                  k.red(ssk[:], junkA[:, 0:256].rearrange("p (a d) -> p a d", d=64))
                  k.rsq(ssk[:], ssk[:], 64.0 * EPS)
                  k.cp("act", zbA[:, 0:256], b0[:, 0:256])
                  k.tt("dve", zbA[:, 256:512].rearrange("p (a d) -> p a d", d=64),
                       b0[:, 256:512].rearrange("p (a d) -> p a d", d=64),
                       ssk[:].unsqueeze(2).to_broadcast([128, 4, 64]), ALU.mult)
                  k.cp("act", Vall[:, tt_, :, 0:64], b1[:, 0:256].rearrange("p (a d) -> p a d", d=64))
                  k.tr(psb[5][:, 0:128], zbA[:, 0:128], ident[:])
                  k.tr(psb[5][:, 128:256], zbA[:, 128:256], ident[:])
                  for a in range(4):
                      k.tr(psb[5][0:64, 256 + a * 128:384 + a * 128], zbA[:, 256 + a * 64:320 + a * 64], ident[:])
                  tsl = slice(tt_ * 128, (tt_ + 1) * 128)
                  k.cp("dve", kvT[:, :, tsl], psb[5][:, 0:256].rearrange("p (a t) -> p a t", a=2))
                  for g in range(2):
                      k.ts("dve", KEs[g][0:64, tsl], psb[5][0:64, 256 + g * 128:384 + g * 128], gks8[:, 0:1], None, ALU.mult)
                      k.ts("dve", KEw[g][0:64, tsl], psb[5][0:64, 512 + g * 128:640 + g * 128], gkw8[:, 0:1], None, ALU.mult)

              front_a(0)
              for tt_ in range(NT):
                  if tt_ + 1 < NT:
                      front_a(tt_ + 1)
                  post_a(tt_)

              ck(2, [KEs[0][:, 0:1024], kvT[:, 0, 0:1024], KEw[1][:, 0:1024], Vall[:, 3, :, :].rearrange("p a d -> p (a d)")])
              SC = contextlib.ExitStack()
              with SC:
                  w1r = sb("w1r", [128, 32, 256], BF, SC)
                  w2b = sb("w2b", [128, 2, 64], BF, SC)
                  posT = sb("posT", [128, 32], BF, SC)
                  posf = sb("posf", [128, 32], F32, SC)
                  b1c = sb("b1c", [128, 2], F32, SC)
                  bias1 = sb("bias1", [128, 2], F32, SC)
                  b2t = sb("b2t", [128, 64], F32, SC)
                  gkc = sb("gkc", [128, 64], F32, SC)
                  hdn = sb("hdn", [128, 2, NCT * 128], BF, SC)
                  ja = sb("cja", [128, 256], F32, SC)
                  jb = sb("cjb", [128, 256], F32, SC)
                  kcf = sb("kcf", [128, 64], F32, SC)
                  kcb = sb("kcb", [128, 64], BF, SC)
                  ssc = sb("ssc", [128, 1], F32, SC)
                  k.dma(gkc[:], kc_g_d.partition_broadcast(128))
                  k.memset("dve", hdn[:], 0.0)
                  for kv in range(2):
                      w1v = cw1_d[kv].rearrange("(l d) h -> d l h", d=64)
                      for half in range(2):
                          for lc in range(4):
                              s = stg[cnt[0] % 2]
                              cnt[0] += 1
                              k.dma(s[half * 64:(half + 1) * 64, :].rearrange("p (l h) -> p l h", h=256),
                                    w1v[:, lc * 8:(lc + 1) * 8, :])
                              k.cp(("pool", "dve")[lc % 2],
                                   w1r[half * 64:(half + 1) * 64, lc * 8:(lc + 1) * 8, :],
                                   s[half * 64:(half + 1) * 64, :].rearrange("p (l h) -> p l h", h=256))
                      s = stg[cnt[0] % 2]
                      cnt[0] += 1
                      k.dma(s[:, 0:128].rearrange("p (c o) -> p c o", c=2), cw2_d[kv].rearrange("(c p) o -> p c o", p=128))
                      k.cp("dve", w2b[:], s[:, 0:128].rearrange("p (c o) -> p c o", c=2))
                      pos_d = (pos_k_d, pos_v_d)[kv]
                      for half in range(2):
                          k.dma(posf[half * 64:(half + 1) * 64, :], pos_d.rearrange("l d -> d l"), slow=True)
                      k.cp("dve", posT[:], posf[:])
                      k.dma(b1c[:], cb1_d[kv].rearrange("(c p) -> p c", p=128), slow=True)
                      k.dma(b2t[:], cb2_d[kv].partition_broadcast(128))
                      for hh in range(2):
                          for l in range(32):
                              k.mm(ps[6][:, hh:hh + 1], w1r[0:64, l, hh * 128:(hh + 1) * 128], posT[0:64, l:l + 1],
                                   start=(l == 0), stop=(l == 31))
                      k.tt("dve", bias1[:], ps[6][:, 0:2], b1c[:], ALU.add)
                      for g in range(2):
                          pr = slice(g * 64, (g + 1) * 64)
                          for hh in range(2):
                              for l in range(32):
                                  k.mm(ps[hh][:, 0:NC], w1r[pr, l, hh * 128:(hh + 1) * 128],
                                       kvT[pr, kv, l:l + 16 * (NC - 1) + 1:16], start=(l == 0), stop=(l == 31))
                          for hh in range(2):
                              for c0 in range(0, NC, 256):
                                  c1 = min(NC, c0 + 256)
                                  gelu_to(hdn[:, hh, c0:c1], ps[hh][:, c0:c1], c1 - c0, ja, jb, bias=bias1[:, hh:hh + 1])
                          for ct in range(NCT):
                              ncv = min(128, NC - ct * 128)
                              for hh in range(2):
                                  k.mm(ps[2][0:ncv, 0:64], hdn[:, hh, ct * 128:ct * 128 + ncv], w2b[:, hh, :],
                                       start=(hh == 0), stop=(hh == 1))
                              k.tt("dve", kcf[0:ncv, :], ps[2][0:ncv, 0:64], b2t[0:ncv, :], ALU.add)
                              if kv == 1:
                                  if ncv < 128:
                                      k.memset("dve", Vc[g][:, ct, :], 0.0)
                                  k.cp("dve", Vc[g][0:ncv, ct, :], kcf[0:ncv, :])
                              else:
                                  k.act(ja[0:ncv, 0:64], kcf[0:ncv, :], AF.Square, accum=ssc[0:ncv, :])
                                  k.rsq(ssc[0:ncv, :], ssc[0:ncv, :], 64.0 * EPS)
                                  k.ts("dve", kcf[0:ncv, :], kcf[0:ncv, :], ssc[0:ncv, 0:1], 8.0, ALU.mult, ALU.mult)
                                  if ncv < 128:
                                      k.memset("dve", kcb[:], 0.0)
                                  k.tt("dve", kcb[0:ncv, :], kcf[0:ncv, :], gkc[0:ncv, :], ALU.mult)
                                  k.tr(psb[5][0:64, 0:128], kcb[:, :], ident[:])
                                  k.cp("dve", KcT[g][:, ct * 128:(ct + 1) * 128], psb[5][0:64, 0:128])

          P.barrier()
          ck(3, [KcT[0][:, 0:128], KcT[1][:, 0:128], Vc[0][:, 0, :], Vc[1][:, 0, :]])
          SB_ = contextlib.ExitStack()
          with SB_:
              w_inB = sb("w_inB", [128, 8, 1560], BF, SB_)
              w_outb = sb("w_outb", [128, 8, 1024], BF, SB_)
              xs = sb("xs", [128, 4, 1024], F32, SB_)
              xsf = xs[:].rearrange("p a b -> p (a b)")
              st4b = [stg[0], stg[1], xsf[:, 0:2048], xsf[:, 2048:4096]]
              load_w(lambda kk, a, b: w_inB[:, kk, a:b], w_in_d[:, 0:512], 8, 512, gcols[:, 0, :], [(0, 512, 0)], stages=st4b)
              load_w(lambda kk, a, b: w_inB[:, kk, a:b], w_in_d[:, 1280:2328], 8, 1048, gcols[:, 0, :],
                     [(0, 24, 1536), (24, 1048, 512)], stages=st4b)
              load_w(lambda kk, a, b: w_outb[:, kk, a:b], w_out_d, 4, 1024, gcols[:, 3, 0:4], stages=st4b)
              load_w(lambda kk, a, b: w_outb[:, 4 + kk, a:b], w_out_d[512:1024, :], 4, 1024, gcols[:, 3, 4:8], stages=st4b)
              WmT = sb("WmT", [128, 8, 128], BF, SB_)
              wsf = sb("wsf", [128, 128], F32, SB_)
              wsb = sb("wsb", [128, 128], BF, SB_)
              hTB = sb("hTB", [128, 8, 128], BF, SB_)
              junkB = stg[0][:, 1024:2048]
              junkC = stg[1][:, 1024:2048]
              hbB = sb("hbB", [128, 1024], BF, SB_)
              ssB = sb("ssB", [128, 1], F32, SB_)
              rsB = sb("rsB", [128, 1], F32, SB_)
              ssq = sb("ssq", [128, 8], F32, SB_)
              qnb = sb("qnb", [128, 512], BF, SB_)
              R = sb("R", [128, 8, 512], BF, SB_)
              u_sb = sb("u_sb", [128, 512], F32, SB_)
              v_sb = sb("v_sb", [128, 512], F32, SB_)
              vnb = sb("vnb", [128, 512], BF, SB_)
              st1 = sb("st1", [128, 4], F32, SB_)
              gof = v_sb
              gob = sb("gob", [128, 512], BF, SB_)
              goT = sb("goT", [128, 4, 512], BF, SB_)
              aoT = sb("aoT", [128, 4, 128], BF, SB_)
              ao = sb("ao", [128, 4, 512], F32, SB_)
              aob = qnb
              rstd_a = sb("rstd_a", [128, 1], F32, SB_)
              sc = sb("sc", [128, 2, 256], F32, SB_)
              ef = sb("ef", [128, 2, 8, 256], BF, SB_)
              ebT = sb("ebT", [128, 2, NCT, 128], BF, SB_)
              csum = sb("csum", [128, 2, 8], F32, SB_)
              crin = sb("crin", [128, 2, 8], F32, SB_)
              icp = sb("icp", [128, 2, NCP + 1], F32, SB_)
              imp = sb("imp", [128, 2, 64], F32, SB_)
              impw = sb("impw", [128, 2, 64], F32, SB_)
              m8 = sb("m8", [128, 2, 16], F32, SB_)
              bm = sb("bm", [128, 2, 128], BF, SB_)
              mbT = sb("mbT", [128, 2, 512], F32, SB_)
              ocs = sb("ocs", [128, 4, 8, 64], F32, SB_)
              pti = [0]
              PT = [sb("PT%d" % i, [128, 512], BF, SB_) for i in range(3)]
              fsum = sb("fsum", [128, 2, 4], F32, SB_)
              coef = sb("coef", [128, 3, 4], F32, SB_)
              tmpo = u_sb[:, 0:256].rearrange("p (a d) -> p a d", d=64)
              x1t = [stg[0][:, 0:1024], stg[1][:, 0:1024]]
              k.memset("dve", icp[:], 0.0)
              k.memset("dve", bm[:], 0.0)
              k.memset("dve", ef[:], 0.0)
              for g in range(8):
                  k.dma(wsf[:], ws_d[g])
                  k.tr(psb[5][:, 0:128], trile[:], ident[:])
                  k.tt("dve", wsb[:], wsf[:], psb[5][:, 0:128], ALU.mult)
                  k.tr(psb[5][:, 128:256], wsb[:], ident[:])
                  k.cp("dve", WmT[:, g, :], psb[5][:, 128:256])

              for Q in range(NQ):
                  ju_a, ju_b = stg[0][:, 0:512], stg[0][:, 512:1024]
                  jv_a, jv_b = stg[0][:, 1024:1536], stg[0][:, 1536:2048]
                  fjunk, qsq, dummy = stg[1][:, 0:1024], stg[1][:, 1024:1536], stg[1][:, 1536:2048]

                  def front_b(qs):
                      tt_ = Q * 4 + qs
                      xt = xs[:, qs, :]
                      k.dma(xt, x_d[tt_ * 128:(tt_ + 1) * 128, :])
                      front(xt, hTB, 0, fjunk, ssB, rsB, hbB, slice(0, 128))
                      for kk in range(8):
                          for cb in range(3):
                              k.mm(ps[cb][:, 0:512], hTB[:, kk, :], w_inB[:, kk, cb * 512:(cb + 1) * 512],
                                   start=(kk == 0), stop=(kk == 7))
                          k.mm(ps[3][:, 0:24], hTB[:, kk, :], w_inB[:, kk, 1536:1560], start=(kk == 0), stop=(kk == 7))

                  def post1_b(qs):
                      tt_ = Q * 4 + qs
                      k.act(qsq, ps[0][:, 0:512], AF.Square)
                      k.red(ssq[:], qsq.rearrange("p (a d) -> p a d", d=64))
                      k.rsq(ssq[:], ssq[:], 64.0 * EPS)
                      k.tt("dve", qnb[:].rearrange("p (a d) -> p a d", d=64),
                           ps[0][:, 0:512].rearrange("p (a d) -> p a d", d=64),
                           ssq[:].unsqueeze(2).to_broadcast([128, 8, 64]), ALU.mult)
                      k.act(gates[:, tt_, :], ps[3][:, 0:24], AF.Exp, scale=-1.0)
                      k.act(ju_a, ps[1][:, 0:512], AF.Copy)
                      k.act(ju_b, ps[1][:, 0:512], AF.Square)
                      k.act(jv_a, ps[2][:, 0:512], AF.Copy)
                      k.act(jv_b, ps[2][:, 0:512], AF.Square)

                  def post2_b(qs):
                      tt_ = Q * 4 + qs
                      k.ts("dve", gates[:, tt_, :], gates[:, tt_, :], 1.0, None, ALU.add)
                      k.recip(gates[:, tt_, :], gates[:, tt_, :])
                      prs = ((ju_a, ju_b), (jv_a, jv_b))
                      for xv, sq in prs:
                          k.ts("dve", sq, sq, 0.044715, 1.0, ALU.mult, ALU.add)
                      for xv, sq in prs:
                          k.tt("dve", sq, sq, xv, ALU.mult)
                      for xv, sq in prs:
                          k.act(sq, sq, AF.Exp, scale=-1.5957691216)
                      for xv, sq in prs:
                          k.act(sq, sq, AF.Ln, bias=1.0)
                      for xv, sq in prs:
                          k.act(sq, sq, AF.Exp, scale=-1.0)
                      k.tt("pool", u_sb[:], ju_b, ju_a, ALU.mult)
                      P.add("dve", lambda e: e.scalar_tensor_tensor(v_sb[:], jv_b, 1.0, jv_a, ALU.mult, ALU.mult, accum_out=st1[:, 0:1]),
                            [v_sb[:], st1[:, 0:1]], [jv_b, jv_a])
                      k.act(dummy, v_sb[:], AF.Square, accum=st1[:, 1:2])
                      k.ts("dve", st1[:, 2:3], st1[:, 0:1], 1.0 / 512, None, ALU.mult)
                      k.stt("dve", st1[:, 3:4], st1[:, 2:3], -1.0, st1[:, 2:3], ALU.mult, ALU.mult)
                      k.stt("dve", st1[:, 3:4], st1[:, 1:2], 1.0 / 512, st1[:, 3:4], ALU.mult, ALU.add)
                      k.rsq(st1[:, 3:4], st1[:, 3:4], EPS)
                      k.ts("dve", v_sb[:], v_sb[:], st1[:, 2:3], st1[:, 3:4], ALU.subtract, ALU.mult)
                      k.tt("dve", v_sb[:], v_sb[:], lng[:], ALU.mult)
                      k.tt("dve", vnb[:], v_sb[:], lnb[:], ALU.add)
                      for g in range(8):
                          k.mm(ps[6][:, g * 64:(g + 1) * 64], WmT[:, g, :], vnb[:, g * 64:(g + 1) * 64])
                      k.tt("dve", gof[:].rearrange("p (a d) -> p a d", d=64),
                           ps[6][:, 0:512].rearrange("p (a d) -> p a d", d=64),
                           bstab[:].unsqueeze(2).to_broadcast([128, 8, 64]), ALU.add)
                      k.tt("pool", gob[:], gof[:], u_sb[:], ALU.mult)
                      k.act(dummy, gob[:], AF.Square, accum=rstd_g[:, tt_:tt_ + 1])
                      k.rsq(rstd_g[:, tt_:tt_ + 1], rstd_g[:, tt_:tt_ + 1], 512.0 * EPS)
                      k.ts("dve", rstd_g[:, tt_:tt_ + 1], rstd_g[:, tt_:tt_ + 1], 22.627416998, None, ALU.mult)
                      for c4 in range(4):
                          k.tr(psb[7][:, c4 * 128:(c4 + 1) * 128], gob[:, c4 * 128:(c4 + 1) * 128], ident[:])
                      k.cp("act", goT[:, :, qs * 128:(qs + 1) * 128], psb[7][:, 0:512].rearrange("p (c t) -> p c t", c=4))
                      for h in range(8):
                          k.tr(psb[5][0:64, h * 128:(h + 1) * 128], qnb[:, h * 64:(h + 1) * 64], ident[:])
                      k.ts("dve", R[0:64, :, qs * 128:(qs + 1) * 128],
                           psb[5][0:64, 0:1024].rearrange("p (h t) -> p h t", h=8), gq[:, 0:1], None, ALU.mult)

                  front_b(0)
                  for qs in range(4):
                      post1_b(qs)
                      if qs + 1 < 4:
                          front_b(qs + 1)
                      post2_b(qs)

                  ck(4, [R[:, 0, :], R[:, 7, :], gof[:], goT[:, 0, :], gates[:, 0:4, :].rearrange("p a d -> p (a d)")])

                  def make_sel(qs):
                      qt = Q * 4 + qs
                      qp = qs % 2
                      ocb = ps[2 + qp]
                      th = []
                      th.append(lambda: k.ts("dve", csum[:, qp, :], csum[:, qp, :], 1e-30, None, ALU.max))
                      th.append(lambda: k.recip(crin[:, qp, :], csum[:, qp, :]))
                      th.append(lambda: k.tt("dve", ocs[:, qs, :, :], ocb[:, 0:512].rearrange("p (h d) -> p h d", h=8),
                                             crin[:, qp, :].unsqueeze(2).to_broadcast([128, 8, 64]), ALU.mult))
                      span = 4 * (NB - 1) + 1
                      for g in range(2):
                          th.append(lambda g=g: k.ts("dve", icp[:, g, 1:1 + NC], ef[:, qp, 4 * g, 0:NC],
                                                     crin[:, qp, 4 * g:4 * g + 1], None, ALU.mult))
                          for r in range(1, 4):
                              th.append(lambda g=g, h=4 * g + r: k.stt("dve", icp[:, g, 1:1 + NC], ef[:, qp, h, 0:NC],
                                                                       crin[:, qp, h:h + 1], icp[:, g, 1:1 + NC], ALU.mult, ALU.add))
                          th.append(lambda g=g: k.cp("dve", imp[:, g, 0:NB], icp[:, g, 0:span:4]))
                          for kk_, wk in ((1, 2.0), (2, 2.0), (3, 2.0), (4, 1.0)):
                              th.append(lambda g=g, kk_=kk_, wk=wk: k.stt("dve", imp[:, g, 0:NB], icp[:, g, kk_:kk_ + span:4], wk,
                                                                          imp[:, g, 0:NB], ALU.mult, ALU.add))
                      if NB < 64:
                          th.append(lambda: k.memset("dve", imp[:, :, NB:64], -1.0))
                      f0 = 2 * (NT - 1) - 2 * qt
                      for g in range(2):
                          th.append(lambda g=g: k.tt("dve", imp[:, g, 0:NB], imp[:, g, 0:NB], Fw[:, f0:f0 + NB], ALU.max))
                          th.append(lambda g=g: k.tt("dve", imp[:, g, 0:NB], imp[:, g, 0:NB], Uw[:, f0:f0 + NB], ALU.min))
                      th.append(lambda: k.memset("dve", imp[:, :, 0:1], 3.0e4))
                      for g in range(2):
                          th.append(lambda g=g: k.max8(m8[:, g, 0:8], imp[:, g, :]))
                          th.append(lambda g=g: k.mrep(impw[:, g, :], m8[:, g, 0:8], imp[:, g, :], -2.0))
                          th.append(lambda g=g: k.max8(m8[:, g, 8:16], impw[:, g, :]))
                          th.append(lambda g=g: k.ts("dve", impw[:, g, :], imp[:, g, :], m8[:, g, 15:16], None, ALU.is_ge))
                          th.append(lambda g=g: k.ts("dve", bm[:, g, 64:128], impw[:, g, :], -NEG, NEG, ALU.mult, ALU.add))
                          th.append(lambda g=g: k.tr(psb[5][:, g * 128:(g + 1) * 128], bm[:, g, :], ident[:]))
                          th.append(lambda g=g: k.cp("dve", mbT[64:128, g, qs * 128:(qs + 1) * 128],
                                                     psb[5][64:128, g * 128:(g + 1) * 128]))
                      return th

                  def heads(qs, pending):
                      qt = Q * 4 + qs
                      qp = qs % 2
                      nctq = min(NCT, (8 * qt + 7 + 127) // 128)
                      gs0 = 8 * (NT - 1) - 8 * qt
                      ocb = ps[2 + qp]

                      def stage_a(h):
                          g = h // 4
                          sbk = ps[0] if h % 2 == 0 else ps[6]
                          k.mm(sbk[:, 0:NC], R[0:64, h, qs * 128:(qs + 1) * 128], KcT[g][:, 0:NC])
                          k.stt("dve", sc[:, h % 2, 0:NC], Gw[:, gs0:gs0 + NC], SLOPES[h], sbk[:, 0:NC], ALU.mult, ALU.add)
                          k.act(ef[:, qp, h, 0:NC], sc[:, h % 2, 0:NC], AF.Exp, accum=csum[:, qp, h:h + 1])

                      def stage_b(h):
                          g = h // 4
                          tb = psb[1] if h % 2 == 0 else psb[7]
                          for ct in range(nctq):
                              k.tr(tb[:, ct * 128:(ct + 1) * 128], ef[:, qp, h, ct * 128:(ct + 1) * 128], ident[:])
                          k.cp("act", ebT[:, h % 2, 0:nctq, :], tb[:, 0:nctq * 128].rearrange("p (c t) -> p c t", c=nctq))
                          for ct in range(nctq):
                              k.mm(ocb[:, h * 64:(h + 1) * 64], ebT[:, h % 2, ct, :], Vc[g][:, ct, :],
                                   start=(ct == 0), stop=(ct == nctq - 1))

                      stage_a(0)
                      for h in range(8):
                          if h + 1 < 8:
                              stage_a(h + 1)
                          stage_b(h)
                          nd = (len(pending) + (7 - h)) // (8 - h)
                          for _ in range(nd):
                              pending.pop(0)()

                  pending = []
                  for qs in range(4):
                      heads(qs, pending)
                      assert not pending
                      pending = make_sel(qs)
                  for th_ in pending:
                      th_()

                  ck(5, [mbT[:, 0, :], mbT[:, 1, :], ocs[:, 3, :, :].rearrange("p a d -> p (a d)"), imp[:, 0, :]])
                  qcols = slice(Q * 512, (Q + 1) * 512)
                  gsl = slice(4 * Q, 4 * Q + 4)
                  tasks = []
                  for h in range(8):
                      for br in range(2):
                          kt_lo = max(0, 4 * Q - 4) if br == 0 else 0
                          kts = list(range(kt_lo, 4 * Q + 4))
                          for kt in kts:
                              tasks.append(dict(h=h, br=br, kt=kt, gfirst=(kt == kts[0]), glast=(kt == kts[-1]), n=len(tasks)))

                  def obank(h, br):
                      return (ps[6 + br] if h % 2 == 0 else ps[br])

                  def emit_S(t):
                      h, br, kt = t["h"], t["br"], t["kt"]
                      g = h // 4
                      if t["gfirst"]:
                          if br == 0:
                              k.ts("dve", R[64:128, h, :], Dt[64:128, qcols], SLOPES[h], None, ALU.mult)
                          else:
                              k.tt("dve", R[64:128, h, :], R[64:128, h, :], mbT[64:128, g, :], ALU.add)
                      i = kt - 4 * Q
                      c_lo = max(0, i)
                      c_hi = min(3, i + 4) if br == 0 else 3
                      t["c"] = (c_lo, c_hi)
                      n0, n1 = c_lo * 128, (c_hi + 1) * 128
                      KE = (KEw, KEs)[br][g]
                      sbank = (ps[3], ps[4], ps[5])[t["n"] % 3]
                      k.mm(sbank[:, n0:n1], KE[:, kt * 128:(kt + 1) * 128], R[:, h, n0:n1])

                  def emit_rest(t):
                      h, br, kt = t["h"], t["br"], t["kt"]
                      g = h // 4
                      i = kt - 4 * Q
                      c_lo, c_hi = t["c"]
                      n0, n1 = c_lo * 128, (c_hi + 1) * 128
                      sbank = (ps[3], ps[4], ps[5])[t["n"] % 3]
                      pt = PT[t["n"] % len(PT)]
                      Ob = obank(h, br)
                      vidx = (2 + g, g)[br]
                      k.act(pt[:, n0:n1], sbank[:, n0:n1], AF.Exp, bias=wb[:, h:h + 1])
                      if i >= 0:
                          k.tt("dve", pt[:, i * 128:(i + 1) * 128], pt[:, i * 128:(i + 1) * 128], trile[:], ALU.mult)
                      if br == 0 and 0 <= i + 4 <= 3:
                          cc = i + 4
                          k.tt("dve", pt[:, cc * 128:(cc + 1) * 128], pt[:, cc * 128:(cc + 1) * 128], trigt[:], ALU.mult)
                      for c in range(c_lo, c_hi + 1):
                          k.mm(Ob[:, c * 65:(c + 1) * 65], pt[:, c * 128:(c + 1) * 128], Vall[:, kt, vidx, 0:65],
                               start=(t["gfirst"] and c == c_lo), stop=(kt == 4 * Q + c), skip=True)
                      if br == 1 and t["glast"]:
                          Ow = obank(h, 0)[:, 0:260].rearrange("p (c d) -> p c d", d=65)
                          Os = obank(h, 1)[:, 0:260].rearrange("p (c d) -> p c d", d=65)
                          k.ts("dve", fsum[:, 0, :], Ow[:, :, 64], 1e-30, None, ALU.max)
                          k.ts("dve", fsum[:, 1, :], Os[:, :, 64], 1e-30, None, ALU.max)
                          k.recip(fsum[:], fsum[:])
                          k.tt("dve", coef[:, 0, :], fsum[:, 0, :], gates[:, gsl, 3 * h + 2], ALU.mult)
                          k.tt("dve", coef[:, 1, :], fsum[:, 1, :], gates[:, gsl, 3 * h + 1], ALU.mult)
                          dst = ao[:, :, h * 64:(h + 1) * 64]
                          k.tt("dve", dst, ocs[:, :, h, :], gates[:, gsl, 3 * h:3 * h + 1].to_broadcast([128, 4, 64]), ALU.mult)
                          k.tt("dve", tmpo, Ow[:, :, 0:64], coef[:, 0, :].unsqueeze(2).to_broadcast([128, 4, 64]), ALU.mult)
                          k.tt("dve", dst, dst, tmpo, ALU.add)
                          k.tt("dve", tmpo, Os[:, :, 0:64], coef[:, 1, :].unsqueeze(2).to_broadcast([128, 4, 64]), ALU.mult)
                          k.tt("dve", dst, dst, tmpo, ALU.add)

                  emit_S(tasks[0])
                  if len(tasks) > 1:
                      emit_S(tasks[1])
                  for n_, t in enumerate(tasks):
                      if n_ + 2 < len(tasks):
                          emit_S(tasks[n_ + 2])
                      emit_rest(t)

                  ck(6, [ao[:, 0, :], ao[:, 3, :]])
                  for qs in range(4):
                      tt_ = Q * 4 + qs
                      x1 = x1t[tt_ % 2]
                      k.act(junkB[:, 0:512], ao[:, qs, :], AF.Square, accum=rstd_a[:])
                      k.rsq(rstd_a[:], rstd_a[:], 512.0 * EPS)
                      k.ts("dve", rstd_a[:], rstd_a[:], 22.627416998, None, ALU.mult)
                      k.cp("pool", aob[:], ao[:, qs, :])
                      for c4 in range(4):
                          k.tr(psb[5][:, c4 * 128:(c4 + 1) * 128], aob[:, c4 * 128:(c4 + 1) * 128], ident[:])
                      k.cp("act", aoT[:], psb[5][:, 0:512].rearrange("p (c t) -> p c t", c=4))
                      for half in range(2):
                          hs = slice(half * 512, (half + 1) * 512)
                          for c4 in range(4):
                              k.mm(ps[half][:, :], aoT[:, c4, :], w_outb[:, c4, hs], start=(c4 == 0), stop=(c4 == 3))
                          for c4 in range(4):
                              k.mm(ps[2 + half][:, :], goT[:, c4, qs * 128:(qs + 1) * 128], w_outb[:, 4 + c4, hs],
                                   start=(c4 == 0), stop=(c4 == 3))
                          k.stt("dve", x1[:, hs], ps[half][:, :], rstd_a[:, 0:1], xs[:, qs, hs], ALU.mult, ALU.add)
                          k.stt("dve", x1[:, hs], ps[2 + half][:, :], rstd_g[:, tt_:tt_ + 1], x1[:, hs], ALU.mult, ALU.add)
                      k.dma(x1_d[tt_ * 128:(tt_ + 1) * 128, :], x1)
          SATT.close()
          cur[0] = ES
          P.barrier()

          SC3 = contextlib.ExitStack()
          with SC3:
              w1b = sb("w1b", [128, 8, 4096], BF, SC3)
              w2f = sb("w2f", [128, 32, 1024], BF, SC3)
              wpg = sb("wpg", [128, 8, 1024], BF, SC3)
              wpl = sb("wpl", [128, 2, 1024], BF, SC3)
              fT = sb("fT", [128, 32, 256], BF, SC3)
              fTf = fT[:].rearrange("p a b -> p (a b)").bitcast(F32)
              st4 = [stg[0], stg[1], fTf[:, 0:2048], fTf[:, 2048:4096]]
              load_w(lambda kk, a, b: w1b[:, kk, a:b], w_ff1_d, 8, D_FF, gcols[:, 1, :], stages=st4)
              load_w(lambda kk, a, b: w2f[:, kk, a:b], w_ff2_d, 32, D_MODEL, stages=st4)
              load_w(lambda kk, a, b: wpg[:, kk, a:b], w_pg_d, 8, D_MODEL, gcols[:, 2, :], stages=st4)
              load_w(lambda kk, a, b: wpl[:, kk, a:b], w_ple_d, 2, D_MODEL, stages=st4)
              xc = sb("xc", [128, 2, 1024], F32, SC3)
              x2 = sb("x2", [128, 2, 1024], F32, SC3)
              hTC = sb("hTC", [128, 8, 256], BF, SC3)
              h3T = sb("h3T", [128, 8, 128], BF, SC3)
              junkD = stg[0][:, 0:1024]
              hbC = sb("hbC", [128, 1024], BF, SC3)
              ssC = sb("ssC", [128, 1], F32, SC3)
              rsC = sb("rsC", [128, 1], F32, SC3)
              rls = [stg[1][:, 1024:1280], stg[1][:, 1536:1792]]
              ptf = stg[1][:, 1280:1536]
              ptb = sb("ptb", [128, 256], BF, SC3)
              pT = sb("pT", [128, 2, 128], BF, SC3)
              th = stg[0][:, 1024:1536]
              outt = stg[1][:, 0:1024]
              xcs = [xc, x2]
              th2 = [stg[0][:, 1024:1536], stg[0][:, 1536:2048]]
              NBT = T // 256

              def s1(b):
                  for j in range(2):
                      tt_ = 2 * b + j
                      k.dma(xcs[b % 2][:, j, :], x1_d[tt_ * 128:(tt_ + 1) * 128, :])
                      front(xcs[b % 2][:, j, :], hTC, 0, junkD, ssC, rsC, hbC, slice(j * 128, (j + 1) * 128))

              def drain(pend, slots_left):
                  nd = (len(pend) + slots_left - 1) // max(1, slots_left)
                  for _ in range(min(nd, len(pend))):
                      pend.pop(0)()

              def s2_s3(b, pend):
                  x2c = xcs[b % 2]
                  for fc in range(32):
                      bank = ps[fc % 2]
                      for kk in range(8):
                          k.mm(bank[:, 0:256], w1b[:, kk, fc * 128:(fc + 1) * 128], hTC[:, kk, :], start=(kk == 0), stop=(kk == 7))
                      rl = rls[fc % 2]
                      k.act(rl, bank[:, 0:256], AF.Relu)
                      k.tt(("pool", "dve")[fc % 2], fT[:, fc, :], rl, rl, ALU.mult)
                      drain(pend, 36 - fc)
                  for gi in range(4):
                      j, half = gi // 2, gi % 2
                      hs = slice(half * 512, (half + 1) * 512)
                      bank = ps[2 + gi % 2]
                      for fc in range(32):
                          k.mm(bank[:, :], fT[:, fc, j * 128:(j + 1) * 128], w2f[:, fc, hs], start=(fc == 0), stop=(fc == 31))
                      k.tt("dve", x2c[:, j, hs], bank[:, :], x2c[:, j, hs], ALU.add)
                      drain(pend, 4 - gi)

              def ple_thunks(b, j):
                  tt_ = 2 * b + j
                  xin = xcs[b % 2][:, j, :]
                  tl = []

                  def f_a():
                      k.act(junkD[:, 0:1024], xin, AF.Square, accum=ssC[:])
                      k.rsq(rsC[:], ssC[:], float(D_MODEL * EPS))
                      k.ts("dve", hbC[:], xin, rsC[:, 0:1], 32.0, ALU.mult, ALU.mult)
                      for kk in range(8):
                          k.tr(psb[4][:, kk * 128:(kk + 1) * 128], hbC[:, kk * 128:(kk + 1) * 128], ident[:])

                  def f_p():
                      k.dma(ptf, p_d[tt_ * 128:(tt_ + 1) * 128, :])
                      k.cp("pool", ptb[:], ptf)

                  def f_ptr():
                      for c2 in range(2):
                          k.tr(psb[5][:, c2 * 128:(c2 + 1) * 128], ptb[:, c2 * 128:(c2 + 1) * 128], ident[:])

                  tl.append(f_a)
                  tl.append(f_p)
                  tl.append(lambda: k.cp("act", h3T[:, :, 0:128], psb[4][:, 0:1024].rearrange("p (k t) -> p k t", k=8)))
                  tl.append(f_ptr)
                  tl.append(lambda: k.cp("act", pT[:], psb[5][:, 0:256].rearrange("p (c t) -> p c t", c=2)))
                  for half in range(2):
                      hs = slice(half * 512, (half + 1) * 512)
                      thh = th2[half]

                      def f_g(hs=hs):
                          for kk in range(8):
                              k.mm(ps[6][:, :], h3T[:, kk, :], wpg[:, kk, hs], start=(kk == 0), stop=(kk == 7))

                      def f_w(hs=hs):
                          for c2 in range(2):
                              k.mm(ps[7][:, :], pT[:, c2, :], wpl[:, c2, hs], start=(c2 == 0), stop=(c2 == 1))

                      tl.append(f_g)
                      tl.append(f_w)
                      tl.append(lambda thh=thh: k.act(thh, ps[6][:, :], AF.Exp, scale=-1.0))
                      tl.append(lambda thh=thh: k.act(thh, thh, AF.Ln, bias=1.0))
                      tl.append(lambda thh=thh: k.act(thh, thh, AF.Exp, scale=-1.0))
                      tl.append(lambda thh=thh: k.tt("dve", thh, thh, ps[7][:, :], ALU.mult))
                      tl.append(lambda thh=thh, hs=hs: k.tt("pool", outt[:, hs], thh, xin[:, hs], ALU.add))
                  tl.append(lambda: k.dma(out_d[tt_ * 128:(tt_ + 1) * 128, :], outt))
                  return tl

              pend = []
              s1(0)
              for b in range(NBT):
                  s2_s3(b, pend)
                  assert not pend
                  if b + 1 < NBT:
                      s1(b + 1)
                  pend = ple_thunks(b, 0) + ple_thunks(b, 1)
              for t_ in pend:
                  t_()
    except _Stop:
        pass
    P.emit(nc)
    return nc


_NC_CACHE = {}


def _core_inputs(inp, b, consts):
    sq = lambda a: np.ascontiguousarray(np.asarray(a)[0], dtype=np.float32)
    m = {
        "x": np.ascontiguousarray(np.asarray(inp["x"])[b], dtype=np.float32),
        "p": np.ascontiguousarray(np.asarray(inp["p"])[0, b], dtype=np.float32),
    }
    for name in ("g_mix", "w_in", "q_norm_g", "kc_norm_g", "ks_norm_g", "kw_norm_g", "cmp_pos_k", "cmp_pos_v",
                 "cmp_k_w1", "cmp_k_b1", "cmp_k_w2", "cmp_k_b2", "cmp_v_w1", "cmp_v_b1", "cmp_v_w2", "cmp_v_b2",
                 "gmlp_ln_g", "gmlp_ln_b", "gmlp_ws", "gmlp_bs", "out_g_nsa", "out_g_gmlp", "w_out",
                 "g_ff", "w_ff1", "w_ff2", "g_ple", "w_ple_gate", "w_ple"):
        m[name] = sq(inp[name])
    m.update(consts)
    return m


def kernel(_stop=None, **inputs):
    x = np.asarray(inputs["x"])
    B, T = x.shape[0], x.shape[1]
    if T not in _NC_CACHE:
        _NC_CACHE[T] = build_nc(T, _stop)
    nc = _NC_CACHE[T]
    consts = make_consts(T)
    in_maps = [_core_inputs(inputs, b, consts) for b in range(B)]
    res = run_bass_kernel_spmd(nc, in_maps, core_ids=list(range(B)))
    return np.stack([np.asarray(r["out"], dtype=np.float32) for r in res.results], axis=0)
```

```python
import contextlib
import math
import numpy as np
import ml_dtypes
import concourse.bass as bass
import concourse.mybir as mybir
from concourse.bass_utils import run_bass_kernel_spmd

F32 = mybir.dt.float32
BF = mybir.dt.bfloat16
ALU = mybir.AluOpType
AF = mybir.ActivationFunctionType
AX = mybir.AxisListType
DSZ = {F32: 4, BF: 2}

D_MODEL = 1024
IN_COLS = 2328
D_FF = 4096
D_PLE = 256
EPS = 1e-6
NDS = 24
SLOPES = [2.0 ** (-(h + 1)) for h in range(8)]
NEG = -30000.0


class Prog:
    ENGS = ("pe", "act", "dve", "pool", "sp")

    def __init__(self):
        self.ops = {e: [] for e in self.ENGS}
        self.all = []
        self.track = {}
        self.seen = {e: {} for e in self.ENGS}
        self.seen_dma = {e: set() for e in self.ENGS}
        self.ndma = 0
        self.dram = set()
        self.psum = set()
        self.pending = {}

    def barrier(self):
        lasts = {E: self.ops[E][-1]["idx"] for E in self.ENGS if self.ops[E]}
        dmas = {}
        for op in self.all:
            if op["dma"]:
                dmas[op["dsem"]] = op["idx"]
        for X in self.ENGS:
            lst = self.pending.setdefault(X, [])
            for E, d in lasts.items():
                if E != X:
                    lst.append(d)
            lst.extend(dmas.values())

    def box(self, ap):
        name = ap.name
        a = ap.ap
        off = int(ap.offset)
        esz = DSZ.get(ap.dtype, 4)
        if name in self.dram:
            ext = 1
            for st, cnt in a:
                ext += (cnt - 1) * abs(st)
            return name, (0, 1, off * esz, (off + ext) * esz)
        if name in self.psum:
            return name, (0, 128, 0, 2048)
        pstride = a[0][0]
        if pstride == 0:
            p0, f0 = 0, off
        else:
            p0, f0 = off // pstride, off % pstride
        ext = 1
        for st, cnt in a[1:]:
            ext += (cnt - 1) * abs(st)
        return name, (p0, p0 + a[0][1], f0 * esz, (f0 + ext) * esz)

    @staticmethod
    def _ov(a, b):
        return a[0] < b[1] and b[0] < a[1] and a[2] < b[3] and b[2] < a[3]

    @staticmethod
    def _inside(a, b):
        return a[0] >= b[0] and a[1] <= b[1] and a[2] >= b[2] and a[3] <= b[3]

    def add(self, eng, fn, outs, ins, dma=False):
        idx = len(self.all)
        op = dict(eng=eng, fn=fn, waits=[], sig=False, dma=dma, seq=len(self.ops[eng]) + 1, idx=idx)
        deps = {}
        for d in self.pending.pop(eng, []):
            deps[d] = True
        inb = [self.box(a) for a in ins]
        outb = [self.box(a) for a in outs]
        for name, b in inb:
            isps = name in self.psum
            for key, ent in self.track.get(name, {}).items():
                if not self._ov(key[0], b):
                    continue
                if key[2] == "w":
                    deps[ent] = True
                elif isps and key[1] != eng:
                    deps.setdefault(ent, False)
        for name, b in outb:
            tr = self.track.setdefault(name, {})
            for key in list(tr.keys()):
                if self._ov(key[0], b):
                    deps.setdefault(tr[key], False)
                    if self._inside(key[0], b):
                        del tr[key]
        for name, b in inb:
            self.track.setdefault(name, {})[(b, eng, "r")] = idx
        for name, b in outb:
            self.track.setdefault(name, {})[(b, eng, "w")] = idx
        for d in sorted(deps):
            raw = deps[d]
            dop = self.all[d]
            if dop["dma"]:
                if d in self.seen_dma[eng]:
                    continue
                self.seen_dma[eng].add(d)
                op["waits"].append(("d", d))
            else:
                E = dop["eng"]
                if E == eng and not dma:
                    if eng == "pe":
                        continue
                if self.seen[eng].get(E, 0) >= dop["seq"]:
                    continue
                self.seen[eng][E] = dop["seq"]
                dop["sig"] = True
                op["waits"].append(("c", d))
        if dma:
            j = self.ndma
            self.ndma += 1
            op["dsem"] = j % NDS
            op["dval"] = 16 * (j // NDS + 1)
            op["dprev"] = 16 * (j // NDS)
        self.all.append(op)
        self.ops[eng].append(op)
        return op

    def emit(self, nc):
        with contextlib.ExitStack() as st:
            esem = {e: st.enter_context(nc.semaphore("s_" + e)) for e in self.ENGS}
            dsem = [st.enter_context(nc.semaphore("d%d" % i)) for i in range(NDS)]
            for e in self.ENGS:
                c = 0
                for op in self.ops[e]:
                    if op["sig"] and not op["dma"]:
                        c += 1
                        op["cnt"] = c
            dfinal = [0] * NDS
            for op in self.all:
                if op["dma"]:
                    dfinal[op["dsem"]] = max(dfinal[op["dsem"]], op["dval"])
            block = st.enter_context(nc.Block())

            def run(e, eng):
                for op in self.ops[e]:
                    if op["dma"] and op["dprev"] > 0:
                        eng.wait_ge(dsem[op["dsem"]], op["dprev"])
                    for kind, d in op["waits"]:
                        dop = self.all[d]
                        if kind == "d":
                            eng.wait_ge(dsem[dop["dsem"]], dop["dval"])
                        else:
                            eng.wait_ge(esem[dop["eng"]], dop["cnt"])
                    ins = op["fn"](eng)
                    if op["dma"]:
                        ins.then_inc(dsem[op["dsem"]], 16)
                    elif op["sig"]:
                        ins.then_inc(esem[e], 1)
                if e == "sp":
                    for i in range(NDS):
                        if dfinal[i] > 0:
                            eng.wait_ge(dsem[i], dfinal[i])

            block.tensor(lambda eng: run("pe", eng))
            block.scalar(lambda eng: run("act", eng))
            block.vector(lambda eng: run("dve", eng))
            block.gpsimd(lambda eng: run("pool", eng))
            block.sync(lambda eng: run("sp", eng))


def _aps(*xs):
    return [x for x in xs if x is not None and not isinstance(x, (int, float))]


class K:
    def __init__(self, P):
        self.P = P
        self.consts = {}

    def eps_ap(self, val, like):
        t = self.consts[round(float(val), 12)]
        p0 = like.base_partition()
        return t[p0:p0 + like.partition_size(), 0:1]

    def mm(self, out, lhsT, rhs, start=True, stop=True, skip=False):
        if skip:
            self.P.add("pe", lambda e: e.matmul(out, lhsT, rhs, start=start, stop=stop, skip_group_check=True), [out], [lhsT, rhs])
        else:
            self.P.add("pe", lambda e: e.matmul(out, lhsT, rhs, start=start, stop=stop), [out], [lhsT, rhs])

    def tr(self, out, in_, ident):
        self.P.add("pe", lambda e: e.transpose(out, in_, ident), [out], [in_, ident])

    def act(self, out, in_, func, bias=None, scale=None, accum=None):
        kw = {}
        if bias is not None:
            kw["bias"] = bias
        if scale is not None:
            kw["scale"] = scale
        if accum is not None:
            kw["accum_out"] = accum
        self.P.add("act", lambda e: e.activation(out, in_, func, **kw), _aps(out, accum), _aps(in_, bias, scale))

    def ts(self, eng, out, in0, s1, s2, op0, op1=None):
        if op1 is None:
            self.P.add(eng, lambda e: e.tensor_scalar(out, in0, s1, None, op0), [out], _aps(in0, s1))
        else:
            self.P.add(eng, lambda e: e.tensor_scalar(out, in0, s1, s2, op0, op1), [out], _aps(in0, s1, s2))

    def tt(self, eng, out, a, b, op):
        self.P.add(eng, lambda e: e.tensor_tensor(out, a, b, op), [out], [a, b])

    def stt(self, eng, out, in0, scalar, in1, op0, op1):
        self.P.add(eng, lambda e: e.scalar_tensor_tensor(out, in0, scalar, in1, op0, op1), [out], _aps(in0, scalar, in1))

    def rsq(self, out, in_, eps, mul=1.0):
        self.act(out, in_, AF.Ln, bias=float(eps))
        self.act(out, out, AF.Exp, scale=-0.5)

    def cp(self, eng, out, in_):
        if eng == "act":
            self.P.add("act", lambda e: e.copy(out, in_), [out], [in_])
        else:
            self.P.add(eng, lambda e: e.tensor_copy(out, in_), [out], [in_])

    def memset(self, eng, out, val):
        self.P.add(eng, lambda e: e.memset(out, val), [out], [])

    def red(self, out, in_, op=ALU.add):
        self.P.add("dve", lambda e: e.tensor_reduce(out, in_, AX.X, op), [out], [in_])

    def recip(self, out, in_):
        self.P.add("dve", lambda e: e.reciprocal(out, in_), [out], [in_])

    def max8(self, out, in_):
        self.P.add("dve", lambda e: e.max(out, in_), [out], [in_])

    def mrep(self, out, rep, vals, imm):
        self.P.add("dve", lambda e: e.match_replace(out, rep, vals, imm), [out], [rep, vals])

    def dma(self, out, in_, slow=False):
        if slow:
            self.P.add("sp", lambda e: e.dma_start(out=out, in_=in_, allow_slow_non_contiguous=True), [out], [in_], dma=True)
        else:
            self.P.add("sp", lambda e: e.dma_start(out=out, in_=in_), [out], [in_], dma=True)


def make_consts(T):
    NT = T // 128
    NB = T // 64
    NC = T // 16 - 1
    bf = ml_dtypes.bfloat16
    c = {}
    c["c_ident"] = np.eye(128, dtype=np.float32).astype(bf)
    key = np.arange(T)
    E = (key[None, :] // 64 == np.arange(64)[:, None]).astype(np.float32)
    c["c_E"] = E.astype(bf)
    D = 64.0 * (np.arange(64)[:, None] - (key[None, :] // 64))
    c["c_D"] = D.astype(np.float32).astype(bf)
    p = np.arange(128)[:, None]
    f = np.arange(128)[None, :]
    c["c_trile"] = (p <= f).astype(np.float32).astype(bf)
    c["c_trigt"] = (p > f).astype(np.float32).astype(bf)
    W = NC + 8 * (NT - 1)
    m = np.arange(W)[None, :] - 8 * (NT - 1)
    G = np.where(16 * m + 31 <= p, -(p - 16.0 * m - 15.5), -1.0e6)
    c["c_G"] = G.astype(np.float32)
    W2 = NB + 2 * (NT - 1)
    jp = np.arange(W2)[None, :] - 2 * (NT - 1)
    cur = (p >= 64).astype(np.int64)
    Fw = np.where(jp == cur, 2.0e4, np.where(jp == cur - 1, 1.0e4, 0.0))
    Uw = np.where(jp <= cur, 1.0e9, -1.0)
    c["c_Fw"] = Fw.astype(np.float32)
    c["c_Uw"] = Uw.astype(np.float32)
    wb = np.zeros((128, 8), np.float32)
    for h in range(8):
        wb[:, h] = SLOPES[h] * (np.arange(128) % 64)
    c["c_wb"] = wb
    return c


class _Stop(Exception):
    pass


def build_nc(T, stop=None):
    NT = T // 128
    NQ = T // 512
    NB = T // 64
    NC = T // 16 - 1
    NCP = T // 16
    NCT = (NCP + 127) // 128
    WG = NC + 8 * (NT - 1)
    W2 = NB + 2 * (NT - 1)
    nc = bass.Bass("TRN2", target_bir_lowering=False)
    P = Prog()
    k = K(P)

    def din(name, shape, dt=F32):
        P.dram.add(name)
        return nc.dram_tensor(name, list(shape), dt, kind="ExternalInput").ap()

    x_d = din("x", [T, D_MODEL])
    p_d = din("p", [T, D_PLE])
    g_mix_d = din("g_mix", [D_MODEL])
    w_in_d = din("w_in", [D_MODEL, IN_COLS])
    q_g_d = din("q_norm_g", [64])
    kc_g_d = din("kc_norm_g", [64])
    ks_g_d = din("ks_norm_g", [64])
    kw_g_d = din("kw_norm_g", [64])
    pos_k_d = din("cmp_pos_k", [32, 64])
    pos_v_d = din("cmp_pos_v", [32, 64])
    cw1_d = [din("cmp_k_w1", [2048, 256]), din("cmp_v_w1", [2048, 256])]
    cb1_d = [din("cmp_k_b1", [256]), din("cmp_v_b1", [256])]
    cw2_d = [din("cmp_k_w2", [256, 64]), din("cmp_v_w2", [256, 64])]
    cb2_d = [din("cmp_k_b2", [64]), din("cmp_v_b2", [64])]
    ln_g_d = din("gmlp_ln_g", [512])
    ln_b_d = din("gmlp_ln_b", [512])
    ws_d = din("gmlp_ws", [8, 128, 128])
    bs_d = din("gmlp_bs", [8, 128])
    og_nsa_d = din("out_g_nsa", [512])
    og_gmlp_d = din("out_g_gmlp", [512])
    w_out_d = din("w_out", [D_MODEL, D_MODEL])
    g_ff_d = din("g_ff", [D_MODEL])
    w_ff1_d = din("w_ff1", [D_MODEL, D_FF])
    w_ff2_d = din("w_ff2", [D_FF, D_MODEL])
    g_ple_d = din("g_ple", [D_MODEL])
    w_pg_d = din("w_ple_gate", [D_MODEL, D_MODEL])
    w_ple_d = din("w_ple", [D_PLE, D_MODEL])
    c_ident_d = din("c_ident", [128, 128], BF)
    c_E_d = din("c_E", [64, T], BF)
    c_D_d = din("c_D", [64, T], BF)
    c_trile_d = din("c_trile", [128, 128], BF)
    c_trigt_d = din("c_trigt", [128, 128], BF)
    c_G_d = din("c_G", [128, WG])
    c_Fw_d = din("c_Fw", [128, W2])
    c_Uw_d = din("c_Uw", [128, W2])
    c_wb_d = din("c_wb", [128, 8])
    P.dram.add("x1s")
    x1_d = nc.dram_tensor("x1s", [T, D_MODEL], F32, kind="Internal").ap()
    P.dram.add("out")
    out_d = nc.dram_tensor("out", [T, D_MODEL], F32, kind="ExternalOutput").ap()

    ES = contextlib.ExitStack()

    def ck(stage, aps):
        if stop != stage:
            return
        for i, a in enumerate(aps):
            n = a.shape[-1] if len(a.shape) == 2 else None
            d = stg[i % 2]
            k.cp("dve", d[0:a.shape[0], 0:n], a)
            k.dma(out_d[i * 128:i * 128 + a.shape[0], 0:n], d[0:a.shape[0], 0:n])
        raise _Stop()

    cur = [ES]

    def sb(name, shape, dt=F32, st=None):
        return (st or cur[0]).enter_context(nc.sbuf_tensor(name, list(shape), dt))

    def col(d_ap, n):
        return d_ap.rearrange("(p o) -> p o", o=1)

    try:
      with ES:
          ps = [ES.enter_context(nc.psum_tensor("ps%d" % i, [128, 512], F32)) for i in range(8)]
          for i in range(8):
              P.psum.add("ps%d" % i)
          psb = [t[:].bitcast(BF) for t in ps]

          ident = sb("ident", [128, 128], BF)
          k.dma(ident[:], c_ident_d)
          trile = sb("trile", [128, 128], BF)
          k.dma(trile[:], c_trile_d)
          trigt = sb("trigt", [128, 128], BF)
          k.dma(trigt[:], c_trigt_d)
          wb = sb("wb", [128, 8])
          k.dma(wb[:], c_wb_d)
          gcols = sb("gcols", [128, 4, 8])
          k.dma(gcols[:, 0, :], g_mix_d.rearrange("(k p) -> p k", p=128), slow=True)
          k.dma(gcols[:, 1, :], g_ff_d.rearrange("(k p) -> p k", p=128), slow=True)
          k.dma(gcols[:, 2, :], g_ple_d.rearrange("(k p) -> p k", p=128), slow=True)
          k.dma(gcols[:, 3, 0:4], og_nsa_d.rearrange("(k p) -> p k", p=128), slow=True)
          k.dma(gcols[:, 3, 4:8], og_gmlp_d.rearrange("(k p) -> p k", p=128), slow=True)
          eps_c = sb("eps_c", [128, 1]); k.memset("dve", eps_c[:], EPS)
          stg = [sb("stg%d" % i, [128, 2048]) for i in range(2)]
          cnt = [0]

          def load_w(dst_fn, src, nk, ncols, gcol=None, segs=None, stages=None):
              if segs is None:
                  segs = [(0, ncols, 0)]
              if stages is None:
                  stages = stg
              for kk in range(nk):
                  for c0 in range(0, ncols, 2048):
                      c1 = min(ncols, c0 + 2048)
                      s = stages[cnt[0] % len(stages)]
                      cnt[0] += 1
                      k.dma(s[:, 0:c1 - c0], src[kk * 128:(kk + 1) * 128, c0:c1])
                      for (a0, a1, d0) in segs:
                          lo, hi = max(a0, c0), min(a1, c1)
                          if lo >= hi:
                              continue
                          dst = dst_fn(kk, d0 + lo - a0, d0 + hi - a0)
                          if gcol is None:
                              k.cp(("act", "dve")[cnt[0] % 2], dst, s[:, lo - c0:hi - c0])
                          else:
                              k.act(dst, s[:, lo - c0:hi - c0], AF.Copy, scale=gcol[:, kk:kk + 1])

          SATT = contextlib.ExitStack()
          cur[0] = SATT
          Gw = sb("Gw", [128, WG])
          k.dma(Gw[:], c_G_d)
          Fw = sb("Fw", [128, W2])
          k.dma(Fw[:], c_Fw_d)
          Uw = sb("Uw", [128, W2])
          k.dma(Uw[:], c_Uw_d)
          Dt = sb("Dt", [128, T], BF)
          k.dma(Dt[64:128, :], c_D_d)
          KEs = [sb("KEs%d" % g, [128, T], BF) for g in range(2)]
          KEw = [sb("KEw%d" % g, [128, T], BF) for g in range(2)]
          for t_ in KEs + KEw:
              k.dma(t_[64:128, :], c_E_d)
          Vall = sb("Vall", [128, NT, 4, 66], BF)
          k.memset("dve", Vall[:], 1.0)
          KcT = [sb("KcT%d" % g, [64, NCT * 128], BF) for g in range(2)]
          Vc = [sb("Vc%d" % g, [128, NCT, 64], BF) for g in range(2)]
          gates = sb("gates", [128, NT, 24])
          rstd_g = sb("rstd_g", [128, NT])
          gq = sb("gq", [64, 1]); k.dma(gq[:], col(q_g_d, 64))
          gks8 = sb("gks8", [64, 1]); k.dma(gks8[:], col(ks_g_d, 64))
          gkw8 = sb("gkw8", [64, 1]); k.dma(gkw8[:], col(kw_g_d, 64))
          k.ts("dve", gks8[:], gks8[:], 8.0, None, ALU.mult)
          k.ts("dve", gkw8[:], gkw8[:], 8.0, None, ALU.mult)
          lng = sb("lng", [128, 512]); k.dma(lng[:], ln_g_d.partition_broadcast(128))
          lnb = sb("lnb", [128, 512]); k.dma(lnb[:], ln_b_d.partition_broadcast(128))
          bstab = sb("bstab", [128, 8]); k.dma(bstab[:], bs_d.rearrange("g t -> t g"), slow=True)

          def front(xt, hT, gidx_unused, junk, ssum, rs, hb, tsl):
              k.act(junk[:, 0:1024], xt, AF.Square, accum=ssum[:])
              k.rsq(rs[:], ssum[:], float(D_MODEL * EPS))
              k.ts("dve", hb[:], xt, rs[:, 0:1], 32.0, ALU.mult, ALU.mult)
              for kk in range(8):
                  k.tr(psb[4][:, kk * 128:(kk + 1) * 128], hb[:, kk * 128:(kk + 1) * 128], ident[:])
              k.cp("act", hT[:, :, tsl], psb[4][:, 0:1024].rearrange("p (k t) -> p k t", k=8))

          def gelu_to(out, src_ps, n, junk_a, junk_b, accum=None, p0=0, p1=128, bias=None):
              xv = junk_a[p0:p1, 0:n]
              sq = junk_b[p0:p1, 0:n]
              if bias is None:
                  k.act(xv, src_ps, AF.Copy)
                  k.act(sq, src_ps, AF.Square)
              else:
                  k.ts("dve", xv, src_ps, bias, None, ALU.add)
                  k.act(sq, src_ps, AF.Square, bias=bias)
              k.ts("dve", sq, sq, 0.044715, 1.0, ALU.mult, ALU.add)
              k.tt("pool", sq, sq, xv, ALU.mult)
              k.act(sq, sq, AF.Exp, scale=-1.5957691216)
              k.ts("dve", sq, sq, 1.0, None, ALU.add)
              k.recip(sq, sq)
              if accum is None:
                  k.tt("dve", out, sq, xv, ALU.mult)
              else:
                  P.add("dve", lambda e: e.scalar_tensor_tensor(out, sq, 1.0, xv, ALU.mult, ALU.mult, accum_out=accum),
                        [out, accum], [sq, xv])

          SA = contextlib.ExitStack()
          with SA:
              w_inA = sb("w_inA", [128, 8, 768], BF, SA)
              kvT = sb("kvT", [128, 2, T], BF, SA)
              segsA = [(0, 384, 0), (384, 512, 512), (512, 640, 384), (640, 768, 640)]
              kvTf = kvT[:].rearrange("p a t -> p (a t)").bitcast(F32)
              load_w(lambda kk, a, b: w_inA[:, kk, a:b], w_in_d[:, 512:1280], 8, 768, gcols[:, 0, :], segsA,
                     stages=[stg[0], stg[1], kvTf[:, 0:2048], kvTf[:, 2048:4096]])
              ck(1, [w_inA[:, 0, 0:512], w_inA[:, 7, 256:768]])
              xtA = [sb("xtA%d" % i, [128, 1024], F32, SA) for i in range(2)]
              hTA = [sb("hTA%d" % i, [128, 8, 128], BF, SA) for i in range(2)]
              junkA = sb("junkA", [128, 1024], F32, SA)
              hbA = sb("hbA", [128, 1024], BF, SA)
              ssA = sb("ssA", [128, 1], F32, SA)
              rsA = sb("rsA", [128, 1], F32, SA)
              ssk = sb("ssk", [128, 4], F32, SA)
              zbA = sb("zbA", [128, 512], BF, SA)
              def front_a(tt_):
                  xt = xtA[tt_ % 2]
                  hT = hTA[tt_ % 2]
                  k.dma(xt[:], x_d[tt_ * 128:(tt_ + 1) * 128, :])
                  front(xt[:], hT, 0, junkA, ssA, rsA, hbA, slice(0, 128))
                  b0, b1 = ps[2 * (tt_ % 2)], ps[2 * (tt_ % 2) + 1]
                  for kk in range(8):
                      k.mm(b0[:, 0:512], hT[:, kk, :], w_inA[:, kk, 0:512], start=(kk == 0), stop=(kk == 7))
                      k.mm(b1[:, 0:256], hT[:, kk, :], w_inA[:, kk, 512:768], start=(kk == 0), stop=(kk == 7))

              def post_a(tt_):
                  b0, b1 = ps[2 * (tt_ % 2)], ps[2 * (tt_ % 2) + 1]
                  k.act(junkA[:, 0:256], b0[:, 256:512], AF.Square)
                  k.red(ssk[:], junkA[:, 0:256].rearrange("p (a d) -> p a d", d=64))
                  k.rsq(ssk[:], ssk[:], 64.0 * EPS)
                  k.cp("act", zbA[:, 0:256], b0[:, 0:256])
                  k.tt("dve", zbA[:, 256:512].rearrange("p (a d) -> p a d", d=64),
                       b0[:, 256:512].rearrange("p (a d) -> p a d", d=64),
                       ssk[:].unsqueeze(2).to_broadcast([128, 4, 64]), ALU.mult)
                  k.cp("act", Vall[:, tt_, :, 0:64], b1[:, 0:256].rearrange("p (a d) -> p a d", d=64))
                  k.tr(psb[5][:, 0:128], zbA[:, 0:128], ident[:])
                  k.tr(psb[5][:, 128:256], zbA[:, 128:256], ident[:])
                  for a in range(4):
                      k.tr(psb[5][0:64, 256 + a * 128:384 + a * 128], zbA[:, 256 + a * 64:320 + a * 64], ident[:])
                  tsl = slice(tt_ * 128, (tt_ + 1) * 128)
                  k.cp("dve", kvT[:, :, tsl], psb[5][:, 0:256].rearrange("p (a t) -> p a t", a=2))
                  for g in range(2):
                      k.ts("dve", KEs[g][0:64, tsl], psb[5][0:64, 256 + g * 128:384 + g * 128], gks8[:, 0:1], None, ALU.mult)
                      k.ts("dve", KEw[g][0:64, tsl], psb[5][0:64, 512 + g * 128:640 + g * 128], gkw8[:, 0:1], None, ALU.mult)

              front_a(0)
              for tt_ in range(NT):
                  if tt_ + 1 < NT:
                      front_a(tt_ + 1)
                  post_a(tt_)

              ck(2, [KEs[0][:, 0:1024], kvT[:, 0, 0:1024], KEw[1][:, 0:1024], Vall[:, 3, :, :].rearrange("p a d -> p (a d)")])
              SC = contextlib.ExitStack()
              with SC:
                  w1r = sb("w1r", [128, 32, 256], BF, SC)
                  w2b = sb("w2b", [128, 2, 64], BF, SC)
                  posT = sb("posT", [128, 32], BF, SC)
                  posf = sb("posf", [128, 32], F32, SC)
                  b1c = sb("b1c", [128, 2], F32, SC)
                  bias1 = sb("bias1", [128, 2], F32, SC)
                  b2t = sb("b2t", [128, 64], F32, SC)
                  gkc = sb("gkc", [128, 64], F32, SC)
                  hdn = sb("hdn", [128, 2, NCT * 128], BF, SC)
                  ja = sb("cja", [128, 256], F32, SC)
                  jb = sb("cjb", [128, 256], F32, SC)
                  kcf = sb("kcf", [128, 64], F32, SC)
                  kcb = sb("kcb", [128, 64], BF, SC)
                  ssc = sb("ssc", [128, 1], F32, SC)
                  k.dma(gkc[:], kc_g_d.partition_broadcast(128))
                  k.memset("dve", hdn[:], 0.0)
                  for kv in range(2):
                      w1v = cw1_d[kv].rearrange("(l d) h -> d l h", d=64)
                      for half in range(2):
                          for lc in range(4):
                              s = stg[cnt[0] % 2]
                              cnt[0] += 1
                              k.dma(s[half * 64:(half + 1) * 64, :].rearrange("p (l h) -> p l h", h=256),
                                    w1v[:, lc * 8:(lc + 1) * 8, :])
                              k.cp(("pool", "dve")[lc % 2],
                                   w1r[half * 64:(half + 1) * 64, lc * 8:(lc + 1) * 8, :],
                                   s[half * 64:(half + 1) * 64, :].rearrange("p (l h) -> p l h", h=256))
                      s = stg[cnt[0] % 2]
                      cnt[0] += 1
                      k.dma(s[:, 0:128].rearrange("p (c o) -> p c o", c=2), cw2_d[kv].rearrange("(c p) o -> p c o", p=128))
                      k.cp("dve", w2b[:], s[:, 0:128].rearrange("p (c o) -> p c o", c=2))
                      pos_d = (pos_k_d, pos_v_d)[kv]
                      for half in range(2):
                          k.dma(posf[half * 64:(half + 1) * 64, :], pos_d.rearrange("l d -> d l"), slow=True)
                      k.cp("dve", posT[:], posf[:])
                      k.dma(b1c[:], cb1_d[kv].rearrange("(c p) -> p c", p=128), slow=True)
                      k.dma(b2t[:], cb2_d[kv].partition_broadcast(128))
                      for hh in range(2):
                          for l in range(32):
                              k.mm(ps[6][:, hh:hh + 1], w1r[0:64, l, hh * 128:(hh + 1) * 128], posT[0:64, l:l + 1],
                                   start=(l == 0), stop=(l == 31))
                      k.tt("dve", bias1[:], ps[6][:, 0:2], b1c[:], ALU.add)
                      for g in range(2):
                          pr = slice(g * 64, (g + 1) * 64)
                          for hh in range(2):
                              for l in range(32):
                                  k.mm(ps[hh][:, 0:NC], w1r[pr, l, hh * 128:(hh + 1) * 128],
                                       kvT[pr, kv, l:l + 16 * (NC - 1) + 1:16], start=(l == 0), stop=(l == 31))
                          for hh in range(2):
                              for c0 in range(0, NC, 256):
                                  c1 = min(NC, c0 + 256)
                                  gelu_to(hdn[:, hh, c0:c1], ps[hh][:, c0:c1], c1 - c0, ja, jb, bias=bias1[:, hh:hh + 1])
                          for ct in range(NCT):
                              ncv = min(128, NC - ct * 128)
                              for hh in range(2):
                                  k.mm(ps[2][0:ncv, 0:64], hdn[:, hh, ct * 128:ct * 128 + ncv], w2b[:, hh, :],
                                       start=(hh == 0), stop=(hh == 1))
                              k.tt("dve", kcf[0:ncv, :], ps[2][0:ncv, 0:64], b2t[0:ncv, :], ALU.add)
                              if kv == 1:
                                  if ncv < 128:
                                      k.memset("dve", Vc[g][:, ct, :], 0.0)
                                  k.cp("dve", Vc[g][0:ncv, ct, :], kcf[0:ncv, :])
                              else:
                                  k.act(ja[0:ncv, 0:64], kcf[0:ncv, :], AF.Square, accum=ssc[0:ncv, :])
                                  k.rsq(ssc[0:ncv, :], ssc[0:ncv, :], 64.0 * EPS)
                                  k.ts("dve", kcf[0:ncv, :], kcf[0:ncv, :], ssc[0:ncv, 0:1], 8.0, ALU.mult, ALU.mult)
                                  if ncv < 128:
                                      k.memset("dve", kcb[:], 0.0)
                                  k.tt("dve", kcb[0:ncv, :], kcf[0:ncv, :], gkc[0:ncv, :], ALU.mult)
                                  k.tr(psb[5][0:64, 0:128], kcb[:, :], ident[:])
                                  k.cp("dve", KcT[g][:, ct * 128:(ct + 1) * 128], psb[5][0:64, 0:128])

          P.barrier()
          ck(3, [KcT[0][:, 0:128], KcT[1][:, 0:128], Vc[0][:, 0, :], Vc[1][:, 0, :]])
          SB_ = contextlib.ExitStack()
          with SB_:
              w_inB = sb("w_inB", [128, 8, 1560], BF, SB_)
              w_outb = sb("w_outb", [128, 8, 1024], BF, SB_)
              xs = sb("xs", [128, 4, 1024], F32, SB_)
              xsf = xs[:].rearrange("p a b -> p (a b)")
              st4b = [stg[0], stg[1], xsf[:, 0:2048], xsf[:, 2048:4096]]
              load_w(lambda kk, a, b: w_inB[:, kk, a:b], w_in_d[:, 0:512], 8, 512, gcols[:, 0, :], [(0, 512, 0)], stages=st4b)
              load_w(lambda kk, a, b: w_inB[:, kk, a:b], w_in_d[:, 1280:2328], 8, 1048, gcols[:, 0, :],
                     [(0, 24, 1536), (24, 1048, 512)], stages=st4b)
              load_w(lambda kk, a, b: w_outb[:, kk, a:b], w_out_d, 4, 1024, gcols[:, 3, 0:4], stages=st4b)
              load_w(lambda kk, a, b: w_outb[:, 4 + kk, a:b], w_out_d[512:1024, :], 4, 1024, gcols[:, 3, 4:8], stages=st4b)
              WmT = sb("WmT", [128, 8, 128], BF, SB_)
              wsf = sb("wsf", [128, 128], F32, SB_)
              wsb = sb("wsb", [128, 128], BF, SB_)
              hTB = sb("hTB", [128, 8, 128], BF, SB_)
              junkB = stg[0][:, 1024:2048]
              junkC = stg[1][:, 1024:2048]
              hbB = sb("hbB", [128, 1024], BF, SB_)
              ssB = sb("ssB", [128, 1], F32, SB_)
              rsB = sb("rsB", [128, 1], F32, SB_)
              ssq = sb("ssq", [128, 8], F32, SB_)
              qnb = sb("qnb", [128, 512], BF, SB_)
              R = sb("R", [128, 8, 512], BF, SB_)
              u_sb = sb("u_sb", [128, 512], F32, SB_)
              v_sb = sb("v_sb", [128, 512], F32, SB_)
              vnb = sb("vnb", [128, 512], BF, SB_)
              st1 = sb("st1", [128, 4], F32, SB_)
              gof = v_sb
              gob = sb("gob", [128, 512], BF, SB_)
              goT = sb("goT", [128, 4, 512], BF, SB_)
              aoT = sb("aoT", [128, 4, 128], BF, SB_)
              ao = sb("ao", [128, 4, 512], F32, SB_)
              aob = qnb
              rstd_a = sb("rstd_a", [128, 1], F32, SB_)
              sc = sb("sc", [128, 2, 256], F32, SB_)
              ef = sb("ef", [128, 2, 8, 256], BF, SB_)
              ebT = sb("ebT", [128, 2, NCT, 128], BF, SB_)
              csum = sb("csum", [128, 2, 8], F32, SB_)
              crin = sb("crin", [128, 2, 8], F32, SB_)
              icp = sb("icp", [128, 2, NCP + 1], F32, SB_)
              imp = sb("imp", [128, 2, 64], F32, SB_)
              impw = sb("impw", [128, 2, 64], F32, SB_)
              m8 = sb("m8", [128, 2, 16], F32, SB_)
              bm = sb("bm", [128, 2, 128], BF, SB_)
              mbT = sb("mbT", [128, 2, 512], F32, SB_)
              ocs = sb("ocs", [128, 4, 8, 64], F32, SB_)
              pti = [0]
              PT = [sb("PT%d" % i, [128, 512], BF, SB_) for i in range(3)]
              fsum = sb("fsum", [128, 2, 4], F32, SB_)
              coef = sb("coef", [128, 3, 4], F32, SB_)
              tmpo = u_sb[:, 0:256].rearrange("p (a d) -> p a d", d=64)
              x1t = [stg[0][:, 0:1024], stg[1][:, 0:1024]]
              k.memset("dve", icp[:], 0.0)
              k.memset("dve", bm[:], 0.0)
              k.memset("dve", ef[:], 0.0)
              for g in range(8):
                  k.dma(wsf[:], ws_d[g])
                  k.tr(psb[5][:, 0:128], trile[:], ident[:])
                  k.tt("dve", wsb[:], wsf[:], psb[5][:, 0:128], ALU.mult)
                  k.tr(psb[5][:, 128:256], wsb[:], ident[:])
                  k.cp("dve", WmT[:, g, :], psb[5][:, 128:256])

              for Q in range(NQ):
                  ju_a, ju_b = stg[0][:, 0:512], stg[0][:, 512:1024]
                  jv_a, jv_b = stg[0][:, 1024:1536], stg[0][:, 1536:2048]
                  fjunk, qsq, dummy = stg[1][:, 0:1024], stg[1][:, 1024:1536], stg[1][:, 1536:2048]

                  def front_b(qs):
                      tt_ = Q * 4 + qs
                      xt = xs[:, qs, :]
                      k.dma(xt, x_d[tt_ * 128:(tt_ + 1) * 128, :])
                      front(xt, hTB, 0, fjunk, ssB, rsB, hbB, slice(0, 128))
                      for kk in range(8):
                          for cb in range(3):
                              k.mm(ps[cb][:, 0:512], hTB[:, kk, :], w_inB[:, kk, cb * 512:(cb + 1) * 512],
                                   start=(kk == 0), stop=(kk == 7))
                          k.mm(ps[3][:, 0:24], hTB[:, kk, :], w_inB[:, kk, 1536:1560], start=(kk == 0), stop=(kk == 7))

                  def post1_b(qs):
                      tt_ = Q * 4 + qs
                      k.act(qsq, ps[0][:, 0:512], AF.Square)
                      k.red(ssq[:], qsq.rearrange("p (a d) -> p a d", d=64))
                      k.rsq(ssq[:], ssq[:], 64.0 * EPS)
                      k.tt("dve", qnb[:].rearrange("p (a d) -> p a d", d=64),
                           ps[0][:, 0:512].rearrange("p (a d) -> p a d", d=64),
                           ssq[:].unsqueeze(2).to_broadcast([128, 8, 64]), ALU.mult)
                      k.act(gates[:, tt_, :], ps[3][:, 0:24], AF.Exp, scale=-1.0)
                      k.act(ju_a, ps[1][:, 0:512], AF.Copy)
                      k.act(ju_b, ps[1][:, 0:512], AF.Square)
                      k.act(jv_a, ps[2][:, 0:512], AF.Copy)
                      k.act(jv_b, ps[2][:, 0:512], AF.Square)

                  def post2_b(qs):
                      tt_ = Q * 4 + qs
                      k.ts("dve", gates[:, tt_, :], gates[:, tt_, :], 1.0, None, ALU.add)
                      k.recip(gates[:, tt_, :], gates[:, tt_, :])
                      prs = ((ju_a, ju_b), (jv_a, jv_b))
                      for xv, sq in prs:
                          k.ts("dve", sq, sq, 0.044715, 1.0, ALU.mult, ALU.add)
                      for xv, sq in prs:
                          k.tt("dve", sq, sq, xv, ALU.mult)
                      for xv, sq in prs:
                          k.act(sq, sq, AF.Exp, scale=-1.5957691216)
                      for xv, sq in prs:
                          k.act(sq, sq, AF.Ln, bias=1.0)
                      for xv, sq in prs:
                          k.act(sq, sq, AF.Exp, scale=-1.0)
                      k.tt("pool", u_sb[:], ju_b, ju_a, ALU.mult)
                      P.add("dve", lambda e: e.scalar_tensor_tensor(v_sb[:], jv_b, 1.0, jv_a, ALU.mult, ALU.mult, accum_out=st1[:, 0:1]),
                            [v_sb[:], st1[:, 0:1]], [jv_b, jv_a])
                      k.act(dummy, v_sb[:], AF.Square, accum=st1[:, 1:2])
                      k.ts("dve", st1[:, 2:3], st1[:, 0:1], 1.0 / 512, None, ALU.mult)
                      k.stt("dve", st1[:, 3:4], st1[:, 2:3], -1.0, st1[:, 2:3], ALU.mult, ALU.mult)
                      k.stt("dve", st1[:, 3:4], st1[:, 1:2], 1.0 / 512, st1[:, 3:4], ALU.mult, ALU.add)
                      k.rsq(st1[:, 3:4], st1[:, 3:4], EPS)
                      k.ts("dve", v_sb[:], v_sb[:], st1[:, 2:3], st1[:, 3:4], ALU.subtract, ALU.mult)
                      k.tt("dve", v_sb[:], v_sb[:], lng[:], ALU.mult)
                      k.tt("dve", vnb[:], v_sb[:], lnb[:], ALU.add)
                      for g in range(8):
                          k.mm(ps[6][:, g * 64:(g + 1) * 64], WmT[:, g, :], vnb[:, g * 64:(g + 1) * 64])
                      k.tt("dve", gof[:].rearrange("p (a d) -> p a d", d=64),
                           ps[6][:, 0:512].rearrange("p (a d) -> p a d", d=64),
                           bstab[:].unsqueeze(2).to_broadcast([128, 8, 64]), ALU.add)
                      k.tt("pool", gob[:], gof[:], u_sb[:], ALU.mult)
                      k.act(dummy, gob[:], AF.Square, accum=rstd_g[:, tt_:tt_ + 1])
                      k.rsq(rstd_g[:, tt_:tt_ + 1], rstd_g[:, tt_:tt_ + 1], 512.0 * EPS)
                      k.ts("dve", rstd_g[:, tt_:tt_ + 1], rstd_g[:, tt_:tt_ + 1], 22.627416998, None, ALU.mult)
                      for c4 in range(4):
                          k.tr(psb[7][:, c4 * 128:(c4 + 1) * 128], gob[:, c4 * 128:(c4 + 1) * 128], ident[:])
                      k.cp("act", goT[:, :, qs * 128:(qs + 1) * 128], psb[7][:, 0:512].rearrange("p (c t) -> p c t", c=4))
                      for h in range(8):
                          k.tr(psb[5][0:64, h * 128:(h + 1) * 128], qnb[:, h * 64:(h + 1) * 64], ident[:])
                      k.ts("dve", R[0:64, :, qs * 128:(qs + 1) * 128],
                           psb[5][0:64, 0:1024].rearrange("p (h t) -> p h t", h=8), gq[:, 0:1], None, ALU.mult)

                  front_b(0)
                  for qs in range(4):
                      post1_b(qs)
                      if qs + 1 < 4:
                          front_b(qs + 1)
                      post2_b(qs)

                  ck(4, [R[:, 0, :], R[:, 7, :], gof[:], goT[:, 0, :], gates[:, 0:4, :].rearrange("p a d -> p (a d)")])

                  def make_sel(qs):
                      qt = Q * 4 + qs
                      qp = qs % 2
                      ocb = ps[2 + qp]
                      th = []
                      th.append(lambda: k.ts("dve", csum[:, qp, :], csum[:, qp, :], 1e-30, None, ALU.max))
                      th.append(lambda: k.recip(crin[:, qp, :], csum[:, qp, :]))
                      th.append(lambda: k.tt("dve", ocs[:, qs, :, :], ocb[:, 0:512].rearrange("p (h d) -> p h d", h=8),
                                             crin[:, qp, :].unsqueeze(2).to_broadcast([128, 8, 64]), ALU.mult))
                      span = 4 * (NB - 1) + 1
                      for g in range(2):
                          th.append(lambda g=g: k.ts("dve", icp[:, g, 1:1 + NC], ef[:, qp, 4 * g, 0:NC],
                                                     crin[:, qp, 4 * g:4 * g + 1], None, ALU.mult))
                          for r in range(1, 4):
                              th.append(lambda g=g, h=4 * g + r: k.stt("dve", icp[:, g, 1:1 + NC], ef[:, qp, h, 0:NC],
                                                                       crin[:, qp, h:h + 1], icp[:, g, 1:1 + NC], ALU.mult, ALU.add))
                          th.append(lambda g=g: k.cp("dve", imp[:, g, 0:NB], icp[:, g, 0:span:4]))
                          for kk_, wk in ((1, 2.0), (2, 2.0), (3, 2.0), (4, 1.0)):
                              th.append(lambda g=g, kk_=kk_, wk=wk: k.stt("dve", imp[:, g, 0:NB], icp[:, g, kk_:kk_ + span:4], wk,
                                                                          imp[:, g, 0:NB], ALU.mult, ALU.add))
                      if NB < 64:
                          th.append(lambda: k.memset("dve", imp[:, :, NB:64], -1.0))
                      f0 = 2 * (NT - 1) - 2 * qt
                      for g in range(2):
                          th.append(lambda g=g: k.tt("dve", imp[:, g, 0:NB], imp[:, g, 0:NB], Fw[:, f0:f0 + NB], ALU.max))
                          th.append(lambda g=g: k.tt("dve", imp[:, g, 0:NB], imp[:, g, 0:NB], Uw[:, f0:f0 + NB], ALU.min))
                      th.append(lambda: k.memset("dve", imp[:, :, 0:1], 3.0e4))
                      for g in range(2):
                          th.append(lambda g=g: k.max8(m8[:, g, 0:8], imp[:, g, :]))
                          th.append(lambda g=g: k.mrep(impw[:, g, :], m8[:, g, 0:8], imp[:, g, :], -2.0))
                          th.append(lambda g=g: k.max8(m8[:, g, 8:16], impw[:, g, :]))
                          th.append(lambda g=g: k.ts("dve", impw[:, g, :], imp[:, g, :], m8[:, g, 15:16], None, ALU.is_ge))
                          th.append(lambda g=g: k.ts("dve", bm[:, g, 64:128], impw[:, g, :], -NEG, NEG, ALU.mult, ALU.add))
                          th.append(lambda g=g: k.tr(psb[5][:, g * 128:(g + 1) * 128], bm[:, g, :], ident[:]))
                          th.append(lambda g=g: k.cp("dve", mbT[64:128, g, qs * 128:(qs + 1) * 128],
                                                     psb[5][64:128, g * 128:(g + 1) * 128]))
                      return th

                  def heads(qs, pending):
                      qt = Q * 4 + qs
                      qp = qs % 2
                      nctq = min(NCT, (8 * qt + 7 + 127) // 128)
                      gs0 = 8 * (NT - 1) - 8 * qt
                      ocb = ps[2 + qp]

                      def stage_a(h):
                          g = h // 4
                          sbk = ps[0] if h % 2 == 0 else ps[6]
                          k.mm(sbk[:, 0:NC], R[0:64, h, qs * 128:(qs + 1) * 128], KcT[g][:, 0:NC])
                          k.stt("dve", sc[:, h % 2, 0:NC], Gw[:, gs0:gs0 + NC], SLOPES[h], sbk[:, 0:NC], ALU.mult, ALU.add)
                          k.act(ef[:, qp, h, 0:NC], sc[:, h % 2, 0:NC], AF.Exp, accum=csum[:, qp, h:h + 1])

                      def stage_b(h):
                          g = h // 4
                          tb = psb[1] if h % 2 == 0 else psb[7]
                          for ct in range(nctq):
                              k.tr(tb[:, ct * 128:(ct + 1) * 128], ef[:, qp, h, ct * 128:(ct + 1) * 128], ident[:])
                          k.cp("act", ebT[:, h % 2, 0:nctq, :], tb[:, 0:nctq * 128].rearrange("p (c t) -> p c t", c=nctq))
                          for ct in range(nctq):
                              k.mm(ocb[:, h * 64:(h + 1) * 64], ebT[:, h % 2, ct, :], Vc[g][:, ct, :],
                                   start=(ct == 0), stop=(ct == nctq - 1))

                      stage_a(0)
                      for h in range(8):
                          if h + 1 < 8:
                              stage_a(h + 1)
                          stage_b(h)
                          nd = (len(pending) + (7 - h)) // (8 - h)
                          for _ in range(nd):
                              pending.pop(0)()

                  pending = []
                  for qs in range(4):
                      heads(qs, pending)
                      assert not pending
                      pending = make_sel(qs)
                  for th_ in pending:
                      th_()

                  ck(5, [mbT[:, 0, :], mbT[:, 1, :], ocs[:, 3, :, :].rearrange("p a d -> p (a d)"), imp[:, 0, :]])
                  qcols = slice(Q * 512, (Q + 1) * 512)
                  gsl = slice(4 * Q, 4 * Q + 4)
                  tasks = []
                  for h in range(8):
                      for br in range(2):
                          kt_lo = max(0, 4 * Q - 4) if br == 0 else 0
                          kts = list(range(kt_lo, 4 * Q + 4))
                          for kt in kts:
                              tasks.append(dict(h=h, br=br, kt=kt, gfirst=(kt == kts[0]), glast=(kt == kts[-1]), n=len(tasks)))

                  def obank(h, br):
                      return (ps[6 + br] if h % 2 == 0 else ps[br])

                  def emit_S(t):
                      h, br, kt = t["h"], t["br"], t["kt"]
                      g = h // 4
                      if t["gfirst"]:
                          if br == 0:
                              k.ts("dve", R[64:128, h, :], Dt[64:128, qcols], SLOPES[h], None, ALU.mult)
                          else:
                              k.tt("dve", R[64:128, h, :], R[64:128, h, :], mbT[64:128, g, :], ALU.add)
                      i = kt - 4 * Q
                      c_lo = max(0, i)
                      c_hi = min(3, i + 4) if br == 0 else 3
                      t["c"] = (c_lo, c_hi)
                      n0, n1 = c_lo * 128, (c_hi + 1) * 128
                      KE = (KEw, KEs)[br][g]
                      sbank = (ps[3], ps[4], ps[5])[t["n"] % 3]
                      k.mm(sbank[:, n0:n1], KE[:, kt * 128:(kt + 1) * 128], R[:, h, n0:n1])

                  def emit_rest(t):
                      h, br, kt = t["h"], t["br"], t["kt"]
                      g = h // 4
                      i = kt - 4 * Q
                      c_lo, c_hi = t["c"]
                      n0, n1 = c_lo * 128, (c_hi + 1) * 128
                      sbank = (ps[3], ps[4], ps[5])[t["n"] % 3]
                      pt = PT[t["n"] % len(PT)]
                      Ob = obank(h, br)
                      vidx = (2 + g, g)[br]
                      k.act(pt[:, n0:n1], sbank[:, n0:n1], AF.Exp, bias=wb[:, h:h + 1])
                      if i >= 0:
                          k.tt("dve", pt[:, i * 128:(i + 1) * 128], pt[:, i * 128:(i + 1) * 128], trile[:], ALU.mult)
                      if br == 0 and 0 <= i + 4 <= 3:
                          cc = i + 4
                          k.tt("dve", pt[:, cc * 128:(cc + 1) * 128], pt[:, cc * 128:(cc + 1) * 128], trigt[:], ALU.mult)
                      for c in range(c_lo, c_hi + 1):
                          k.mm(Ob[:, c * 65:(c + 1) * 65], pt[:, c * 128:(c + 1) * 128], Vall[:, kt, vidx, 0:65],
                               start=(t["gfirst"] and c == c_lo), stop=(kt == 4 * Q + c), skip=True)
                      if br == 1 and t["glast"]:
                          Ow = obank(h, 0)[:, 0:260].rearrange("p (c d) -> p c d", d=65)
                          Os = obank(h, 1)[:, 0:260].rearrange("p (c d) -> p c d", d=65)
                          k.ts("dve", fsum[:, 0, :], Ow[:, :, 64], 1e-30, None, ALU.max)
                          k.ts("dve", fsum[:, 1, :], Os[:, :, 64], 1e-30, None, ALU.max)
                          k.recip(fsum[:], fsum[:])
                          k.tt("dve", coef[:, 0, :], fsum[:, 0, :], gates[:, gsl, 3 * h + 2], ALU.mult)
                          k.tt("dve", coef[:, 1, :], fsum[:, 1, :], gates[:, gsl, 3 * h + 1], ALU.mult)
                          dst = ao[:, :, h * 64:(h + 1) * 64]
                          k.tt("dve", dst, ocs[:, :, h, :], gates[:, gsl, 3 * h:3 * h + 1].to_broadcast([128, 4, 64]), ALU.mult)
                          k.tt("dve", tmpo, Ow[:, :, 0:64], coef[:, 0, :].unsqueeze(2).to_broadcast([128, 4, 64]), ALU.mult)
                          k.tt("dve", dst, dst, tmpo, ALU.add)
                          k.tt("dve", tmpo, Os[:, :, 0:64], coef[:, 1, :].unsqueeze(2).to_broadcast([128, 4, 64]), ALU.mult)
                          k.tt("dve", dst, dst, tmpo, ALU.add)

                  emit_S(tasks[0])
                  if len(tasks) > 1:
                      emit_S(tasks[1])
                  for n_, t in enumerate(tasks):
                      if n_ + 2 < len(tasks):
                          emit_S(tasks[n_ + 2])
                      emit_rest(t)

                  ck(6, [ao[:, 0, :], ao[:, 3, :]])
                  for qs in range(4):
                      tt_ = Q * 4 + qs
                      x1 = x1t[tt_ % 2]
                      k.act(junkB[:, 0:512], ao[:, qs, :], AF.Square, accum=rstd_a[:])
                      k.rsq(rstd_a[:], rstd_a[:], 512.0 * EPS)
                      k.ts("dve", rstd_a[:], rstd_a[:], 22.627416998, None, ALU.mult)
                      k.cp("pool", aob[:], ao[:, qs, :])
                      for c4 in range(4):
                          k.tr(psb[5][:, c4 * 128:(c4 + 1) * 128], aob[:, c4 * 128:(c4 + 1) * 128], ident[:])
                      k.cp("act", aoT[:], psb[5][:, 0:512].rearrange("p (c t) -> p c t", c=4))
                      for half in range(2):
                          hs = slice(half * 512, (half + 1) * 512)
                          for c4 in range(4):
                              k.mm(ps[half][:, :], aoT[:, c4, :], w_outb[:, c4, hs], start=(c4 == 0), stop=(c4 == 3))
                          for c4 in range(4):
                              k.mm(ps[2 + half][:, :], goT[:, c4, qs * 128:(qs + 1) * 128], w_outb[:, 4 + c4, hs],
                                   start=(c4 == 0), stop=(c4 == 3))
                          k.stt("dve", x1[:, hs], ps[half][:, :], rstd_a[:, 0:1], xs[:, qs, hs], ALU.mult, ALU.add)
                          k.stt("dve", x1[:, hs], ps[2 + half][:, :], rstd_g[:, tt_:tt_ + 1], x1[:, hs], ALU.mult, ALU.add)
                      k.dma(x1_d[tt_ * 128:(tt_ + 1) * 128, :], x1)
          SATT.close()
          cur[0] = ES
          P.barrier()

          SC3 = contextlib.ExitStack()
          with SC3:
              w1b = sb("w1b", [128, 8, 4096], BF, SC3)
              w2f = sb("w2f", [128, 32, 1024], BF, SC3)
              wpg = sb("wpg", [128, 8, 1024], BF, SC3)
              wpl = sb("wpl", [128, 2, 1024], BF, SC3)
              fT = sb("fT", [128, 32, 256], BF, SC3)
              fTf = fT[:].rearrange("p a b -> p (a b)").bitcast(F32)
              st4 = [stg[0], stg[1], fTf[:, 0:2048], fTf[:, 2048:4096]]
              load_w(lambda kk, a, b: w1b[:, kk, a:b], w_ff1_d, 8, D_FF, gcols[:, 1, :], stages=st4)
              load_w(lambda kk, a, b: w2f[:, kk, a:b], w_ff2_d, 32, D_MODEL, stages=st4)
              load_w(lambda kk, a, b: wpg[:, kk, a:b], w_pg_d, 8, D_MODEL, gcols[:, 2, :], stages=st4)
              load_w(lambda kk, a, b: wpl[:, kk, a:b], w_ple_d, 2, D_MODEL, stages=st4)
              xc = sb("xc", [128, 2, 1024], F32, SC3)
              x2 = sb("x2", [128, 2, 1024], F32, SC3)
              hTC = sb("hTC", [128, 8, 256], BF, SC3)
              h3T = sb("h3T", [128, 8, 128], BF, SC3)
              junkD = stg[0][:, 0:1024]
              hbC = sb("hbC", [128, 1024], BF, SC3)
              ssC = sb("ssC", [128, 1], F32, SC3)
              rsC = sb("rsC", [128, 1], F32, SC3)
              rls = [stg[1][:, 1024:1280], stg[1][:, 1536:1792]]
              ptf = stg[1][:, 1280:1536]
              ptb = sb("ptb", [128, 256], BF, SC3)
              pT = sb("pT", [128, 2, 128], BF, SC3)
              th = stg[0][:, 1024:1536]
              outt = stg[1][:, 0:1024]
              xcs = [xc, x2]
              th2 = [stg[0][:, 1024:1536], stg[0][:, 1536:2048]]
              NBT = T // 256

              def s1(b):
                  for j in range(2):
                      tt_ = 2 * b + j
                      k.dma(xcs[b % 2][:, j, :], x1_d[tt_ * 128:(tt_ + 1) * 128, :])
                      front(xcs[b % 2][:, j, :], hTC, 0, junkD, ssC, rsC, hbC, slice(j * 128, (j + 1) * 128))

              def drain(pend, slots_left):
                  nd = (len(pend) + slots_left - 1) // max(1, slots_left)
                  for _ in range(min(nd, len(pend))):
                      pend.pop(0)()

              def s2_s3(b, pend):
                  x2c = xcs[b % 2]
                  for fc in range(32):
                      bank = ps[fc % 2]
                      for kk in range(8):
                          k.mm(bank[:, 0:256], w1b[:, kk, fc * 128:(fc + 1) * 128], hTC[:, kk, :], start=(kk == 0), stop=(kk == 7))
                      rl = rls[fc % 2]
                      k.act(rl, bank[:, 0:256], AF.Relu)
                      k.tt(("pool", "dve")[fc % 2], fT[:, fc, :], rl, rl, ALU.mult)
                      drain(pend, 36 - fc)
                  for gi in range(4):
                      j, half = gi // 2, gi % 2
                      hs = slice(half * 512, (half + 1) * 512)
                      bank = ps[2 + gi % 2]
                      for fc in range(32):
                          k.mm(bank[:, :], fT[:, fc, j * 128:(j + 1) * 128], w2f[:, fc, hs], start=(fc == 0), stop=(fc == 31))
                      k.tt("dve", x2c[:, j, hs], bank[:, :], x2c[:, j, hs], ALU.add)
                      drain(pend, 4 - gi)

              def ple_thunks(b, j):
                  tt_ = 2 * b + j
                  xin = xcs[b % 2][:, j, :]
                  tl = []

                  def f_a():
                      k.act(junkD[:, 0:1024], xin, AF.Square, accum=ssC[:])
                      k.rsq(rsC[:], ssC[:], float(D_MODEL * EPS))
                      k.ts("dve", hbC[:], xin, rsC[:, 0:1], 32.0, ALU.mult, ALU.mult)
                      for kk in range(8):
                          k.tr(psb[4][:, kk * 128:(kk + 1) * 128], hbC[:, kk * 128:(kk + 1) * 128], ident[:])

                  def f_p():
                      k.dma(ptf, p_d[tt_ * 128:(tt_ + 1) * 128, :])
                      k.cp("pool", ptb[:], ptf)

                  def f_ptr():
                      for c2 in range(2):
                          k.tr(psb[5][:, c2 * 128:(c2 + 1) * 128], ptb[:, c2 * 128:(c2 + 1) * 128], ident[:])

                  tl.append(f_a)
                  tl.append(f_p)
                  tl.append(lambda: k.cp("act", h3T[:, :, 0:128], psb[4][:, 0:1024].rearrange("p (k t) -> p k t", k=8)))
                  tl.append(f_ptr)
                  tl.append(lambda: k.cp("act", pT[:], psb[5][:, 0:256].rearrange("p (c t) -> p c t", c=2)))
                  for half in range(2):
                      hs = slice(half * 512, (half + 1) * 512)
                      thh = th2[half]

                      def f_g(hs=hs):
                          for kk in range(8):
                              k.mm(ps[6][:, :], h3T[:, kk, :], wpg[:, kk, hs], start=(kk == 0), stop=(kk == 7))

                      def f_w(hs=hs):
                          for c2 in range(2):
                              k.mm(ps[7][:, :], pT[:, c2, :], wpl[:, c2, hs], start=(c2 == 0), stop=(c2 == 1))

                      tl.append(f_g)
                      tl.append(f_w)
                      tl.append(lambda thh=thh: k.act(thh, ps[6][:, :], AF.Exp, scale=-1.0))
                      tl.append(lambda thh=thh: k.act(thh, thh, AF.Ln, bias=1.0))
                      tl.append(lambda thh=thh: k.act(thh, thh, AF.Exp, scale=-1.0))
                      tl.append(lambda thh=thh: k.tt("dve", thh, thh, ps[7][:, :], ALU.mult))
                      tl.append(lambda thh=thh, hs=hs: k.tt("pool", outt[:, hs], thh, xin[:, hs], ALU.add))
                  tl.append(lambda: k.dma(out_d[tt_ * 128:(tt_ + 1) * 128, :], outt))
                  return tl

              pend = []
              s1(0)
              for b in range(NBT):
                  s2_s3(b, pend)
                  assert not pend
                  if b + 1 < NBT:
                      s1(b + 1)
                  pend = ple_thunks(b, 0) + ple_thunks(b, 1)
              for t_ in pend:
                  t_()
    except _Stop:
        pass
    P.emit(nc)
    return nc


_NC_CACHE = {}


def _core_inputs(inp, b, consts):
    sq = lambda a: np.ascontiguousarray(np.asarray(a)[0], dtype=np.float32)
    m = {
        "x": np.ascontiguousarray(np.asarray(inp["x"])[b], dtype=np.float32),
        "p": np.ascontiguousarray(np.asarray(inp["p"])[0, b], dtype=np.float32),
    }
    for name in ("g_mix", "w_in", "q_norm_g", "kc_norm_g", "ks_norm_g", "kw_norm_g", "cmp_pos_k", "cmp_pos_v",
                 "cmp_k_w1", "cmp_k_b1", "cmp_k_w2", "cmp_k_b2", "cmp_v_w1", "cmp_v_b1", "cmp_v_w2", "cmp_v_b2",
                 "gmlp_ln_g", "gmlp_ln_b", "gmlp_ws", "gmlp_bs", "out_g_nsa", "out_g_gmlp", "w_out",
                 "g_ff", "w_ff1", "w_ff2", "g_ple", "w_ple_gate", "w_ple"):
        m[name] = sq(inp[name])
    m.update(consts)
    return m


def kernel(_stop=None, **inputs):
    x = np.asarray(inputs["x"])
    B, T = x.shape[0], x.shape[1]
    if T not in _NC_CACHE:
        _NC_CACHE[T] = build_nc(T, _stop)
    nc = _NC_CACHE[T]
    consts = make_consts(T)
    in_maps = [_core_inputs(inputs, b, consts) for b in range(B)]
    res = run_bass_kernel_spmd(nc, in_maps, core_ids=list(range(B)))
    return np.stack([np.asarray(r["out"], dtype=np.float32) for r in res.results], axis=0)
```

```python
import contextlib
import math
import numpy as np
import ml_dtypes
import concourse.bass as bass
import concourse.mybir as mybir
from concourse.bass_utils import run_bass_kernel_spmd

F32 = mybir.dt.float32
BF = mybir.dt.bfloat16
ALU = mybir.AluOpType
AF = mybir.ActivationFunctionType
AX = mybir.AxisListType
DSZ = {F32: 4, BF: 2}

D_MODEL = 1024
IN_COLS = 2328
D_FF = 4096
D_PLE = 256
EPS = 1e-6
NDS = 24
SLOPES = [2.0 ** (-(h + 1)) for h in range(8)]
NEG = -30000.0


class Prog:
    ENGS = ("pe", "act", "dve", "pool", "sp")

    def __init__(self):
        self.ops = {e: [] for e in self.ENGS}
        self.all = []
        self.track = {}
        self.seen = {e: {} for e in self.ENGS}
        self.seen_dma = {e: set() for e in self.ENGS}
        self.ndma = 0
        self.dram = set()
        self.psum = set()
        self.pending = {}

    def barrier(self):
        lasts = {E: self.ops[E][-1]["idx"] for E in self.ENGS if self.ops[E]}
        dmas = {}
        for op in self.all:
            if op["dma"]:
                dmas[op["dsem"]] = op["idx"]
        for X in self.ENGS:
            lst = self.pending.setdefault(X, [])
            for E, d in lasts.items():
                if E != X:
                    lst.append(d)
            lst.extend(dmas.values())

    def box(self, ap):
        name = ap.name
        a = ap.ap
        off = int(ap.offset)
        esz = DSZ.get(ap.dtype, 4)
        if name in self.dram:
            ext = 1
            for st, cnt in a:
                ext += (cnt - 1) * abs(st)
            return name, (0, 1, off * esz, (off + ext) * esz)
        if name in self.psum:
            return name, (0, 128, 0, 2048)
        pstride = a[0][0]
        if pstride == 0:
            p0, f0 = 0, off
        else:
            p0, f0 = off // pstride, off % pstride
        ext = 1
        for st, cnt in a[1:]:
            ext += (cnt - 1) * abs(st)
        return name, (p0, p0 + a[0][1], f0 * esz, (f0 + ext) * esz)

    @staticmethod
    def _ov(a, b):
        return a[0] < b[1] and b[0] < a[1] and a[2] < b[3] and b[2] < a[3]

    @staticmethod
    def _inside(a, b):
        return a[0] >= b[0] and a[1] <= b[1] and a[2] >= b[2] and a[3] <= b[3]

    def add(self, eng, fn, outs, ins, dma=False):
        idx = len(self.all)
        op = dict(eng=eng, fn=fn, waits=[], sig=False, dma=dma, seq=len(self.ops[eng]) + 1, idx=idx)
        deps = {}
        for d in self.pending.pop(eng, []):
            deps[d] = True
        inb = [self.box(a) for a in ins]
        outb = [self.box(a) for a in outs]
        for name, b in inb:
            isps = name in self.psum
            for key, ent in self.track.get(name, {}).items():
                if not self._ov(key[0], b):
                    continue
                if key[2] == "w":
                    deps[ent] = True
                elif isps and key[1] != eng:
                    deps.setdefault(ent, False)
        for name, b in outb:
            tr = self.track.setdefault(name, {})
            for key in list(tr.keys()):
                if self._ov(key[0], b):
                    deps.setdefault(tr[key], False)
                    if self._inside(key[0], b):
                        del tr[key]
        for name, b in inb:
            self.track.setdefault(name, {})[(b, eng, "r")] = idx
        for name, b in outb:
            self.track.setdefault(name, {})[(b, eng, "w")] = idx
        for d in sorted(deps):
            raw = deps[d]
            dop = self.all[d]
            if dop["dma"]:
                if d in self.seen_dma[eng]:
                    continue
                self.seen_dma[eng].add(d)
                op["waits"].append(("d", d))
            else:
                E = dop["eng"]
                if E == eng and not dma:
                    if eng == "pe":
                        continue
                if self.seen[eng].get(E, 0) >= dop["seq"]:
                    continue
                self.seen[eng][E] = dop["seq"]
                dop["sig"] = True
                op["waits"].append(("c", d))
        if dma:
            j = self.ndma
            self.ndma += 1
            op["dsem"] = j % NDS
            op["dval"] = 16 * (j // NDS + 1)
            op["dprev"] = 16 * (j // NDS)
        self.all.append(op)
        self.ops[eng].append(op)
        return op

    def emit(self, nc):
        with contextlib.ExitStack() as st:
            esem = {e: st.enter_context(nc.semaphore("s_" + e)) for e in self.ENGS}
            dsem = [st.enter_context(nc.semaphore("d%d" % i)) for i in range(NDS)]
            for e in self.ENGS:
                c = 0
                for op in self.ops[e]:
                    if op["sig"] and not op["dma"]:
                        c += 1
                        op["cnt"] = c
            dfinal = [0] * NDS
            for op in self.all:
                if op["dma"]:
                    dfinal[op["dsem"]] = max(dfinal[op["dsem"]], op["dval"])
            block = st.enter_context(nc.Block())

            def run(e, eng):
                for op in self.ops[e]:
                    if op["dma"] and op["dprev"] > 0:
                        eng.wait_ge(dsem[op["dsem"]], op["dprev"])
                    for kind, d in op["waits"]:
                        dop = self.all[d]
                        if kind == "d":
                            eng.wait_ge(dsem[dop["dsem"]], dop["dval"])
                        else:
                            eng.wait_ge(esem[dop["eng"]], dop["cnt"])
                    ins = op["fn"](eng)
                    if op["dma"]:
                        ins.then_inc(dsem[op["dsem"]], 16)
                    elif op["sig"]:
                        ins.then_inc(esem[e], 1)
                if e == "sp":
                    for i in range(NDS):
                        if dfinal[i] > 0:
                            eng.wait_ge(dsem[i], dfinal[i])

            block.tensor(lambda eng: run("pe", eng))
            block.scalar(lambda eng: run("act", eng))
            block.vector(lambda eng: run("dve", eng))
            block.gpsimd(lambda eng: run("pool", eng))
            block.sync(lambda eng: run("sp", eng))


def _aps(*xs):
    return [x for x in xs if x is not None and not isinstance(x, (int, float))]


class K:
    def __init__(self, P):
        self.P = P
        self.consts = {}

    def eps_ap(self, val, like):
        t = self.consts[round(float(val), 12)]
        p0 = like.base_partition()
        return t[p0:p0 + like.partition_size(), 0:1]

    def mm(self, out, lhsT, rhs, start=True, stop=True, skip=False):
        if skip:
            self.P.add("pe", lambda e: e.matmul(out, lhsT, rhs, start=start, stop=stop, skip_group_check=True), [out], [lhsT, rhs])
        else:
            self.P.add("pe", lambda e: e.matmul(out, lhsT, rhs, start=start, stop=stop), [out], [lhsT, rhs])

    def tr(self, out, in_, ident):
        self.P.add("pe", lambda e: e.transpose(out, in_, ident), [out], [in_, ident])

    def act(self, out, in_, func, bias=None, scale=None, accum=None):
        kw = {}
        if bias is not None:
            kw["bias"] = bias
        if scale is not None:
            kw["scale"] = scale
        if accum is not None:
            kw["accum_out"] = accum
        self.P.add("act", lambda e: e.activation(out, in_, func, **kw), _aps(out, accum), _aps(in_, bias, scale))

    def ts(self, eng, out, in0, s1, s2, op0, op1=None):
        if op1 is None:
            self.P.add(eng, lambda e: e.tensor_scalar(out, in0, s1, None, op0), [out], _aps(in0, s1))
        else:
            self.P.add(eng, lambda e: e.tensor_scalar(out, in0, s1, s2, op0, op1), [out], _aps(in0, s1, s2))

    def tt(self, eng, out, a, b, op):
        self.P.add(eng, lambda e: e.tensor_tensor(out, a, b, op), [out], [a, b])

    def stt(self, eng, out, in0, scalar, in1, op0, op1):
        self.P.add(eng, lambda e: e.scalar_tensor_tensor(out, in0, scalar, in1, op0, op1), [out], _aps(in0, scalar, in1))

    def rsq(self, out, in_, eps, mul=1.0):
        self.act(out, in_, AF.Ln, bias=float(eps))
        self.act(out, out, AF.Exp, scale=-0.5)

    def cp(self, eng, out, in_):
        if eng == "act":
            self.P.add("act", lambda e: e.copy(out, in_), [out], [in_])
        else:
            self.P.add(eng, lambda e: e.tensor_copy(out, in_), [out], [in_])

    def memset(self, eng, out, val):
        self.P.add(eng, lambda e: e.memset(out, val), [out], [])

    def red(self, out, in_, op=ALU.add):
        self.P.add("dve", lambda e: e.tensor_reduce(out, in_, AX.X, op), [out], [in_])

    def recip(self, out, in_):
        self.P.add("dve", lambda e: e.reciprocal(out, in_), [out], [in_])

    def max8(self, out, in_):
        self.P.add("dve", lambda e: e.max(out, in_), [out], [in_])

    def mrep(self, out, rep, vals, imm):
        self.P.add("dve", lambda e: e.match_replace(out, rep, vals, imm), [out], [rep, vals])

    def dma(self, out, in_, slow=False):
        if slow:
            self.P.add("sp", lambda e: e.dma_start(out=out, in_=in_, allow_slow_non_contiguous=True), [out], [in_], dma=True)
        else:
            self.P.add("sp", lambda e: e.dma_start(out=out, in_=in_), [out], [in_], dma=True)


def make_consts(T):
    NT = T // 128
    NB = T // 64
    NC = T // 16 - 1
    bf = ml_dtypes.bfloat16
    c = {}
    c["c_ident"] = np.eye(128, dtype=np.float32).astype(bf)
    key = np.arange(T)
    E = (key[None, :] // 64 == np.arange(64)[:, None]).astype(np.float32)
    c["c_E"] = E.astype(bf)
    D = 64.0 * (np.arange(64)[:, None] - (key[None, :] // 64))
    c["c_D"] = D.astype(np.float32).astype(bf)
    p = np.arange(128)[:, None]
    f = np.arange(128)[None, :]
    c["c_trile"] = (p <= f).astype(np.float32).astype(bf)
    c["c_trigt"] = (p > f).astype(np.float32).astype(bf)
    W = NC + 8 * (NT - 1)
    m = np.arange(W)[None, :] - 8 * (NT - 1)
    G = np.where(16 * m + 31 <= p, -(p - 16.0 * m - 15.5), -1.0e6)
    c["c_G"] = G.astype(np.float32)
    W2 = NB + 2 * (NT - 1)
    jp = np.arange(W2)[None, :] - 2 * (NT - 1)
    cur = (p >= 64).astype(np.int64)
    Fw = np.where(jp == cur, 2.0e4, np.where(jp == cur - 1, 1.0e4, 0.0))
    Uw = np.where(jp <= cur, 1.0e9, -1.0)
    c["c_Fw"] = Fw.astype(np.float32)
    c["c_Uw"] = Uw.astype(np.float32)
    wb = np.zeros((128, 8), np.float32)
    for h in range(8):
        wb[:, h] = SLOPES[h] * (np.arange(128) % 64)
    c["c_wb"] = wb
    return c


class _Stop(Exception):
    pass


def build_nc(T, stop=None):
    NT = T // 128
    NQ = T // 512
    NB = T // 64
    NC = T // 16 - 1
    NCP = T // 16
    NCT = (NCP + 127) // 128
    WG = NC + 8 * (NT - 1)
    W2 = NB + 2 * (NT - 1)
    nc = bass.Bass("TRN2", target_bir_lowering=False)
    P = Prog()
    k = K(P)

    def din(name, shape, dt=F32):
        P.dram.add(name)
        return nc.dram_tensor(name, list(shape), dt, kind="ExternalInput").ap()

    x_d = din("x", [T, D_MODEL])
    p_d = din("p", [T, D_PLE])
    g_mix_d = din("g_mix", [D_MODEL])
    w_in_d = din("w_in", [D_MODEL, IN_COLS])
    q_g_d = din("q_norm_g", [64])
    kc_g_d = din("kc_norm_g", [64])
    ks_g_d = din("ks_norm_g", [64])
    kw_g_d = din("kw_norm_g", [64])
    pos_k_d = din("cmp_pos_k", [32, 64])
    pos_v_d = din("cmp_pos_v", [32, 64])
    cw1_d = [din("cmp_k_w1", [2048, 256]), din("cmp_v_w1", [2048, 256])]
    cb1_d = [din("cmp_k_b1", [256]), din("cmp_v_b1", [256])]
    cw2_d = [din("cmp_k_w2", [256, 64]), din("cmp_v_w2", [256, 64])]
    cb2_d = [din("cmp_k_b2", [64]), din("cmp_v_b2", [64])]
    ln_g_d = din("gmlp_ln_g", [512])
    ln_b_d = din("gmlp_ln_b", [512])
    ws_d = din("gmlp_ws", [8, 128, 128])
    bs_d = din("gmlp_bs", [8, 128])
    og_nsa_d = din("out_g_nsa", [512])
    og_gmlp_d = din("out_g_gmlp", [512])
    w_out_d = din("w_out", [D_MODEL, D_MODEL])
    g_ff_d = din("g_ff", [D_MODEL])
    w_ff1_d = din("w_ff1", [D_MODEL, D_FF])
    w_ff2_d = din("w_ff2", [D_FF, D_MODEL])
    g_ple_d = din("g_ple", [D_MODEL])
    w_pg_d = din("w_ple_gate", [D_MODEL, D_MODEL])
    w_ple_d = din("w_ple", [D_PLE, D_MODEL])
    c_ident_d = din("c_ident", [128, 128], BF)
    c_E_d = din("c_E", [64, T], BF)
    c_D_d = din("c_D", [64, T], BF)
    c_trile_d = din("c_trile", [128, 128], BF)
    c_trigt_d = din("c_trigt", [128, 128], BF)
    c_G_d = din("c_G", [128, WG])
    c_Fw_d = din("c_Fw", [128, W2])
    c_Uw_d = din("c_Uw", [128, W2])
    c_wb_d = din("c_wb", [128, 8])
    P.dram.add("x1s")
    x1_d = nc.dram_tensor("x1s", [T, D_MODEL], F32, kind="Internal").ap()
    P.dram.add("out")
    out_d = nc.dram_tensor("out", [T, D_MODEL], F32, kind="ExternalOutput").ap()

    ES = contextlib.ExitStack()

    def ck(stage, aps):
        if stop != stage:
            return
        for i, a in enumerate(aps):
            n = a.shape[-1] if len(a.shape) == 2 else None
            d = stg[i % 2]
            k.cp("dve", d[0:a.shape[0], 0:n], a)
            k.dma(out_d[i * 128:i * 128 + a.shape[0], 0:n], d[0:a.shape[0], 0:n])
        raise _Stop()

    cur = [ES]

    def sb(name, shape, dt=F32, st=None):
        return (st or cur[0]).enter_context(nc.sbuf_tensor(name, list(shape), dt))

    def col(d_ap, n):
        return d_ap.rearrange("(p o) -> p o", o=1)

    try:
      with ES:
          ps = [ES.enter_context(nc.psum_tensor("ps%d" % i, [128, 512], F32)) for i in range(8)]
          for i in range(8):
              P.psum.add("ps%d" % i)
          psb = [t[:].bitcast(BF) for t in ps]

          ident = sb("ident", [128, 128], BF)
          k.dma(ident[:], c_ident_d)
          trile = sb("trile", [128, 128], BF)
          k.dma(trile[:], c_trile_d)
          trigt = sb("trigt", [128, 128], BF)
          k.dma(trigt[:], c_trigt_d)
          wb = sb("wb", [128, 8])
          k.dma(wb[:], c_wb_d)
          gcols = sb("gcols", [128, 4, 8])
          k.dma(gcols[:, 0, :], g_mix_d.rearrange("(k p) -> p k", p=128), slow=True)
          k.dma(gcols[:, 1, :], g_ff_d.rearrange("(k p) -> p k", p=128), slow=True)
          k.dma(gcols[:, 2, :], g_ple_d.rearrange("(k p) -> p k", p=128), slow=True)
          k.dma(gcols[:, 3, 0:4], og_nsa_d.rearrange("(k p) -> p k", p=128), slow=True)
          k.dma(gcols[:, 3, 4:8], og_gmlp_d.rearrange("(k p) -> p k", p=128), slow=True)
          eps_c = sb("eps_c", [128, 1]); k.memset("dve", eps_c[:], EPS)
          stg = [sb("stg%d" % i, [128, 2048]) for i in range(2)]
          cnt = [0]

          def load_w(dst_fn, src, nk, ncols, gcol=None, segs=None, stages=None):
              if segs is None:
                  segs = [(0, ncols, 0)]
              if stages is None:
                  stages = stg
              for kk in range(nk):
                  for c0 in range(0, ncols, 2048):
                      c1 = min(ncols, c0 + 2048)
                      s = stages[cnt[0] % len(stages)]
                      cnt[0] += 1
                      k.dma(s[:, 0:c1 - c0], src[kk * 128:(kk + 1) * 128, c0:c1])
                      for (a0, a1, d0) in segs:
                          lo, hi = max(a0, c0), min(a1, c1)
                          if lo >= hi:
                              continue
                          dst = dst_fn(kk, d0 + lo - a0, d0 + hi - a0)
                          if gcol is None:
                              k.cp(("act", "dve")[cnt[0] % 2], dst, s[:, lo - c0:hi - c0])
                          else:
                              k.act(dst, s[:, lo - c0:hi - c0], AF.Copy, scale=gcol[:, kk:kk + 1])

          SATT = contextlib.ExitStack()
          cur[0] = SATT
          Gw = sb("Gw", [128, WG])
          k.dma(Gw[:], c_G_d)
          Fw = sb("Fw", [128, W2])
          k.dma(Fw[:], c_Fw_d)
          Uw = sb("Uw", [128, W2])
          k.dma(Uw[:], c_Uw_d)
          Dt = sb("Dt", [128, T], BF)
          k.dma(Dt[64:128, :], c_D_d)
          KEs = [sb("KEs%d" % g, [128, T], BF) for g in range(2)]
          KEw = [sb("KEw%d" % g, [128, T], BF) for g in range(2)]
          for t_ in KEs + KEw:
              k.dma(t_[64:128, :], c_E_d)
          Vall = sb("Vall", [128, NT, 4, 66], BF)
          k.memset("dve", Vall[:], 1.0)
          KcT = [sb("KcT%d" % g, [64, NCT * 128], BF) for g in range(2)]
          Vc = [sb("Vc%d" % g, [128, NCT, 64], BF) for g in range(2)]
          gates = sb("gates", [128, NT, 24])
          rstd_g = sb("rstd_g", [128, NT])
          gq = sb("gq", [64, 1]); k.dma(gq[:], col(q_g_d, 64))
          gks8 = sb("gks8", [64, 1]); k.dma(gks8[:], col(ks_g_d, 64))
          gkw8 = sb("gkw8", [64, 1]); k.dma(gkw8[:], col(kw_g_d, 64))
          k.ts("dve", gks8[:], gks8[:], 8.0, None, ALU.mult)
          k.ts("dve", gkw8[:], gkw8[:], 8.0, None, ALU.mult)
          lng = sb("lng", [128, 512]); k.dma(lng[:], ln_g_d.partition_broadcast(128))
          lnb = sb("lnb", [128, 512]); k.dma(lnb[:], ln_b_d.partition_broadcast(128))
          bstab = sb("bstab", [128, 8]); k.dma(bstab[:], bs_d.rearrange("g t -> t g"), slow=True)

          def front(xt, hT, gidx_unused, junk, ssum, rs, hb, tsl):
              k.act(junk[:, 0:1024], xt, AF.Square, accum=ssum[:])
              k.rsq(rs[:], ssum[:], float(D_MODEL * EPS))
              k.ts("dve", hb[:], xt, rs[:, 0:1], 32.0, ALU.mult, ALU.mult)
              for kk in range(8):
                  k.tr(psb[4][:, kk * 128:(kk + 1) * 128], hb[:, kk * 128:(kk + 1) * 128], ident[:])
              k.cp("act", hT[:, :, tsl], psb[4][:, 0:1024].rearrange("p (k t) -> p k t", k=8))

          def gelu_to(out, src_ps, n, junk_a, junk_b, accum=None, p0=0, p1=128, bias=None):
              xv = junk_a[p0:p1, 0:n]
              sq = junk_b[p0:p1, 0:n]
              if bias is None:
                  k.act(xv, src_ps, AF.Copy)
                  k.act(sq, src_ps, AF.Square)
              else:
                  k.ts("dve", xv, src_ps, bias, None, ALU.add)
                  k.act(sq, src_ps, AF.Square, bias=bias)
              k.ts("dve", sq, sq, 0.044715, 1.0, ALU.mult, ALU.add)
              k.tt("pool", sq, sq, xv, ALU.mult)
              k.act(sq, sq, AF.Exp, scale=-1.5957691216)
              k.ts("dve", sq, sq, 1.0, None, ALU.add)
              k.recip(sq, sq)
              if accum is None:
                  k.tt("dve", out, sq, xv, ALU.mult)
              else:
                  P.add("dve", lambda e: e.scalar_tensor_tensor(out, sq, 1.0, xv, ALU.mult, ALU.mult, accum_out=accum),
                        [out, accum], [sq, xv])

          SA = contextlib.ExitStack()
          with SA:
              w_inA = sb("w_inA", [128, 8, 768], BF, SA)
              kvT = sb("kvT", [128, 2, T], BF, SA)
              segsA = [(0, 384, 0), (384, 512, 512), (512, 640, 384), (640, 768, 640)]
              kvTf = kvT[:].rearrange("p a t -> p (a t)").bitcast(F32)
              load_w(lambda kk, a, b: w_inA[:, kk, a:b], w_in_d[:, 512:1280], 8, 768, gcols[:, 0, :], segsA,
                     stages=[stg[0], stg[1], kvTf[:, 0:2048], kvTf[:, 2048:4096]])
              ck(1, [w_inA[:, 0, 0:512], w_inA[:, 7, 256:768]])
              xtA = [sb("xtA%d" % i, [128, 1024], F32, SA) for i in range(2)]
              hTA = [sb("hTA%d" % i, [128, 8, 128], BF, SA) for i in range(2)]
              junkA = sb("junkA", [128, 1024], F32, SA)
              hbA = sb("hbA", [128, 1024], BF, SA)
              ssA = sb("ssA", [128, 1], F32, SA)
              rsA = sb("rsA", [128, 1], F32, SA)
              ssk = sb("ssk", [128, 4], F32, SA)
              zbA = sb("zbA", [128, 512], BF, SA)
              def front_a(tt_):
                  xt = xtA[tt_ % 2]
                  hT = hTA[tt_ % 2]
                  k.dma(xt[:], x_d[tt_ * 128:(tt_ + 1) * 128, :])
                  front(xt[:], hT, 0, junkA, ssA, rsA, hbA, slice(0, 128))
                  b0, b1 = ps[2 * (tt_ % 2)], ps[2 * (tt_ % 2) + 1]
                  for kk in range(8):
                      k.mm(b0[:, 0:512], hT[:, kk, :], w_inA[:, kk, 0:512], start=(kk == 0), stop=(kk == 7))
                      k.mm(b1[:, 0:256], hT[:, kk, :], w_inA[:, kk, 512:768], start=(kk == 0), stop=(kk == 7))

              def post_a(tt_):
                  b0, b1 = ps[2 * (tt_ % 2)], ps[2 * (tt_ % 2) + 1]
                  k.act(junkA[:, 0:256], b0[:, 256:512], AF.Square)
                  k.red(ssk[:], junkA[:, 0:256].rearrange("p (a d) -> p a d", d=64))
                  k.rsq(ssk[:], ssk[:], 64.0 * EPS)
                  k.cp("act", zbA[:, 0:256], b0[:, 0:256])
                  k.tt("dve", zbA[:, 256:512].rearrange("p (a d) -> p a d", d=64),
                       b0[:, 256:512].rearrange("p (a d) -> p a d", d=64),
                       ssk[:].unsqueeze(2).to_broadcast([128, 4, 64]), ALU.mult)
                  k.cp("act", Vall[:, tt_, :, 0:64], b1[:, 0:256].rearrange("p (a d) -> p a d", d=64))
                  k.tr(psb[5][:, 0:128], zbA[:, 0:128], ident[:])
                  k.tr(psb[5][:, 128:256], zbA[:, 128:256], ident[:])
                  for a in range(4):
                      k.tr(psb[5][0:64, 256 + a * 128:384 + a * 128], zbA[:, 256 + a * 64:320 + a * 64], ident[:])
                  tsl = slice(tt_ * 128, (tt_ + 1) * 128)
                  k.cp("dve", kvT[:, :, tsl], psb[5][:, 0:256].rearrange("p (a t) -> p a t", a=2))
                  for g in range(2):
                      k.ts("dve", KEs[g][0:64, tsl], psb[5][0:64, 256 + g * 128:384 + g * 128], gks8[:, 0:1], None, ALU.mult)
                      k.ts("dve", KEw[g][0:64, tsl], psb[5][0:64, 512 + g * 128:640 + g * 128], gkw8[:, 0:1], None, ALU.mult)

              front_a(0)
              for tt_ in range(NT):
                  if tt_ + 1 < NT:
                      front_a(tt_ + 1)
                  post_a(tt_)

              ck(2, [KEs[0][:, 0:1024], kvT[:, 0, 0:1024], KEw[1][:, 0:1024], Vall[:, 3, :, :].rearrange("p a d -> p (a d)")])
              SC = contextlib.ExitStack()
              with SC:
                  w1r = sb("w1r", [128, 32, 256], BF, SC)
                  w2b = sb("w2b", [128, 2, 64], BF, SC)
                  posT = sb("posT", [128, 32], BF, SC)
                  posf = sb("posf", [128, 32], F32, SC)
                  b1c = sb("b1c", [128, 2], F32, SC)
                  bias1 = sb("bias1", [128, 2], F32, SC)
                  b2t = sb("b2t", [128, 64], F32, SC)
                  gkc = sb("gkc", [128, 64], F32, SC)
                  hdn = sb("hdn", [128, 2, NCT * 128], BF, SC)
                  ja = sb("cja", [128, 256], F32, SC)
                  jb = sb("cjb", [128, 256], F32, SC)
                  kcf = sb("kcf", [128, 64], F32, SC)
                  kcb = sb("kcb", [128, 64], BF, SC)
                  ssc = sb("ssc", [128, 1], F32, SC)
                  k.dma(gkc[:], kc_g_d.partition_broadcast(128))
                  k.memset("dve", hdn[:], 0.0)
                  for kv in range(2):
                      w1v = cw1_d[kv].rearrange("(l d) h -> d l h", d=64)
                      for half in range(2):
                          for lc in range(4):
                              s = stg[cnt[0] % 2]
                              cnt[0] += 1
                              k.dma(s[half * 64:(half + 1) * 64, :].rearrange("p (l h) -> p l h", h=256),
                                    w1v[:, lc * 8:(lc + 1) * 8, :])
                              k.cp(("act", "dve")[lc % 2],
                                   w1r[half * 64:(half + 1) * 64, lc * 8:(lc + 1) * 8, :],
                                   s[half * 64:(half + 1) * 64, :].rearrange("p (l h) -> p l h", h=256))
                      s = stg[cnt[0] % 2]
                      cnt[0] += 1
                      k.dma(s[:, 0:128].rearrange("p (c o) -> p c o", c=2), cw2_d[kv].rearrange("(c p) o -> p c o", p=128))
                      k.cp("dve", w2b[:], s[:, 0:128].rearrange("p (c o) -> p c o", c=2))
                      pos_d = (pos_k_d, pos_v_d)[kv]
                      for half in range(2):
                          k.dma(posf[half * 64:(half + 1) * 64, :], pos_d.rearrange("l d -> d l"), slow=True)
                      k.cp("dve", posT[:], posf[:])
                      k.dma(b1c[:], cb1_d[kv].rearrange("(c p) -> p c", p=128), slow=True)
                      k.dma(b2t[:], cb2_d[kv].partition_broadcast(128))
                      for hh in range(2):
                          for l in range(32):
                              k.mm(ps[6][:, hh:hh + 1], w1r[0:64, l, hh * 128:(hh + 1) * 128], posT[0:64, l:l + 1],
                                   start=(l == 0), stop=(l == 31))
                      k.tt("dve", bias1[:], ps[6][:, 0:2], b1c[:], ALU.add)
                      for g in range(2):
                          pr = slice(g * 64, (g + 1) * 64)
                          for hh in range(2):
                              for l in range(32):
                                  k.mm(ps[hh][:, 0:NC], w1r[pr, l, hh * 128:(hh + 1) * 128],
                                       kvT[pr, kv, l:l + 16 * (NC - 1) + 1:16], start=(l == 0), stop=(l == 31))
                          for hh in range(2):
                              for c0 in range(0, NC, 256):
                                  c1 = min(NC, c0 + 256)
                                  gelu_to(hdn[:, hh, c0:c1], ps[hh][:, c0:c1], c1 - c0, ja, jb, bias=bias1[:, hh:hh + 1])
                          for ct in range(NCT):
                              ncv = min(128, NC - ct * 128)
                              for hh in range(2):
                                  k.mm(ps[2][0:ncv, 0:64], hdn[:, hh, ct * 128:ct * 128 + ncv], w2b[:, hh, :],
                                       start=(hh == 0), stop=(hh == 1))
                              k.tt("dve", kcf[0:ncv, :], ps[2][0:ncv, 0:64], b2t[0:ncv, :], ALU.add)
                              if kv == 1:
                                  if ncv < 128:
                                      k.memset("dve", Vc[g][:, ct, :], 0.0)
                                  k.cp("dve", Vc[g][0:ncv, ct, :], kcf[0:ncv, :])
                              else:
                                  k.act(ja[0:ncv, 0:64], kcf[0:ncv, :], AF.Square, accum=ssc[0:ncv, :])
                                  k.rsq(ssc[0:ncv, :], ssc[0:ncv, :], 64.0 * EPS)
                                  k.ts("dve", kcf[0:ncv, :], kcf[0:ncv, :], ssc[0:ncv, 0:1], 8.0, ALU.mult, ALU.mult)
                                  if ncv < 128:
                                      k.memset("dve", kcb[:], 0.0)
                                  k.tt("dve", kcb[0:ncv, :], kcf[0:ncv, :], gkc[0:ncv, :], ALU.mult)
                                  k.tr(psb[5][0:64, 0:128], kcb[:, :], ident[:])
                                  k.cp("dve", KcT[g][:, ct * 128:(ct + 1) * 128], psb[5][0:64, 0:128])

          P.barrier()
          ck(3, [KcT[0][:, 0:128], KcT[1][:, 0:128], Vc[0][:, 0, :], Vc[1][:, 0, :]])
          SB_ = contextlib.ExitStack()
          with SB_:
              w_inB = sb("w_inB", [128, 8, 1560], BF, SB_)
              w_outb = sb("w_outb", [128, 8, 1024], BF, SB_)
              xs = sb("xs", [128, 4, 1024], F32, SB_)
              xsf = xs[:].rearrange("p a b -> p (a b)")
              st4b = [stg[0], stg[1], xsf[:, 0:2048], xsf[:, 2048:4096]]
              load_w(lambda kk, a, b: w_inB[:, kk, a:b], w_in_d[:, 0:512], 8, 512, gcols[:, 0, :], [(0, 512, 0)], stages=st4b)
              load_w(lambda kk, a, b: w_inB[:, kk, a:b], w_in_d[:, 1280:2328], 8, 1048, gcols[:, 0, :],
                     [(0, 24, 1536), (24, 1048, 512)], stages=st4b)
              load_w(lambda kk, a, b: w_outb[:, kk, a:b], w_out_d, 4, 1024, gcols[:, 3, 0:4], stages=st4b)
              load_w(lambda kk, a, b: w_outb[:, 4 + kk, a:b], w_out_d[512:1024, :], 4, 1024, gcols[:, 3, 4:8], stages=st4b)
              WmT = sb("WmT", [128, 8, 128], BF, SB_)
              wsf = sb("wsf", [128, 128], F32, SB_)
              wsb = sb("wsb", [128, 128], BF, SB_)
              hTB = sb("hTB", [128, 8, 128], BF, SB_)
              junkB = stg[0][:, 1024:2048]
              junkC = stg[1][:, 1024:2048]
              hbB = sb("hbB", [128, 1024], BF, SB_)
              ssB = sb("ssB", [128, 1], F32, SB_)
              rsB = sb("rsB", [128, 1], F32, SB_)
              ssq = sb("ssq", [128, 8], F32, SB_)
              qnb = sb("qnb", [128, 512], BF, SB_)
              R = sb("R", [128, 8, 512], BF, SB_)
              u_sb = sb("u_sb", [128, 512], F32, SB_)
              v_sb = sb("v_sb", [128, 512], F32, SB_)
              vnb = sb("vnb", [128, 512], BF, SB_)
              st1 = sb("st1", [128, 4], F32, SB_)
              gof = v_sb
              gob = sb("gob", [128, 512], BF, SB_)
              goT = sb("goT", [128, 4, 512], BF, SB_)
              aoT = sb("aoT", [128, 4, 128], BF, SB_)
              ao = sb("ao", [128, 4, 512], F32, SB_)
              aob = qnb
              rstd_a = sb("rstd_a", [128, 1], F32, SB_)
              sc = sb("sc", [128, 2, 256], F32, SB_)
              ef = sb("ef", [128, 2, 8, 256], BF, SB_)
              ebT = sb("ebT", [128, 2, NCT, 128], BF, SB_)
              csum = sb("csum", [128, 2, 8], F32, SB_)
              crin = sb("crin", [128, 2, 8], F32, SB_)
              icp = sb("icp", [128, 2, NCP + 1], F32, SB_)
              imp = sb("imp", [128, 2, 64], F32, SB_)
              impw = sb("impw", [128, 2, 64], F32, SB_)
              m8 = sb("m8", [128, 2, 16], F32, SB_)
              bm = sb("bm", [128, 2, 128], BF, SB_)
              mbT = sb("mbT", [128, 2, 512], F32, SB_)
              ocs = sb("ocs", [128, 4, 8, 64], F32, SB_)
              pti = [0]
              PT = [sb("PT%d" % i, [128, 512], BF, SB_) for i in range(3)]
              fsum = sb("fsum", [128, 2, 4], F32, SB_)
              coef = sb("coef", [128, 3, 4], F32, SB_)
              tmpo = u_sb[:, 0:256].rearrange("p (a d) -> p a d", d=64)
              x1t = [stg[0][:, 0:1024], stg[1][:, 0:1024]]
              k.memset("dve", icp[:], 0.0)
              k.memset("dve", bm[:], 0.0)
              k.memset("dve", ef[:], 0.0)
              for g in range(8):
                  k.dma(wsf[:], ws_d[g])
                  k.tr(psb[5][:, 0:128], trile[:], ident[:])
                  k.tt("dve", wsb[:], wsf[:], psb[5][:, 0:128], ALU.mult)
                  k.tr(psb[5][:, 128:256], wsb[:], ident[:])
                  k.cp("dve", WmT[:, g, :], psb[5][:, 128:256])

              for Q in range(NQ):
                  ju_a, ju_b = stg[0][:, 0:512], stg[0][:, 512:1024]
                  jv_a, jv_b = stg[0][:, 1024:1536], stg[0][:, 1536:2048]
                  fjunk, qsq, dummy = stg[1][:, 0:1024], stg[1][:, 1024:1536], stg[1][:, 1536:2048]

                  def front_b(qs):
                      tt_ = Q * 4 + qs
                      xt = xs[:, qs, :]
                      k.dma(xt, x_d[tt_ * 128:(tt_ + 1) * 128, :])
                      front(xt, hTB, 0, fjunk, ssB, rsB, hbB, slice(0, 128))
                      for kk in range(8):
                          for cb in range(3):
                              k.mm(ps[cb][:, 0:512], hTB[:, kk, :], w_inB[:, kk, cb * 512:(cb + 1) * 512],
                                   start=(kk == 0), stop=(kk == 7))
                          k.mm(ps[3][:, 0:24], hTB[:, kk, :], w_inB[:, kk, 1536:1560], start=(kk == 0), stop=(kk == 7))

                  def post1_b(qs):
                      tt_ = Q * 4 + qs
                      k.act(qsq, ps[0][:, 0:512], AF.Square)
                      k.red(ssq[:], qsq.rearrange("p (a d) -> p a d", d=64))
                      k.rsq(ssq[:], ssq[:], 64.0 * EPS)
                      k.tt("dve", qnb[:].rearrange("p (a d) -> p a d", d=64),
                           ps[0][:, 0:512].rearrange("p (a d) -> p a d", d=64),
                           ssq[:].unsqueeze(2).to_broadcast([128, 8, 64]), ALU.mult)
                      k.act(gates[:, tt_, :], ps[3][:, 0:24], AF.Exp, scale=-1.0)
                      k.act(ju_a, ps[1][:, 0:512], AF.Copy)
                      k.act(ju_b, ps[1][:, 0:512], AF.Square)
                      k.act(jv_a, ps[2][:, 0:512], AF.Copy)
                      k.act(jv_b, ps[2][:, 0:512], AF.Square)

                  def post2_b(qs):
                      tt_ = Q * 4 + qs
                      k.ts("dve", gates[:, tt_, :], gates[:, tt_, :], 1.0, None, ALU.add)
                      k.recip(gates[:, tt_, :], gates[:, tt_, :])
                      prs = ((ju_a, ju_b), (jv_a, jv_b))
                      for xv, sq in prs:
                          k.ts("dve", sq, sq, 0.044715, 1.0, ALU.mult, ALU.add)
                      for xv, sq in prs:
                          k.tt("dve", sq, sq, xv, ALU.mult)
                      for xv, sq in prs:
                          k.act(sq, sq, AF.Exp, scale=-1.5957691216)
                      for xv, sq in prs:
                          k.act(sq, sq, AF.Ln, bias=1.0)
                      for xv, sq in prs:
                          k.act(sq, sq, AF.Exp, scale=-1.0)
                      k.tt("pool", u_sb[:], ju_b, ju_a, ALU.mult)
                      P.add("dve", lambda e: e.scalar_tensor_tensor(v_sb[:], jv_b, 1.0, jv_a, ALU.mult, ALU.mult, accum_out=st1[:, 0:1]),
                            [v_sb[:], st1[:, 0:1]], [jv_b, jv_a])
                      k.act(dummy, v_sb[:], AF.Square, accum=st1[:, 1:2])
                      k.ts("dve", st1[:, 2:3], st1[:, 0:1], 1.0 / 512, None, ALU.mult)
                      k.stt("dve", st1[:, 3:4], st1[:, 2:3], -1.0, st1[:, 2:3], ALU.mult, ALU.mult)
                      k.stt("dve", st1[:, 3:4], st1[:, 1:2], 1.0 / 512, st1[:, 3:4], ALU.mult, ALU.add)
                      k.rsq(st1[:, 3:4], st1[:, 3:4], EPS)
                      k.ts("dve", v_sb[:], v_sb[:], st1[:, 2:3], st1[:, 3:4], ALU.subtract, ALU.mult)
                      k.tt("dve", v_sb[:], v_sb[:], lng[:], ALU.mult)
                      k.tt("dve", vnb[:], v_sb[:], lnb[:], ALU.add)
                      for g in range(8):
                          k.mm(ps[6][:, g * 64:(g + 1) * 64], WmT[:, g, :], vnb[:, g * 64:(g + 1) * 64])
                      k.tt("dve", gof[:].rearrange("p (a d) -> p a d", d=64),
                           ps[6][:, 0:512].rearrange("p (a d) -> p a d", d=64),
                           bstab[:].unsqueeze(2).to_broadcast([128, 8, 64]), ALU.add)
                      k.tt("pool", gob[:], gof[:], u_sb[:], ALU.mult)
                      k.act(dummy, gob[:], AF.Square, accum=rstd_g[:, tt_:tt_ + 1])
                      k.rsq(rstd_g[:, tt_:tt_ + 1], rstd_g[:, tt_:tt_ + 1], 512.0 * EPS)
                      k.ts("dve", rstd_g[:, tt_:tt_ + 1], rstd_g[:, tt_:tt_ + 1], 22.627416998, None, ALU.mult)
                      for c4 in range(4):
                          k.tr(psb[7][:, c4 * 128:(c4 + 1) * 128], gob[:, c4 * 128:(c4 + 1) * 128], ident[:])
                      k.cp("act", goT[:, :, qs * 128:(qs + 1) * 128], psb[7][:, 0:512].rearrange("p (c t) -> p c t", c=4))
                      for h in range(8):
                          k.tr(psb[5][0:64, h * 128:(h + 1) * 128], qnb[:, h * 64:(h + 1) * 64], ident[:])
                      k.ts("dve", R[0:64, :, qs * 128:(qs + 1) * 128],
                           psb[5][0:64, 0:1024].rearrange("p (h t) -> p h t", h=8), gq[:, 0:1], None, ALU.mult)

                  front_b(0)
                  for qs in range(4):
                      post1_b(qs)
                      if qs + 1 < 4:
                          front_b(qs + 1)
                      post2_b(qs)

                  ck(4, [R[:, 0, :], R[:, 7, :], gof[:], goT[:, 0, :], gates[:, 0:4, :].rearrange("p a d -> p (a d)")])

                  def make_sel(qs):
                      qt = Q * 4 + qs
                      qp = qs % 2
                      ocb = ps[2 + qp]
                      th = []
                      th.append(lambda: k.ts("dve", csum[:, qp, :], csum[:, qp, :], 1e-30, None, ALU.max))
                      th.append(lambda: k.recip(crin[:, qp, :], csum[:, qp, :]))
                      th.append(lambda: k.tt("dve", ocs[:, qs, :, :], ocb[:, 0:512].rearrange("p (h d) -> p h d", h=8),
                                             crin[:, qp, :].unsqueeze(2).to_broadcast([128, 8, 64]), ALU.mult))
                      span = 4 * (NB - 1) + 1
                      for g in range(2):
                          th.append(lambda g=g: k.ts("dve", icp[:, g, 1:1 + NC], ef[:, qp, 4 * g, 0:NC],
                                                     crin[:, qp, 4 * g:4 * g + 1], None, ALU.mult))
                          for r in range(1, 4):
                              th.append(lambda g=g, h=4 * g + r: k.stt("dve", icp[:, g, 1:1 + NC], ef[:, qp, h, 0:NC],
                                                                       crin[:, qp, h:h + 1], icp[:, g, 1:1 + NC], ALU.mult, ALU.add))
                          th.append(lambda g=g: k.cp("dve", imp[:, g, 0:NB], icp[:, g, 0:span:4]))
                          for kk_, wk in ((1, 2.0), (2, 2.0), (3, 2.0), (4, 1.0)):
                              th.append(lambda g=g, kk_=kk_, wk=wk: k.stt("dve", imp[:, g, 0:NB], icp[:, g, kk_:kk_ + span:4], wk,
                                                                          imp[:, g, 0:NB], ALU.mult, ALU.add))
                      if NB < 64:
                          th.append(lambda: k.memset("dve", imp[:, :, NB:64], -1.0))
                      f0 = 2 * (NT - 1) - 2 * qt
                      for g in range(2):
                          th.append(lambda g=g: k.tt("dve", imp[:, g, 0:NB], imp[:, g, 0:NB], Fw[:, f0:f0 + NB], ALU.max))
                          th.append(lambda g=g: k.tt("dve", imp[:, g, 0:NB], imp[:, g, 0:NB], Uw[:, f0:f0 + NB], ALU.min))
                      th.append(lambda: k.memset("dve", imp[:, :, 0:1], 3.0e4))
                      for g in range(2):
                          th.append(lambda g=g: k.max8(m8[:, g, 0:8], imp[:, g, :]))
                          th.append(lambda g=g: k.mrep(impw[:, g, :], m8[:, g, 0:8], imp[:, g, :], -2.0))
                          th.append(lambda g=g: k.max8(m8[:, g, 8:16], impw[:, g, :]))
                          th.append(lambda g=g: k.ts("dve", impw[:, g, :], imp[:, g, :], m8[:, g, 15:16], None, ALU.is_ge))
                          th.append(lambda g=g: k.ts("dve", bm[:, g, 64:128], impw[:, g, :], -NEG, NEG, ALU.mult, ALU.add))
                          th.append(lambda g=g: k.tr(psb[5][:, g * 128:(g + 1) * 128], bm[:, g, :], ident[:]))
                          th.append(lambda g=g: k.cp("dve", mbT[64:128, g, qs * 128:(qs + 1) * 128],
                                                     psb[5][64:128, g * 128:(g + 1) * 128]))
                      return th

                  def heads(qs, pending):
                      qt = Q * 4 + qs
                      qp = qs % 2
                      nctq = min(NCT, (8 * qt + 7 + 127) // 128)
                      gs0 = 8 * (NT - 1) - 8 * qt
                      ocb = ps[2 + qp]

                      def stage_a(h):
                          g = h // 4
                          sbk = ps[0] if h % 2 == 0 else ps[6]
                          k.mm(sbk[:, 0:NC], R[0:64, h, qs * 128:(qs + 1) * 128], KcT[g][:, 0:NC])
                          k.stt("dve", sc[:, h % 2, 0:NC], Gw[:, gs0:gs0 + NC], SLOPES[h], sbk[:, 0:NC], ALU.mult, ALU.add)
                          k.act(ef[:, qp, h, 0:NC], sc[:, h % 2, 0:NC], AF.Exp, accum=csum[:, qp, h:h + 1])

                      def stage_b(h):
                          g = h // 4
                          tb = psb[1] if h % 2 == 0 else psb[7]
                          for ct in range(nctq):
                              k.tr(tb[:, ct * 128:(ct + 1) * 128], ef[:, qp, h, ct * 128:(ct + 1) * 128], ident[:])
                          k.cp("act", ebT[:, h % 2, 0:nctq, :], tb[:, 0:nctq * 128].rearrange("p (c t) -> p c t", c=nctq))
                          for ct in range(nctq):
                              k.mm(ocb[:, h * 64:(h + 1) * 64], ebT[:, h % 2, ct, :], Vc[g][:, ct, :],
                                   start=(ct == 0), stop=(ct == nctq - 1))

                      stage_a(0)
                      for h in range(8):
                          if h + 1 < 8:
                              stage_a(h + 1)
                          stage_b(h)
                          nd = (len(pending) + (7 - h)) // (8 - h)
                          for _ in range(nd):
                              pending.pop(0)()

                  pending = []
                  for qs in range(4):
                      heads(qs, pending)
                      assert not pending
                      pending = make_sel(qs)
                  for th_ in pending:
                      th_()

                  ck(5, [mbT[:, 0, :], mbT[:, 1, :], ocs[:, 3, :, :].rearrange("p a d -> p (a d)"), imp[:, 0, :]])
                  qcols = slice(Q * 512, (Q + 1) * 512)
                  gsl = slice(4 * Q, 4 * Q + 4)
                  tasks = []
                  for h in range(8):
                      for br in range(2):
                          kt_lo = max(0, 4 * Q - 4) if br == 0 else 0
                          kts = list(range(kt_lo, 4 * Q + 4))
                          for kt in kts:
                              tasks.append(dict(h=h, br=br, kt=kt, gfirst=(kt == kts[0]), glast=(kt == kts[-1]), n=len(tasks)))

                  def obank(h, br):
                      return (ps[6 + br] if h % 2 == 0 else ps[br])

                  def emit_S(t):
                      h, br, kt = t["h"], t["br"], t["kt"]
                      g = h // 4
                      if t["gfirst"]:
                          if br == 0:
                              k.ts("dve", R[64:128, h, :], Dt[64:128, qcols], SLOPES[h], None, ALU.mult)
                          else:
                              k.tt("dve", R[64:128, h, :], R[64:128, h, :], mbT[64:128, g, :], ALU.add)
                      i = kt - 4 * Q
                      c_lo = max(0, i)
                      c_hi = min(3, i + 4) if br == 0 else 3
                      t["c"] = (c_lo, c_hi)
                      n0, n1 = c_lo * 128, (c_hi + 1) * 128
                      KE = (KEw, KEs)[br][g]
                      sbank = (ps[3], ps[4], ps[5])[t["n"] % 3]
                      k.mm(sbank[:, n0:n1], KE[:, kt * 128:(kt + 1) * 128], R[:, h, n0:n1])

                  def emit_rest(t):
                      h, br, kt = t["h"], t["br"], t["kt"]
                      g = h // 4
                      i = kt - 4 * Q
                      c_lo, c_hi = t["c"]
                      n0, n1 = c_lo * 128, (c_hi + 1) * 128
                      sbank = (ps[3], ps[4], ps[5])[t["n"] % 3]
                      pt = PT[t["n"] % len(PT)]
                      Ob = obank(h, br)
                      vidx = (2 + g, g)[br]
                      k.act(pt[:, n0:n1], sbank[:, n0:n1], AF.Exp, bias=wb[:, h:h + 1])
                      if i >= 0:
                          k.tt("dve", pt[:, i * 128:(i + 1) * 128], pt[:, i * 128:(i + 1) * 128], trile[:], ALU.mult)
                      if br == 0 and 0 <= i + 4 <= 3:
                          cc = i + 4
                          k.tt("dve", pt[:, cc * 128:(cc + 1) * 128], pt[:, cc * 128:(cc + 1) * 128], trigt[:], ALU.mult)
                      for c in range(c_lo, c_hi + 1):
                          k.mm(Ob[:, c * 65:(c + 1) * 65], pt[:, c * 128:(c + 1) * 128], Vall[:, kt, vidx, 0:65],
                               start=(t["gfirst"] and c == c_lo), stop=(kt == 4 * Q + c), skip=True)
                      if br == 1 and t["glast"]:
                          Ow = obank(h, 0)[:, 0:260].rearrange("p (c d) -> p c d", d=65)
                          Os = obank(h, 1)[:, 0:260].rearrange("p (c d) -> p c d", d=65)
                          k.ts("dve", fsum[:, 0, :], Ow[:, :, 64], 1e-30, None, ALU.max)
                          k.ts("dve", fsum[:, 1, :], Os[:, :, 64], 1e-30, None, ALU.max)
                          k.recip(fsum[:], fsum[:])
                          k.tt("dve", coef[:, 0, :], fsum[:, 0, :], gates[:, gsl, 3 * h + 2], ALU.mult)
                          k.tt("dve", coef[:, 1, :], fsum[:, 1, :], gates[:, gsl, 3 * h + 1], ALU.mult)
                          dst = ao[:, :, h * 64:(h + 1) * 64]
                          k.tt("dve", dst, ocs[:, :, h, :], gates[:, gsl, 3 * h:3 * h + 1].to_broadcast([128, 4, 64]), ALU.mult)
                          k.tt("dve", tmpo, Ow[:, :, 0:64], coef[:, 0, :].unsqueeze(2).to_broadcast([128, 4, 64]), ALU.mult)
                          k.tt("dve", dst, dst, tmpo, ALU.add)
                          k.tt("dve", tmpo, Os[:, :, 0:64], coef[:, 1, :].unsqueeze(2).to_broadcast([128, 4, 64]), ALU.mult)
                          k.tt("dve", dst, dst, tmpo, ALU.add)

                  emit_S(tasks[0])
                  if len(tasks) > 1:
                      emit_S(tasks[1])
                  for n_, t in enumerate(tasks):
                      if n_ + 2 < len(tasks):
                          emit_S(tasks[n_ + 2])
                      emit_rest(t)

                  ck(6, [ao[:, 0, :], ao[:, 3, :]])
                  for qs in range(4):
                      tt_ = Q * 4 + qs
                      x1 = x1t[tt_ % 2]
                      k.act(junkB[:, 0:512], ao[:, qs, :], AF.Square, accum=rstd_a[:])
                      k.rsq(rstd_a[:], rstd_a[:], 512.0 * EPS)
                      k.ts("dve", rstd_a[:], rstd_a[:], 22.627416998, None, ALU.mult)
                      k.cp("pool", aob[:], ao[:, qs, :])
                      for c4 in range(4):
                          k.tr(psb[5][:, c4 * 128:(c4 + 1) * 128], aob[:, c4 * 128:(c4 + 1) * 128], ident[:])
                      k.cp("act", aoT[:], psb[5][:, 0:512].rearrange("p (c t) -> p c t", c=4))
                      for half in range(2):
                          hs = slice(half * 512, (half + 1) * 512)
                          for c4 in range(4):
                              k.mm(ps[half][:, :], aoT[:, c4, :], w_outb[:, c4, hs], start=(c4 == 0), stop=(c4 == 3))
                          for c4 in range(4):
                              k.mm(ps[2 + half][:, :], goT[:, c4, qs * 128:(qs + 1) * 128], w_outb[:, 4 + c4, hs],
                                   start=(c4 == 0), stop=(c4 == 3))
                          k.stt("dve", x1[:, hs], ps[half][:, :], rstd_a[:, 0:1], xs[:, qs, hs], ALU.mult, ALU.add)
                          k.stt("dve", x1[:, hs], ps[2 + half][:, :], rstd_g[:, tt_:tt_ + 1], x1[:, hs], ALU.mult, ALU.add)
                      k.dma(x1_d[tt_ * 128:(tt_ + 1) * 128, :], x1)
          SATT.close()
          cur[0] = ES
          P.barrier()

          SC3 = contextlib.ExitStack()
          with SC3:
              w1b = sb("w1b", [128, 8, 4096], BF, SC3)
              w2f = sb("w2f", [128, 32, 1024], BF, SC3)
              wpg = sb("wpg", [128, 8, 1024], BF, SC3)
              wpl = sb("wpl", [128, 2, 1024], BF, SC3)
              fT = sb("fT", [128, 32, 256], BF, SC3)
              fTf = fT[:].rearrange("p a b -> p (a b)").bitcast(F32)
              st4 = [stg[0], stg[1], fTf[:, 0:2048], fTf[:, 2048:4096]]
              load_w(lambda kk, a, b: w1b[:, kk, a:b], w_ff1_d, 8, D_FF, gcols[:, 1, :], stages=st4)
              load_w(lambda kk, a, b: w2f[:, kk, a:b], w_ff2_d, 32, D_MODEL, stages=st4)
              load_w(lambda kk, a, b: wpg[:, kk, a:b], w_pg_d, 8, D_MODEL, gcols[:, 2, :], stages=st4)
              load_w(lambda kk, a, b: wpl[:, kk, a:b], w_ple_d, 2, D_MODEL, stages=st4)
              xc = sb("xc", [128, 2, 1024], F32, SC3)
              x2 = sb("x2", [128, 2, 1024], F32, SC3)
              hTC = sb("hTC", [128, 8, 256], BF, SC3)
              h3T = sb("h3T", [128, 8, 128], BF, SC3)
              junkD = stg[0][:, 0:1024]
              hbC = sb("hbC", [128, 1024], BF, SC3)
              ssC = sb("ssC", [128, 1], F32, SC3)
              rsC = sb("rsC", [128, 1], F32, SC3)
              rls = [stg[1][:, 1024:1280], stg[1][:, 1536:1792]]
              ptf = stg[1][:, 1280:1536]
              ptb = sb("ptb", [128, 256], BF, SC3)
              pT = sb("pT", [128, 2, 128], BF, SC3)
              th = stg[0][:, 1024:1536]
              outt = stg[1][:, 0:1024]
              xcs = [xc, x2]
              th2 = [stg[0][:, 1024:1536], stg[0][:, 1536:2048]]
              NBT = T // 256

              def s1(b):
                  for j in range(2):
                      tt_ = 2 * b + j
                      k.dma(xcs[b % 2][:, j, :], x1_d[tt_ * 128:(tt_ + 1) * 128, :])
                      front(xcs[b % 2][:, j, :], hTC, 0, junkD, ssC, rsC, hbC, slice(j * 128, (j + 1) * 128))

              def drain(pend, slots_left):
                  nd = (len(pend) + slots_left - 1) // max(1, slots_left)
                  for _ in range(min(nd, len(pend))):
                      pend.pop(0)()

              def s2_s3(b, pend):
                  x2c = xcs[b % 2]
                  for fc in range(32):
                      bank = ps[fc % 2]
                      for kk in range(8):
                          k.mm(bank[:, 0:256], w1b[:, kk, fc * 128:(fc + 1) * 128], hTC[:, kk, :], start=(kk == 0), stop=(kk == 7))
                      rl = rls[fc % 2]
                      k.act(rl, bank[:, 0:256], AF.Relu)
                      k.tt(("pool", "dve")[fc % 2], fT[:, fc, :], rl, rl, ALU.mult)
                      drain(pend, 36 - fc)
                  for gi in range(4):
                      j, half = gi // 2, gi % 2
                      hs = slice(half * 512, (half + 1) * 512)
                      bank = ps[2 + gi % 2]
                      for fc in range(32):
                          k.mm(bank[:, :], fT[:, fc, j * 128:(j + 1) * 128], w2f[:, fc, hs], start=(fc == 0), stop=(fc == 31))
                      k.tt("dve", x2c[:, j, hs], bank[:, :], x2c[:, j, hs], ALU.add)
                      drain(pend, 4 - gi)

              def ple_thunks(b, j):
                  tt_ = 2 * b + j
                  xin = xcs[b % 2][:, j, :]
                  tl = []

                  def f_a():
                      k.act(junkD[:, 0:1024], xin, AF.Square, accum=ssC[:])
                      k.rsq(rsC[:], ssC[:], float(D_MODEL * EPS))
                      k.ts("dve", hbC[:], xin, rsC[:, 0:1], 32.0, ALU.mult, ALU.mult)
                      for kk in range(8):
                          k.tr(psb[4][:, kk * 128:(kk + 1) * 128], hbC[:, kk * 128:(kk + 1) * 128], ident[:])

                  def f_p():
                      k.dma(ptf, p_d[tt_ * 128:(tt_ + 1) * 128, :])
                      k.cp("pool", ptb[:], ptf)

                  def f_ptr():
                      for c2 in range(2):
                          k.tr(psb[5][:, c2 * 128:(c2 + 1) * 128], ptb[:, c2 * 128:(c2 + 1) * 128], ident[:])

                  tl.append(f_a)
                  tl.append(f_p)
                  tl.append(lambda: k.cp("act", h3T[:, :, 0:128], psb[4][:, 0:1024].rearrange("p (k t) -> p k t", k=8)))
                  tl.append(f_ptr)
                  tl.append(lambda: k.cp("act", pT[:], psb[5][:, 0:256].rearrange("p (c t) -> p c t", c=2)))
                  for half in range(2):
                      hs = slice(half * 512, (half + 1) * 512)
                      thh = th2[half]

                      def f_g(hs=hs):
                          for kk in range(8):
                              k.mm(ps[6][:, :], h3T[:, kk, :], wpg[:, kk, hs], start=(kk == 0), stop=(kk == 7))

                      def f_w(hs=hs):
                          for c2 in range(2):
                              k.mm(ps[7][:, :], pT[:, c2, :], wpl[:, c2, hs], start=(c2 == 0), stop=(c2 == 1))

                      tl.append(f_g)
                      tl.append(f_w)
                      tl.append(lambda thh=thh: k.act(thh, ps[6][:, :], AF.Exp, scale=-1.0))
                      tl.append(lambda thh=thh: k.act(thh, thh, AF.Ln, bias=1.0))
                      tl.append(lambda thh=thh: k.act(thh, thh, AF.Exp, scale=-1.0))
                      tl.append(lambda thh=thh: k.tt("dve", thh, thh, ps[7][:, :], ALU.mult))
                      tl.append(lambda thh=thh, hs=hs: k.tt("pool", outt[:, hs], thh, xin[:, hs], ALU.add))
                  tl.append(lambda: k.dma(out_d[tt_ * 128:(tt_ + 1) * 128, :], outt))
                  return tl

              pend = []
              s1(0)
              for b in range(NBT):
                  s2_s3(b, pend)
                  assert not pend
                  if b + 1 < NBT:
                      s1(b + 1)
                  pend = ple_thunks(b, 0) + ple_thunks(b, 1)
              for t_ in pend:
                  t_()
    except _Stop:
        pass
    P.emit(nc)
    return nc


_NC_CACHE = {}


def _core_inputs(inp, b, consts):
    sq = lambda a: np.ascontiguousarray(np.asarray(a)[0], dtype=np.float32)
    m = {
        "x": np.ascontiguousarray(np.asarray(inp["x"])[b], dtype=np.float32),
        "p": np.ascontiguousarray(np.asarray(inp["p"])[0, b], dtype=np.float32),
    }
    for name in ("g_mix", "w_in", "q_norm_g", "kc_norm_g", "ks_norm_g", "kw_norm_g", "cmp_pos_k", "cmp_pos_v",
                 "cmp_k_w1", "cmp_k_b1", "cmp_k_w2", "cmp_k_b2", "cmp_v_w1", "cmp_v_b1", "cmp_v_w2", "cmp_v_b2",
                 "gmlp_ln_g", "gmlp_ln_b", "gmlp_ws", "gmlp_bs", "out_g_nsa", "out_g_gmlp", "w_out",
                 "g_ff", "w_ff1", "w_ff2", "g_ple", "w_ple_gate", "w_ple"):
        m[name] = sq(inp[name])
    m.update(consts)
    return m


def kernel(_stop=None, **inputs):
    x = np.asarray(inputs["x"])
    B, T = x.shape[0], x.shape[1]
    if T not in _NC_CACHE:
        _NC_CACHE[T] = build_nc(T, _stop)
    nc = _NC_CACHE[T]
    consts = make_consts(T)
    in_maps = [_core_inputs(inputs, b, consts) for b in range(B)]
    res = run_bass_kernel_spmd(nc, in_maps, core_ids=list(range(B)))
    return np.stack([np.asarray(r["out"], dtype=np.float32) for r in res.results], axis=0)
```
